# Optimizing a Trainium2 kernel written in Bass

```python
import math
import jax, jax.numpy as jnp
from jax import lax
import numpy as np

D_MODEL = 1024
BATCH = 4
SEQ = 8192
DEPTH = 1

NSA_HEADS = 8
NSA_KV_HEADS = 2
HEAD_DIM = 64
CMP_BLOCK = 32
CMP_STRIDE = 16
CMP_HIDDEN = 256
SEL_BLOCK = 64
N_SEL = 16
WINDOW = 512
Q_BLOCK = 64
FORCE_BONUS = 1e3
NEG_INF = -1e30
ROPE_THETA = 500000.0
ROPE_DIM = HEAD_DIM // 4
SSM_WIDTH = D_MODEL // 2
SSM_GROUP = 16
SSM_GROUPS = SSM_WIDTH // SSM_GROUP
SSM_STATE = 64
D_FF = 4 * D_MODEL
EPS = 1e-6

NSA_WIDTH = NSA_HEADS * HEAD_DIM
KV_WIDTH = NSA_KV_HEADS * HEAD_DIM
SPLITS = (NSA_WIDTH, 6 * KV_WIDTH, 3 * NSA_HEADS, SSM_WIDTH, 2 * D_MODEL)
IN_WIDTH = sum(SPLITS)

kernel_name = "hybrid_nsa_s5_gated_block"


def rmsnorm(x, g):
    x32 = x.astype(jnp.float32)
    y = x32 * lax.rsqrt(jnp.mean(x32 * x32, axis=-1, keepdims=True) + EPS)
    return (y * g.astype(jnp.float32)).astype(x.dtype)


def rope_partial(x, pos):
    half = ROPE_DIM // 2
    inv = ROPE_THETA ** (-(jnp.arange(half, dtype=jnp.float32) * 2.0) / ROPE_DIM)
    ang = pos.astype(jnp.float32)[:, None] * inv[None, :]
    cos = jnp.cos(ang)[None, :, None, :]
    sin = jnp.sin(ang)[None, :, None, :]
    x32 = x.astype(jnp.float32)
    x1, x2 = x32[..., :half], x32[..., half:ROPE_DIM]
    out = jnp.concatenate([x1 * cos - x2 * sin, x2 * cos + x1 * sin, x32[..., ROPE_DIM:]], axis=-1)
    return out.astype(x.dtype)


def compress(kv, pe, w1, w2):
    b, s, g, d = kv.shape
    r = kv.reshape(b, s // CMP_STRIDE, CMP_STRIDE, g, d)
    blocks = jnp.concatenate([r[:, :-1], r[:, 1:]], axis=2) + pe[None, None, :, None, :]
    nc = blocks.shape[1]
    flat = blocks.transpose(0, 1, 3, 2, 4).reshape(b, nc, g, CMP_BLOCK * d)
    return jax.nn.gelu(flat @ w1) @ w2


def nsa(q_raw, q_rot, kc, vc, ks, vs, kw, vw, gates):
    b, s, h, d = q_raw.shape
    g = kc.shape[2]
    hg = h // g
    nc = kc.shape[1]
    nb = s // SEL_BLOCK
    n_sel = min(N_SEL, nb)
    scale = d ** -0.5
    q_raw = q_raw.reshape(b, s, g, hg, d)
    q_rot = q_rot.reshape(b, s, g, hg, d)
    gates = gates.reshape(b, s, g, hg, 3)
    cmp_end = jnp.arange(nc) * CMP_STRIDE + CMP_BLOCK - 1
    n_np = np.arange(nc)[:, None] * CMP_STRIDE
    j_np = np.arange(nb)[None, :] * SEL_BLOCK
    overlap = jnp.asarray(((n_np < j_np + SEL_BLOCK) & (n_np + CMP_BLOCK > j_np)).astype(np.float32))
    ks_blk = ks.reshape(b, nb, SEL_BLOCK, g, d).transpose(0, 3, 1, 2, 4)
    vs_blk = vs.reshape(b, nb, SEL_BLOCK, g, d).transpose(0, 3, 1, 2, 4)
    kw_pad = jnp.pad(kw, ((0, 0), (WINDOW, 0), (0, 0), (0, 0)))
    vw_pad = jnp.pad(vw, ((0, 0), (WINDOW, 0), (0, 0), (0, 0)))
    gather = jax.vmap(jax.vmap(lambda blk, idx: blk[idx]))
    jb = jnp.arange(nb)

    def block(i):
        t0 = i * Q_BLOCK
        tq = t0 + jnp.arange(Q_BLOCK)
        sl = lambda a: lax.dynamic_slice_in_dim(a, t0, Q_BLOCK, axis=1)
        qr, qp, gt = sl(q_raw), sl(q_rot), sl(gates).astype(jnp.float32)
        mask_c = cmp_end[None, :] <= tq[:, None]
        s_c = jnp.einsum('bqghd,bngd->bghqn', qr, kc).astype(jnp.float32) * scale
        p_c = jax.nn.softmax(jnp.where(mask_c, s_c, NEG_INF), axis=-1)
        p_c = p_c * jnp.any(mask_c, axis=-1)[:, None].astype(jnp.float32)
        o_c = jnp.einsum('bghqn,bngd->bqghd', p_c.astype(vc.dtype), vc)
        imp = jnp.einsum('bghqn,nj->bgqj', p_c, overlap)
        cur = tq // SEL_BLOCK
        valid_j = jb[None, :] * SEL_BLOCK <= tq[:, None]
        forced = (jb[None, :] == 0) | (jb[None, :] == cur[:, None]) | (jb[None, :] == cur[:, None] - 1)
        score = jnp.where(valid_j, imp + FORCE_BONUS * forced.astype(jnp.float32), NEG_INF)
        _, idx = lax.top_k(score, n_sel)
        k_sel = gather(ks_blk, idx).reshape(b, g, Q_BLOCK, n_sel * SEL_BLOCK, d)
        v_sel = gather(vs_blk, idx).reshape(b, g, Q_BLOCK, n_sel * SEL_BLOCK, d)
        kpos = (idx[..., None] * SEL_BLOCK + jnp.arange(SEL_BLOCK)).reshape(b, g, Q_BLOCK, n_sel * SEL_BLOCK)
        mask_s = kpos <= tq[:, None]
        s_s = jnp.einsum('bqghd,bgqkd->bghqk', qp, k_sel).astype(jnp.float32) * scale
        p_s = jax.nn.softmax(jnp.where(mask_s[:, :, None], s_s, NEG_INF), axis=-1)
        o_s = jnp.einsum('bghqk,bgqkd->bqghd', p_s.astype(v_sel.dtype), v_sel)
        kwb = lax.dynamic_slice_in_dim(kw_pad, t0, WINDOW + Q_BLOCK, axis=1)
        vwb = lax.dynamic_slice_in_dim(vw_pad, t0, WINDOW + Q_BLOCK, axis=1)
        kp = t0 - WINDOW + jnp.arange(WINDOW + Q_BLOCK)
        mask_w = (kp[None, :] <= tq[:, None]) & (kp[None, :] > tq[:, None] - WINDOW) & (kp[None, :] >= 0)
        s_w = jnp.einsum('bqghd,bkgd->bghqk', qp, kwb).astype(jnp.float32) * scale
        p_w = jax.nn.softmax(jnp.where(mask_w, s_w, NEG_INF), axis=-1)
        o_w = jnp.einsum('bghqk,bkgd->bqghd', p_w.astype(vwb.dtype), vwb)
        out = gt[..., 0:1] * o_c + gt[..., 1:2] * o_s + gt[..., 2:3] * o_w
        return out.reshape(b, Q_BLOCK, h * d).astype(q_raw.dtype)

    out = lax.map(block, jnp.arange(s // Q_BLOCK))
    return out.transpose(1, 0, 2, 3).reshape(b, s, h * d)


def _scan_op(e1, e2):
    a1, b1 = e1
    a2, b2 = e2
    return a1 * a2, a2 * b1 + b2


def s5(u, lam_re, lam_im, log_step, b_re, b_im, c_re, c_im, d_skip):
    b, s, _ = u.shape
    f32 = jnp.float32
    u32 = u.astype(f32).reshape(b, s, SSM_GROUPS, SSM_GROUP)
    lam = lax.complex(lam_re.astype(f32), lam_im.astype(f32))
    step = jnp.exp(log_step.astype(f32))[:, None]
    lam_bar = jnp.exp(lam * step)
    b_bar = ((lam_bar - 1.0) / lam)[..., None] * lax.complex(b_re.astype(f32), b_im.astype(f32))
    bu = jnp.einsum('bsgc,gnc->bsgn', u32.astype(jnp.complex64), b_bar)
    a = jnp.broadcast_to(lam_bar, bu.shape)
    _, hs = lax.associative_scan(_scan_op, (a, bu), axis=1)
    cc = lax.complex(c_re.astype(f32), c_im.astype(f32))
    y = jnp.real(jnp.einsum('bsgn,gcn->bsgc', hs, cc)) + d_skip.astype(f32).reshape(SSM_GROUPS, SSM_GROUP) * u32
    return y.reshape(b, s, SSM_WIDTH).astype(u.dtype)


def setup_inputs(seed: int = 0) -> dict:
    key = jax.random.key(seed)
    ks = jax.random.split(key, 24)
    nrm = lambda k, shape, fan: jax.random.normal(k, shape, jnp.float32) * fan ** -0.5
    L = DEPTH
    n_idx = jnp.arange(SSM_STATE, dtype=jnp.float32)
    return {
        "x": jax.random.normal(ks[0], (BATCH, SEQ, D_MODEL), jnp.float32),
        "norm_mix_g": 1.0 + 0.02 * jax.random.normal(ks[1], (L, D_MODEL), jnp.float32),
        "w_in": nrm(ks[2], (L, D_MODEL, IN_WIDTH), D_MODEL),
        "cmp_pe": 0.02 * jax.random.normal(ks[3], (L, CMP_BLOCK, HEAD_DIM), jnp.float32),
        "cmp_k_w1": nrm(ks[4], (L, CMP_BLOCK * HEAD_DIM, CMP_HIDDEN), CMP_BLOCK * HEAD_DIM),
        "cmp_k_w2": nrm(ks[5], (L, CMP_HIDDEN, HEAD_DIM), CMP_HIDDEN),
        "cmp_v_w1": nrm(ks[6], (L, CMP_BLOCK * HEAD_DIM, CMP_HIDDEN), CMP_BLOCK * HEAD_DIM),
        "cmp_v_w2": nrm(ks[7], (L, CMP_HIDDEN, HEAD_DIM), CMP_HIDDEN),
        "ssm_lam_re": -0.5 * jnp.exp(0.05 * jax.random.normal(ks[8], (L, SSM_GROUPS, SSM_STATE), jnp.float32)),
        "ssm_lam_im": jnp.broadcast_to(math.pi * n_idx, (L, SSM_GROUPS, SSM_STATE)),
        "ssm_log_step": jax.random.uniform(ks[9], (L, SSM_GROUPS), jnp.float32, math.log(1e-3), math.log(1e-1)),
        "ssm_b_re": nrm(ks[10], (L, SSM_GROUPS, SSM_STATE, SSM_GROUP), 2 * SSM_GROUP),
        "ssm_b_im": nrm(ks[11], (L, SSM_GROUPS, SSM_STATE, SSM_GROUP), 2 * SSM_GROUP),
        "ssm_c_re": nrm(ks[12], (L, SSM_GROUPS, SSM_GROUP, SSM_STATE), SSM_STATE),
        "ssm_c_im": nrm(ks[13], (L, SSM_GROUPS, SSM_GROUP, SSM_STATE), SSM_STATE),
        "ssm_d": jax.random.normal(ks[14], (L, SSM_WIDTH), jnp.float32),
        "w_attn_branch": nrm(ks[15], (L, NSA_WIDTH, D_MODEL), NSA_WIDTH),
        "w_ssm_val": nrm(ks[16], (L, SSM_WIDTH, D_MODEL), SSM_WIDTH),
        "w_ssm_gate": nrm(ks[17], (L, SSM_WIDTH, D_MODEL), SSM_WIDTH),
        "w_out": nrm(ks[18], (L, D_MODEL, D_MODEL), D_MODEL),
        "norm_mlp_g": 1.0 + 0.02 * jax.random.normal(ks[19], (L, D_MODEL), jnp.float32),
        "w_up": nrm(ks[20], (L, D_MODEL, D_FF), D_MODEL),
        "w_down": nrm(ks[21], (L, D_FF, D_MODEL), D_FF),
        "norm_final_g": 1.0 + 0.02 * jax.random.normal(ks[22], (D_MODEL,), jnp.float32),
    }


def reference(x, norm_mix_g, w_in, cmp_pe, cmp_k_w1, cmp_k_w2, cmp_v_w1, cmp_v_w2,
              ssm_lam_re, ssm_lam_im, ssm_log_step, ssm_b_re, ssm_b_im, ssm_c_re, ssm_c_im, ssm_d,
              w_attn_branch, w_ssm_val, w_ssm_gate, w_out, norm_mlp_g, w_up, w_down, norm_final_g):
    b, s, _ = x.shape
    pos = jnp.arange(s)
    offs = [int(v) for v in np.cumsum(SPLITS)[:-1]]
    for l in range(DEPTH):
        h = rmsnorm(x, norm_mix_g[l])
        proj = h @ w_in[l]
        q, kv, nsa_g, u, merge_g = jnp.split(proj, offs, axis=-1)
        q = q.reshape(b, s, NSA_HEADS, HEAD_DIM)
        kv = kv.reshape(b, s, 6, NSA_KV_HEADS, HEAD_DIM)
        k_c, v_c, k_s, v_s, k_w, v_w = (kv[:, :, i] for i in range(6))
        q_rot = rope_partial(q, pos)
        k_s = rope_partial(k_s, pos)
        k_w = rope_partial(k_w, pos)
        kc = compress(k_c, cmp_pe[l], cmp_k_w1[l], cmp_k_w2[l])
        vc = compress(v_c, cmp_pe[l], cmp_v_w1[l], cmp_v_w2[l])
        gates = jax.nn.sigmoid(nsa_g.astype(jnp.float32)).reshape(b, s, NSA_HEADS, 3)
        y_a = nsa(q, q_rot, kc, vc, k_s, v_s, k_w, v_w, gates) @ w_attn_branch[l]
        y_ssm = jax.nn.gelu(s5(u, ssm_lam_re[l], ssm_lam_im[l], ssm_log_step[l], ssm_b_re[l], ssm_b_im[l],
                               ssm_c_re[l], ssm_c_im[l], ssm_d[l]))
        y_b = (y_ssm @ w_ssm_val[l]) * jax.nn.sigmoid(y_ssm @ w_ssm_gate[l])
        g_a, g_b = jnp.split(merge_g, 2, axis=-1)
        merged = jax.nn.sigmoid(g_a) * y_a + jax.nn.sigmoid(g_b) * y_b
        x = x + merged @ w_out[l]
        h2 = rmsnorm(x, norm_mlp_g[l])
        x = x + jnp.square(jax.nn.relu(h2 @ w_up[l])) @ w_down[l]
    return rmsnorm(x, norm_final_g)
```

```python
import math
import numpy as np
import ml_dtypes
import concourse.bass as bass
import concourse.mybir as mybir
from concourse.bass_utils import run_bass_kernel_spmd
from contextlib import ExitStack

F32 = mybir.dt.float32
BF16 = mybir.dt.bfloat16
ALU = mybir.AluOpType
AF = mybir.ActivationFunctionType

T = 8192
NT = T // 128
TO = T // 2
NTO = TO // 128
D = 1024
INW = 3864
QOFF, KVOFF, NGOFF, UOFF, MGOFF = 0, 512, 1280, 1304, 1816
NEG = -30000.0
NO_SAME_ENGINE_SYNC = False
DBG = {}


class Buf:
    __slots__ = ("name", "last_w", "readers")

    def __init__(self, name=""):
        self.name = name
        self.last_w = None
        self.readers = []


class Op:
    __slots__ = ("eng", "fn", "idx", "deps", "sig", "is_dma", "sem", "val", "prewait")


class Prog:
    ENGS = ("pe", "act", "dve", "pool", "sp")
    NDSEM = 12

    def __init__(self, nc, stack):
        self.nc = nc
        self.stack = stack
        self.ops = []
        self.engobj = {"pe": nc.tensor, "act": nc.scalar, "dve": nc.vector,
                       "pool": nc.gpsimd, "sp": nc.sync}
        self.nsem = 0
        self.last_on = {}
        self.dmas_since_barrier = []
        self.off = 16640
        self.ntens = 0

    def sb(self, shape, dt, name=None, at=None):
        esz = 4 if dt == F32 else 2
        n = 1
        for s in shape[1:]:
            n *= s
        nbytes = (n * esz + 63) // 64 * 64
        self.ntens += 1
        if at is not None:
            return self.nc.alloc_sbuf_tensor_at(f"{name or 't'}_{self.ntens}", list(shape), dt, offset=at)
        t = self.nc.alloc_sbuf_tensor_at(f"{name or 't'}_{self.ntens}", list(shape), dt, offset=self.off)
        self.off += nbytes
        assert self.off <= 226000, f"SBUF overflow {self.off}"
        return t

    def newsem(self, name):
        self.nsem += 1
        return self.stack.enter_context(self.nc.semaphore(f"{name}_{self.nsem}"))

    def _add(self, eng, fn, reads, writes, is_dma):
        o = Op()
        o.eng, o.fn, o.idx, o.deps, o.sig, o.is_dma = eng, fn, len(self.ops), set(), False, is_dma
        o.sem, o.val, o.prewait = None, 0, None
        for b in reads:
            if b.last_w is not None:
                o.deps.add(b.last_w)
        for b in writes:
            if b.last_w is not None:
                o.deps.add(b.last_w)
            o.deps.update(b.readers)
        for b in reads:
            b.readers.append(o.idx)
        for b in writes:
            b.last_w = o.idx
            b.readers = []
        o.deps.discard(o.idx)
        self.ops.append(o)
        if is_dma:
            self.dmas_since_barrier.append(o.idx)
        elif fn is not None:
            self.last_on[eng] = o.idx
        return o

    def op(self, eng, fn, reads=(), writes=()):
        return self._add(eng, fn, reads, writes, False)

    def dma(self, q, out, in_, reads=(), writes=(), slow=False):
        if slow:
            return self._add(q, lambda e: e.dma_start(out=out, in_=in_, allow_slow_non_contiguous=True), reads, writes, True)
        return self._add(q, lambda e: e.dma_start(out=out, in_=in_), reads, writes, True)

    def barrier(self):
        deps = set(self.last_on.values()) | set(self.dmas_since_barrier)
        self.dmas_since_barrier = []
        for e in self.ENGS:
            o = self._add(e, None, (), (), False)
            o.deps = set(d for d in deps)

    def mm(self, out, lhsT, rhs, start, stop, r, w):
        self.op("pe", lambda e: e.matmul(out, lhsT, rhs, start=start, stop=stop, skip_group_check=True), r, w)

    def tr(self, out, in_, ident, r, w):
        self.op("pe", lambda e: e.transpose(out, in_, ident), r, w)

    def act(self, out, in_, func, r, w, bias=None, scale=None, accum=None):
        kw = {}
        if bias is not None:
            kw["bias"] = bias
        if scale is not None:
            kw["scale"] = scale
        if accum is not None:
            kw["accum_out"] = accum
        self.op("act", lambda e: e.activation(out=out, in_=in_, func=func, **kw), r, w)

    def tt(self, out, in0, in1, op, r, w, eng="dve"):
        self.op(eng, lambda e: e.tensor_tensor(out=out, in0=in0, in1=in1, op=op), r, w)

    def ts(self, out, in0, s1, s2, op0, op1, r, w, eng="dve"):
        if op1 is None:
            self.op(eng, lambda e: e.tensor_scalar(out=out, in0=in0, scalar1=s1, scalar2=None, op0=op0), r, w)
        else:
            self.op(eng, lambda e: e.tensor_scalar(out=out, in0=in0, scalar1=s1, scalar2=s2, op0=op0, op1=op1), r, w)

    def stt(self, out, in0, scalar, in1, op0, op1, r, w, eng="dve"):
        self.op(eng, lambda e: e.scalar_tensor_tensor(out=out, in0=in0, scalar=scalar, in1=in1, op0=op0, op1=op1), r, w)

    def cp(self, out, in_, r, w, eng="dve"):
        self.op(eng, lambda e: e.tensor_copy(out=out, in_=in_), r, w)

    def memset(self, ap, v, w, eng="dve"):
        self.op(eng, lambda e: e.memset(ap, v), (), w)

    def finalize(self):
        ops = self.ops
        for o in ops:
            if o.eng == "pe" and o.fn is not None:
                o.deps = set(d for d in o.deps if not (ops[d].eng == "pe" and not ops[d].is_dma))
            if NO_SAME_ENGINE_SYNC and o.eng in ("dve", "act") and o.fn is not None and not o.is_dma:
                o.deps = set(d for d in o.deps if not (ops[d].eng == o.eng and not ops[d].is_dma))
            for d in o.deps:
                ops[d].sig = True
            if o.is_dma:
                o.sig = True
        ROLL = 12000
        cur, cnt = {}, {}
        dsem = {}
        dcount = {}
        for o in ops:
            if not o.sig:
                continue
            if o.is_dma:
                q = o.eng
                if q not in dsem:
                    dsem[q] = [self.newsem("d" + q) for _ in range(self.NDSEM)]
                    dcount[q] = 0
                k = dcount[q]
                dcount[q] += 1
                o.sem = dsem[q][k % self.NDSEM]
                o.val = 16 * (k // self.NDSEM + 1)
                if k >= self.NDSEM:
                    o.prewait = (o.sem, o.val - 16)
            else:
                e = o.eng
                if e not in cur or cnt[e] >= ROLL:
                    cur[e] = self.newsem(e)
                    cnt[e] = 0
                cnt[e] += 1
                o.sem, o.val = cur[e], cnt[e]
        waited = {e: {} for e in self.ENGS}
        nwait = 0
        for o in ops:
            e = self.engobj[o.eng]
            need = {}
            for d in o.deps:
                p = ops[d]
                k = id(p.sem)
                if k not in need or need[k][1] < p.val:
                    need[k] = (p.sem, p.val)
            if o.prewait is not None:
                k = id(o.prewait[0])
                if k not in need or need[k][1] < o.prewait[1]:
                    need[k] = o.prewait
            w = waited[o.eng]
            for k, (s, v) in need.items():
                if w.get(k, 0) >= v:
                    continue
                e.wait_ge(s, v)
                w[k] = v
                nwait += 1
            if o.fn is None:
                continue
            ins = o.fn(e)
            if o.sig:
                ins.then_inc(o.sem, 16 if o.is_dma else 1)
        fe = self.engobj["sp"]
        for q, sems in dsem.items():
            n = dcount[q]
            for j, s in enumerate(sems):
                c = (n - j + self.NDSEM - 1) // self.NDSEM if n > j else 0
                if c > 0:
                    fe.wait_ge(s, 16 * c)
        return nwait


def host_consts(par=0):
    c = {}
    bf = ml_dtypes.bfloat16
    c["ident"] = np.eye(128, dtype=np.float32).astype(bf)
    c["identf"] = np.eye(128, dtype=np.float32)
    c["i4"] = np.tile(np.eye(128, dtype=np.float32), (1, 4)).astype(bf)
    half = 8
    inv = 500000.0 ** (-(np.arange(half, dtype=np.float32) * 2.0) / 16.0)
    ang = np.arange(T, dtype=np.float32)[None, :] * inv[:, None].astype(np.float32)
    cosv, sinv = np.cos(ang).astype(np.float32), np.sin(ang).astype(np.float32)
    rc = np.ones((128, T), np.float32)
    rs = np.zeros((128, T), np.float32)
    for g in range(2):
        rc[64 * g:64 * g + 8] = cosv
        rc[64 * g + 8:64 * g + 16] = cosv
        rs[64 * g:64 * g + 8] = sinv
        rs[64 * g + 8:64 * g + 16] = sinv
    c["ropec"], c["ropes"] = rc, rs
    own_pos = (np.arange(TO) // 128 * 2 + par) * 128 + np.arange(TO) % 128
    c["ropeco"], c["ropeso"] = np.ascontiguousarray(rc[:, own_pos]), np.ascontiguousarray(rs[:, own_pos])
    pm = np.zeros((128, 128), np.float32)
    for g in range(2):
        for d in range(8):
            pm[64 * g + d + 8, 64 * g + d] = -1.0
            pm[64 * g + d, 64 * g + d + 8] = 1.0
    c["pm"] = pm.astype(bf)
    r = np.arange(128)[:, None]
    kl = np.arange(128)[None, :]
    def vis_mask(p, window):
        dk = kl - r
        v = dk <= 128 * (par - p)
        if window:
            v = v & (dk > 128 * (par - p) - 512)
        return np.where(v, 0.0, NEG).astype(np.float32)
    c["wmask"] = np.stack([vis_mask(p, True) for p in (1, 0, -3, -4)], axis=1).astype(bf)
    c["smask"] = np.stack([vis_mask(p, False) for p in (0, 1)], axis=1).astype(bf)
    m = np.arange(0, 288)[None, :] - 136 - 8 * par
    fl = np.floor((np.arange(128)[:, None] - 31) / 16.0)
    c["cbase"] = np.where(m <= fl, 0.0, NEG).astype(np.float32).astype(bf)
    c["sel01"] = np.tile(np.array([[1.0 - par, float(par)]], np.float32), (128, 1))
    n_ = np.arange(512)[:, None] * 16
    j_ = np.arange(128)[None, :] * 64
    ov = ((n_ < j_ + 64) & (n_ + 32 > j_)).astype(np.float32)
    ov[511] = 0
    c["ov"] = ov.reshape(4, 128, 128).transpose(1, 0, 2).copy().astype(bf)
    mm = np.arange(-128, 128)[None, :]
    hi = (np.arange(128)[:, None] >= 64).astype(np.int64) + 2 * par
    wf = np.zeros((128, 256), np.float32)
    wf[(mm == hi) | (mm == hi - 1)] = 1000.0
    wf[np.broadcast_to(mm > hi, wf.shape)] = -1e30
    c["wf"] = wf
    we = np.zeros((128, T), np.float32)
    we[np.arange(T) // 64, np.arange(T)] = 1.0
    c["we"] = we.astype(bf)
    sg = np.zeros((24, 24, 64), np.float32)
    for k in range(24):
        sg[k, k, :] = 1.0
    c["selg"] = sg.astype(bf)
    mb = np.zeros((128, 4, 8), np.float32)
    mc = np.zeros((128, 4, 2), np.float32)
    for q in range(4):
        for gp in range(2):
            mb[64 * gp:64 * gp + 64, q, 2 * q + gp] = 1.0
            mc[32 * q + 16 * gp:32 * q + 16 * gp + 16, q, gp] = 1.0
    c["mb"], c["mc"] = mb, mc
    rowsel = np.zeros((128, 4), np.float32)
    rowsel[0:64, 0] = 1.0
    rowsel[64:128, 1] = 1.0
    rowsel[:, 2:4] = -rowsel[:, 0:2]
    c["rowsel"] = rowsel
    tq = np.arange(128) // 16
    c["cmask8"] = (tq[None, :] >= tq[:, None]).astype(np.float32)
    selc = np.zeros((128, 64), np.float32)
    for j in range(64):
        selc[(2 * (j // 16) + par) * 16 + j % 16, j] = 1.0
    c["selc"] = selc.astype(bf)
    return c


CONST_SPECS = None


def build(dbg=None):
    nc = bass.Bass("TRN2", target_bir_lowering=False)
    consts = host_consts()
    dram_in = {}

    def din(name, shape, dt):
        dram_in[name] = nc.dram_tensor(name, list(shape), dt, kind="ExternalInput").ap()
        return dram_in[name]

    x = din("x", [T, D], F32)
    xown = din("xown", [TO, D], F32)
    w_in = din("w_in", [D, INW], F32)
    g_mix = din("norm_mix_g", [D], F32)
    g_mlp = din("norm_mlp_g", [D], F32)
    g_fin = din("norm_final_g", [D], F32)
    cmp_pe = din("cmp_pe", [32, 64], F32)
    cw1 = [din("cmp_k_w1", [2048, 256], F32), din("cmp_v_w1", [2048, 256], F32)]
    cw2 = [din("cmp_k_w2", [256, 64], F32), din("cmp_v_w2", [256, 64], F32)]
    lam_re = din("ssm_lam_re", [32, 64], F32)
    lam_im = din("ssm_lam_im", [32, 64], F32)
    log_step = din("ssm_log_step", [32], F32)
    b_re = din("ssm_b_re", [32, 64, 16], F32)
    b_im = din("ssm_b_im", [32, 64, 16], F32)
    c_re = din("ssm_c_re", [32, 16, 64], F32)
    c_im = din("ssm_c_im", [32, 16, 64], F32)
    ssm_d = din("ssm_d", [512], F32)
    w_attn = din("w_attn_branch", [512, D], F32)
    w_val = din("w_ssm_val", [512, D], F32)
    w_gate = din("w_ssm_gate", [512, D], F32)
    w_out = din("w_out", [D, D], F32)
    w_up = din("w_up", [D, 4096], F32)
    w_down = din("w_down", [4096, D], F32)
    cd = {}
    for k, v in consts.items():
        cd[k] = din("c_" + k, v.shape, BF16 if v.dtype == ml_dtypes.bfloat16 else F32)
    out = nc.dram_tensor("out", [TO, D], F32, kind="ExternalOutput").ap()

    def scratch(name, shape, dt):
        return nc.dram_tensor("s_" + name, list(shape), dt, kind="Internal").ap()

    QR = scratch("qr", [128, 4, TO], BF16)
    QP = scratch("qp", [128, 4, TO], BF16)
    KS = scratch("ks", [128, T], BF16)
    KW = scratch("kw", [128, T], BF16)
    KC = scratch("kc", [128, T], BF16)
    VC = scratch("vc", [128, T], BF16)
    VSW = scratch("vsw", [NT, 128, 2, 2, 128], BF16)
    GM = scratch("gm", [128, 16, TO], BF16)
    UT = scratch("ut", [128, 4, T], BF16)
    UTM = scratch("utm", [T, 512], BF16)
    NG = scratch("ng", [24, TO], BF16)
    YB = scratch("yb", [128, 8, TO], BF16)
    X2 = scratch("x2", [TO, D], F32)

    with ExitStack() as st:
        P = Prog(nc, st)
        ps = [st.enter_context(nc.psum_tensor(f"ps{i}", [128, 1024], F32)) for i in range(4)]
        psb = [[Buf(f"ps{i}a"), Buf(f"ps{i}b")] for i in range(4)]

        def bank(i):
            return ps[i // 2][:, (i % 2) * 512:(i % 2) * 512 + 512], psb[i // 2][i % 2]

        dq = ["sp", "sp"]
        dqi = [0]
        bq_i = [0]

        def bulkq():
            bq_i[0] += 1
            return ("sp", "pool")[bq_i[0] % 2]

        def nq():
            dqi[0] += 1
            return dq[dqi[0] % 2]

        ident = P.sb([128, 128], BF16, "ident")
        identf = P.sb([128, 128], F32, "identf")
        bconst = Buf("const")
        P.dma("sp", ident[:], cd["ident"], writes=[bconst])
        P.dma("sp", identf[:], cd["identf"], writes=[bconst])
        persist_off = P.off

        def load_w_bf16(dst, src_ap, shape, bdst, stage, bstage, conv_eng="dve"):
            n = src_ap.shape[-1]
            if len(src_ap.shape) == 2 and n > 1024:
                nch = (n + 965) // 966
                for c in range(nch):
                    lo, hi = c * 966, min(n, (c + 1) * 966)
                    P.dma(nq(), stage[:, lo:hi], src_ap[:, lo:hi], writes=[bstage])
            else:
                P.dma(nq(), stage, src_ap, writes=[bstage])
            P.cp(dst, stage, [bstage], [bdst], eng=conv_eng)

        def phase1a():
            P.off = persist_off
            win = P.sb([128, 8, INW], BF16, "win")
            bwin = Buf("win")
            stg = [P.sb([128, INW], F32, "stg") for _ in range(2)]
            bstg = [Buf("stg0"), Buf("stg1")]
            for kc in range(8):
                load_w_bf16(win[:, kc, :], w_in[kc * 128:(kc + 1) * 128, :], None, bwin, stg[kc % 2][:], bstg[kc % 2],
                            conv_eng="dve")
            gT = P.sb([128, 8], F32, "gT")
            P.dma("sp", gT[:], g_mix.rearrange("(c p) -> p c", p=128), writes=[bconst], slow=True)
            pm = P.sb([128, 128], BF16, "pm")
            P.dma("sp", pm[:], cd["pm"], writes=[bconst])
            xt = P.sb([128, 4, D], F32, "xt")
            bxt = [Buf() for _ in range(4)]
            junk = P.sb([128, D], BF16, "junk")
            bjunk = Buf()
            ssq = P.sb([128, 4], F32, "ssq")
            rstd = P.sb([128, 4], F32, "rstd")
            bss = [Buf() for _ in range(4)]
            hb4 = [P.sb([128, D], BF16, "hb") for _ in range(4)]
            bhb4 = [Buf() for _ in range(4)]
            hT2 = [P.sb([128, 8, 512], BF16, "hT") for _ in range(2)]
            bhT2 = [Buf(), Buf()]
            cur = {"hT": hT2[0], "bhT": bhT2[0]}
            ropc = P.sb([128, 512], F32, "ropc")
            rops = P.sb([128, 512], F32, "rops")
            brope = Buf()
            qr_sb = P.sb([128, 4, 512], BF16, "qr")
            qp_sb = P.sb([128, 4, 512], BF16, "qp")
            bqr, bqp = Buf(), Buf()
            kraw = P.sb([128, 512], BF16, "kraw")
            bkraw = Buf()
            kout = P.sb([128, 4, 512], BF16, "kout")
            bko = [Buf() for _ in range(4)]
            t1 = P.sb([128, 512], F32, "t1")
            t2 = P.sb([128, 512], F32, "t2")
            bt1, bt2 = Buf(), Buf()
            gm_sb = P.sb([128, 16, 512], BF16, "gm")
            bgm = Buf()
            u_sb = P.sb([128, 4, 512], BF16, "u")
            bu = Buf()
            ng_sb = P.sb([24, 512], BF16, "ng")
            bng = Buf()
            v_sb = P.sb([128, 2, 2, 128], BF16, "v")
            bv = Buf()
            P.memset(v_sb[:], 1.0, [bv])
            psT = ps[0][:, 0:512].bitcast(BF16)
            bpsT = psb[0][0]
            acc_banks = [2, 3, 4, 5]
            acc_i = [0]

            def next_acc():
                acc_i[0] += 1
                return bank(acc_banks[acc_i[0] % 4])

            def proj(cols, M):
                pa, pb = next_acc()
                for kc in range(8):
                    P.mm(pa[0:M, :], win[:, kc, cols:cols + M], cur["hT"][:, kc, :], kc == 0, kc == 7, [bwin, cur["bhT"]], [pb])
                return pa, pb

            def rope(src_sb, bsrc, dst_sb, bdst):
                pp, pbp = bank(6)
                P.mm(pp, pm[:], src_sb, True, True, [bconst, bsrc], [pbp])
                P.tt(t1[:], src_sb, ropc[:], ALU.mult, [bsrc, brope], [bt1])
                P.tt(t2[:], pp, rops[:], ALU.mult, [pbp, brope], [bt2])
                P.tt(dst_sb, t1[:], t2[:], ALU.add, [bt1, bt2], [bdst])

            psT2 = [ps[0][:, 0:512].bitcast(BF16), ps[0][:, 512:1024].bitcast(BF16)]
            bpsT2 = [psb[0][0], psb[0][1]]

            def stats(xsrc, blk):
                for tt in range(4):
                    tile = blk * 4 + tt
                    P.dma(nq(), xt[:, tt, :], xsrc[tile * 128:(tile + 1) * 128, :], writes=[bxt[tt]])
                    P.act(junk[:], xt[:, tt, :], AF.Square, [bxt[tt]], [bjunk, bss[tt]], accum=ssq[:, tt:tt + 1])
                    P.ts(rstd[:, tt:tt + 1], ssq[:, tt:tt + 1], 1.0 / D, 1e-6, ALU.mult, ALU.add, [bss[tt]], [bss[tt]])
                    P.act(rstd[:, tt:tt + 1], rstd[:, tt:tt + 1], AF.Sqrt, [bss[tt]], [bss[tt]])
                    P.op("dve", (lambda tt=tt: (lambda e: e.reciprocal(out=rstd[:, tt:tt + 1], in_=rstd[:, tt:tt + 1])))(), [bss[tt]], [bss[tt]])
                    P.act(hb4[tt][:], xt[:, tt, :], AF.Copy, [bxt[tt], bss[tt]], [bhb4[tt]], scale=rstd[:, tt:tt + 1])

            def trans(par_):
                hT_, bhT_ = hT2[par_], bhT2[par_]
                for tt in range(4):
                    pT, bpT = psT2[tt % 2], bpsT2[tt % 2]
                    for kc in range(8):
                        P.tr(pT[:, kc * 128:(kc + 1) * 128], hb4[tt][:, kc * 128:(kc + 1) * 128], ident[:], [bhb4[tt], bconst], [bpT])
                    P.tt(hT_[:, :, tt * 128:(tt + 1) * 128], pT.rearrange("p (c t) -> p c t", c=8),
                         gT[:].unsqueeze(2).to_broadcast([128, 8, 128]), ALU.mult, [bpT, bconst], [bhT_])

            def projA(blk):
                hT = cur["hT"]
                bhT = cur["bhT"]
                tok0 = blk * 512
                P.dma("sp", ropc[:], cd["ropec"][:, tok0:tok0 + 512], writes=[brope])
                P.dma("sp", rops[:], cd["ropes"][:, tok0:tok0 + 512], writes=[brope])
                for j, (kvi, dst) in enumerate([(0, KC), (1, VC)]):
                    pa, pb = proj(KVOFF + 128 * kvi, 128)
                    P.act(kout[:, 2 + j, :], pa, AF.Copy, [pb], [bko[2 + j]])
                    P.dma(nq(), dst[:, tok0:tok0 + 512], kout[:, 2 + j, :], reads=[bko[2 + j]])
                for j, (kvi, dst) in enumerate([(2, KS), (4, KW)]):
                    pa, pb = proj(KVOFF + 128 * kvi, 128)
                    P.act(kraw[:], pa, AF.Copy, [pb], [bkraw])
                    rope(kraw[:], bkraw, kout[:, j, :], bko[j])
                    P.dma(nq(), dst[:, tok0:tok0 + 512], kout[:, j, :], reads=[bko[j]])
                for tt in range(4):
                    tile = blk * 4 + tt
                    pu, pub = bank(6)
                    for kc in range(8):
                        P.mm(pu, hT[:, kc, tt * 128:(tt + 1) * 128], win[:, kc, UOFF:UOFF + 512], kc == 0, kc == 7, [bhT, bwin], [pub])
                    P.act(u_sb[:, tt, :], pu, AF.Copy, [pub], [bu])
                    P.dma(nq(), UTM[tile * 128:(tile + 1) * 128, :], u_sb[:, tt, :], reads=[bu])
                for tt in range(4):
                    tile = blk * 4 + tt
                    pv, pvb = bank(7)
                    for sw, kvi in enumerate([3, 5]):
                        for kc in range(8):
                            P.mm(pv[:, sw * 128:(sw + 1) * 128], hT[:, kc, tt * 128:(tt + 1) * 128],
                                 win[:, kc, KVOFF + 128 * kvi:KVOFF + 128 * kvi + 128], kc == 0, kc == 7, [bhT, bwin], [pvb])
                    P.cp(v_sb[:, :, :, 0:64], pv[:, 0:256].rearrange("p (s g d) -> p s g d", s=2, g=2), [pvb], [bv])
                    P.dma(nq(), VSW[tile], v_sb[:], reads=[bv])

            def projB(blk):
                tok0 = blk * 512
                P.dma("sp", ropc[:], cd["ropeco"][:, tok0:tok0 + 512], writes=[brope])
                P.dma("sp", rops[:], cd["ropeso"][:, tok0:tok0 + 512], writes=[brope])
                for m in range(4):
                    pa, pb = proj(QOFF + 128 * m, 128)
                    g = m // 2
                    hl = 2 * (m % 2)
                    P.act(qr_sb[64 * g:64 * g + 64, hl, :], pa[0:64, :], AF.Copy, [pb], [bqr])
                    P.cp(qr_sb[64 * g:64 * g + 64, hl + 1, :], pa[64:128, :], [pb], [bqr])
                for hl in range(4):
                    rope(qr_sb[:, hl, :], bqr, qp_sb[:, hl, :], bqp)
                P.dma(nq(), QR[:, :, tok0:tok0 + 512], qr_sb[:], reads=[bqr])
                P.dma(nq(), QP[:, :, tok0:tok0 + 512], qp_sb[:], reads=[bqp])
                for m in range(16):
                    pa, pb = proj(MGOFF + 128 * m, 128)
                    P.act(gm_sb[:, m, :], pa, AF.Sigmoid, [pb], [bgm])
                P.dma(nq(), GM[:, :, tok0:tok0 + 512], gm_sb[:], reads=[bgm])
                pa, pb = proj(NGOFF, 24)
                P.act(ng_sb[:], pa[0:24, :], AF.Sigmoid, [pb], [bng])
                P.dma(nq(), NG[:, tok0:tok0 + 512], ng_sb[:], reads=[bng])

            blocks = [("A", b_) for b_ in range(NT // 4)] + [("B", b_) for b_ in range(NTO // 4)]
            srcs = {"A": x, "B": xown}
            stats(srcs[blocks[0][0]], blocks[0][1])
            trans(0)
            for bi, (kind, b_) in enumerate(blocks):
                cur["hT"], cur["bhT"] = hT2[bi % 2], bhT2[bi % 2]
                if bi + 1 < len(blocks):
                    stats(srcs[blocks[bi + 1][0]], blocks[bi + 1][1])
                (projA if kind == "A" else projB)(b_)
                if bi + 1 < len(blocks):
                    trans((bi + 1) % 2)

        def phase1b():
            P.off = persist_off
            TB = 256
            NBK = T // TB
            bp = Buf("ssmparams")
            lre = P.sb([128, 16], F32)
            lim = P.sb([128, 16], F32)
            stp = P.sb([128, 16], F32)
            P.dma("sp", lre[:], lam_re.rearrange("(p gp) n -> (gp n) p", gp=2), writes=[bp], slow=True)
            P.dma("sp", lim[:], lam_im.rearrange("(p gp) n -> (gp n) p", gp=2), writes=[bp], slow=True)
            ls2 = log_step.rearrange("(p gp) -> gp p", gp=2)
            for gp in range(2):
                P.dma("sp", stp[64 * gp:64 * gp + 64, :], ls2[gp:gp + 1, :].to_broadcast([64, 16]), writes=[bp], slow=True)
            bre = P.sb([128, 16, 16], F32)
            bim = P.sb([128, 16, 16], F32)
            P.dma("sp", bre[:], b_re.rearrange("(p gp) n c -> (gp n) p c", gp=2), writes=[bp])
            P.dma("sp", bim[:], b_im.rearrange("(p gp) n c -> (gp n) p c", gp=2), writes=[bp])
            cct = P.sb([128, 2, 4, 64], F32)
            P.dma("sp", cct[:, 0, :, :], c_re.rearrange("(j gl) c n -> (gl c) j n", j=4), writes=[bp])
            P.dma("sp", cct[:, 1, :, :], c_im.rearrange("(j gl) c n -> (gl c) j n", j=4), writes=[bp])
            dsk = P.sb([128, 4], F32)
            P.dma("sp", dsk[:], ssm_d.rearrange("(j p) -> p j", p=128), writes=[bp], slow=True)
            mb = P.sb([128, 4, 8], F32)
            mc = P.sb([128, 4, 2], F32)
            P.dma("sp", mb[:], cd["mb"], writes=[bp])
            P.dma("sp", mc[:], cd["mc"], writes=[bp])
            sc = [P.sb([128, 16], F32) for _ in range(12)]
            bs = Buf("ssmscratch")
            R, W = [bp, bs], [bs]
            step, r_, th, s8, s16, pc, psn, tmpa, tmpb, cre, cim, den = sc
            P.act(step[:], stp[:], AF.Exp, [bp], W)
            P.tt(tmpa[:], step[:], lre[:], ALU.mult, R, W)
            P.act(r_[:], tmpa[:], AF.Exp, R, W)
            P.tt(th[:], step[:], lim[:], ALU.mult, R, W)
            P.act(s8[:], th[:], AF.Sin, R, W, scale=1.0 / 8)
            P.act(s16[:], th[:], AF.Sin, R, W, scale=1.0 / 16)
            P.tt(tmpa[:], s16[:], s16[:], ALU.mult, R, W)
            P.ts(pc[:], tmpa[:], -2.0, 1.0, ALU.mult, ALU.add, R, W)
            P.cp(psn[:], s8[:], R, W)
            pw = P.sb([128, 11, 2, 16], F32)
            bpw = Buf("pw")

            def csq(oc, os_, ic, is_):
                P.tt(tmpa[:], ic, ic, ALU.mult, R + [bpw], W)
                P.tt(tmpb[:], is_, is_, ALU.mult, R + [bpw], W)
                P.tt(cim[:], ic, is_, ALU.mult, R + [bpw], W)
                P.tt(oc, tmpa[:], tmpb[:], ALU.subtract, R + [bpw], W + [bpw])
                P.ts(os_, cim[:], 2.0, None, ALU.mult, None, R + [bpw], W + [bpw])
            csq(cre[:], den[:], pc[:], psn[:])
            csq(pc[:], psn[:], cre[:], den[:])
            csq(pw[:, 0, 0, :], pw[:, 0, 1, :], pc[:], psn[:])
            for j in range(1, 11):
                csq(pw[:, j, 0, :], pw[:, j, 1, :], pw[:, j - 1, 0, :], pw[:, j - 1, 1, :])
            lmr = P.sb([128, 16], F32)
            lmi = P.sb([128, 16], F32)
            P.tt(lmr[:], r_[:], pw[:, 0, 0, :], ALU.mult, R + [bpw], W)
            P.tt(lmi[:], r_[:], pw[:, 0, 1, :], ALU.mult, R + [bpw], W)
            P.ts(tmpa[:], lmr[:], -1.0, None, ALU.add, None, R, W)
            P.tt(den[:], lre[:], lre[:], ALU.mult, R, W)
            P.tt(tmpb[:], lim[:], lim[:], ALU.mult, R, W)
            P.tt(den[:], den[:], tmpb[:], ALU.add, R, W)
            P.op("dve", lambda e: e.reciprocal(out=den[:], in_=den[:]), R, W)
            P.tt(cre[:], tmpa[:], lre[:], ALU.mult, R, W)
            P.tt(tmpb[:], lmi[:], lim[:], ALU.mult, R, W)
            P.tt(cre[:], cre[:], tmpb[:], ALU.add, R, W)
            P.tt(cre[:], cre[:], den[:], ALU.mult, R, W)
            P.tt(cim[:], lmi[:], lre[:], ALU.mult, R, W)
            P.tt(tmpb[:], tmpa[:], lim[:], ALU.mult, R, W)
            P.tt(cim[:], cim[:], tmpb[:], ALU.subtract, R, W)
            P.tt(cim[:], cim[:], den[:], ALU.mult, R, W)
            bbr = P.sb([128, 16, 16], F32)
            bbi = P.sb([128, 16, 16], F32)
            tb3 = P.sb([128, 16, 16], F32)
            creb = cre[:].unsqueeze(2).to_broadcast([128, 16, 16])
            cimb = cim[:].unsqueeze(2).to_broadcast([128, 16, 16])
            P.tt(bbr[:], bre[:], creb, ALU.mult, R, W)
            P.tt(tb3[:], bim[:], cimb, ALU.mult, R, W)
            P.tt(bbr[:], bbr[:], tb3[:], ALU.subtract, R, W)
            P.tt(bbi[:], bim[:], creb, ALU.mult, R, W)
            P.tt(tb3[:], bre[:], cimb, ALU.mult, R, W)
            P.tt(bbi[:], bbi[:], tb3[:], ALU.add, R, W)
            TBc = 128
            rowsel = P.sb([128, 4], F32)
            cmask8 = P.sb([128, 128], F32)
            selc = P.sb([128, 64], BF16)
            dvec8 = P.sb([128, 32], F32)
            P.dma("sp", rowsel[:], cd["rowsel"], writes=[bp])
            P.dma("sp", cmask8[:], cd["cmask8"], writes=[bp])
            P.dma("sp", selc[:], cd["selc"], writes=[bp])
            dsrc = ssm_d.rearrange("(g c) -> c g", c=16)
            for t_ in range(8):
                P.dma("sp", dvec8[16 * t_:16 * t_ + 16, :], dsrc, writes=[bp], slow=True)
            Bpad8 = P.sb([128, 16, 2, 2, 128], BF16)
            CgZ = P.sb([128, 16, 2, 2, 128], BF16)
            Tg = P.sb([128, 32, 128], BF16)
            Ct = P.sb([128, 16, TBc], F32)
            Dt = P.sb([128, 16, TBc], F32)
            r8 = P.sb([128, 16], F32)
            bBC = Buf("BC")
            btab = Buf("tab")
            wv = P.sb([128, 4, D], BF16)
            wg = P.sb([128, 4, D], BF16)
            bwv = Buf("wv")
            mark_tmp = P.off
            lp = P.sb([128, 9, 2, 16], F32)
            blp = Buf("lp")
            RL, WL = [bp, bs, bpw, blp], [blp, bs]
            P.memset(lp[:, 0, 0, :], 1.0, [blp])
            P.memset(lp[:, 0, 1, :], 0.0, [blp])
            for m_ in range(8):
                a_r, a_i = lp[:, m_, 0, :], lp[:, m_, 1, :]
                P.tt(tmpa[:], a_r, lmr[:], ALU.mult, RL, WL)
                P.tt(tmpb[:], a_i, lmi[:], ALU.mult, RL, WL)
                P.tt(lp[:, m_ + 1, 0, :], tmpa[:], tmpb[:], ALU.subtract, RL, WL)
                P.tt(tmpa[:], a_r, lmi[:], ALU.mult, RL, WL)
                P.tt(tmpb[:], a_i, lmr[:], ALU.mult, RL, WL)
                P.tt(lp[:, m_ + 1, 1, :], tmpa[:], tmpb[:], ALU.add, RL, WL)
            inv8 = P.sb([128, 2, 16], F32)
            P.tt(tmpa[:], r_[:], r_[:], ALU.mult, RL, WL)
            P.tt(tmpa[:], tmpa[:], tmpa[:], ALU.mult, RL, WL)
            P.tt(r8[:], tmpa[:], tmpa[:], ALU.mult, RL, WL + [btab])
            P.tt(tmpb[:], r8[:], r8[:], ALU.mult, RL + [btab], WL)
            P.op("dve", lambda e: e.reciprocal(out=tmpb[:], in_=tmpb[:]), RL, WL)
            P.tt(inv8[:, 0, :], lp[:, 8, 0, :], tmpb[:], ALU.mult, RL, WL)
            P.tt(tmpa[:], lp[:, 8, 1, :], tmpb[:], ALU.mult, RL, WL)
            P.ts(inv8[:, 1, :], tmpa[:], -1.0, None, ALU.mult, None, RL, WL)
            ccTp = P.sb([128, 16, 2, 16], F32)
            bcc = Buf("ccTp")
            pz, pzb = bank(2)
            for ri in range(2):
                for j in range(4):
                    P.tr(pz[0:64, 0:128], cct[:, ri, j, :], identf[:], [bp, bconst], [pzb])
                    for gp in range(2):
                        P.cp(ccTp[64 * gp:64 * gp + 64, 4 * j:4 * j + 4, ri, :],
                             pz[0:64, 0:128].rearrange("n (q g c) -> n q g c", q=4, g=2)[:, :, gp, :], [pzb], [bcc])
            B8 = P.sb([128, 16, 2, 8, 16], F32)
            C8 = P.sb([128, 16, 2, 8, 16], F32)
            CQ = P.sb([128, 16, 2, 8, 16], F32)
            tb3b = P.sb([128, 16, 16], F32)
            b8 = Buf("B8")
            R8, W8 = [bp, bs, blp, bcc, b8], [b8]

            def cmul_bc(out_r, out_i, a_r, a_i, s_r, s_i, shape):
                sr = s_r.unsqueeze(2).to_broadcast(shape)
                si = s_i.unsqueeze(2).to_broadcast(shape)
                P.tt(out_r, a_r, sr, ALU.mult, R8, W8)
                P.tt(tb3[:], a_i, si, ALU.mult, R8, W8)
                P.tt(out_r, out_r, tb3[:], ALU.subtract, R8, W8)
                P.tt(out_i, a_i, sr, ALU.mult, R8, W8)
                P.tt(tb3[:], a_r, si, ALU.mult, R8, W8)
                P.tt(out_i, out_i, tb3[:], ALU.add, R8, W8)
            lpr = P.sb([128, 8, 2, 16], F32)
            for t_ in range(8):
                P.cp(lpr[:, t_, :, :], lp[:, 7 - t_, :, :], R8, W8)
            tb4 = P.sb([128, 16, 8, 16], F32)
            SH4 = [128, 16, 8, 16]

            def cmul4(out_r, out_i, a_r, a_i, s_r, s_i):
                P.tt(out_r, a_r, s_r, ALU.mult, R8, W8)
                P.tt(tb4[:], a_i, s_i, ALU.mult, R8, W8)
                P.tt(out_r, out_r, tb4[:], ALU.subtract, R8, W8)
                P.tt(out_i, a_i, s_r, ALU.mult, R8, W8)
                P.tt(tb4[:], a_r, s_i, ALU.mult, R8, W8)
                P.tt(out_i, out_i, tb4[:], ALU.add, R8, W8)

            def bc_t(a3):
                return a3.unsqueeze(2).to_broadcast(SH4)

            def bc_c(tab):
                return tab.rearrange("q t p -> q p t").unsqueeze(3).to_broadcast(SH4)
            cmul4(B8[:, :, 0], B8[:, :, 1], bc_t(bbr[:]), bc_t(bbi[:]), bc_c(lpr[:, :, 0, :]), bc_c(lpr[:, :, 1, :]))
            cmul4(C8[:, :, 0], C8[:, :, 1], bc_t(ccTp[:, :, 0, :]), bc_t(ccTp[:, :, 1, :]),
                  bc_c(lp[:, 1:9, 0, :]), bc_c(lp[:, 1:9, 1, :]))
            i8r = inv8[:, 0, :].unsqueeze(2).unsqueeze(3).to_broadcast(SH4)
            i8i = inv8[:, 1, :].unsqueeze(2).unsqueeze(3).to_broadcast(SH4)
            cmul4(CQ[:, :, 0], CQ[:, :, 1], C8[:, :, 0], C8[:, :, 1], i8r, i8i)
            P.ts(CQ[:, :, 1, :, :], CQ[:, :, 1, :, :], -1.0, None, ALU.mult, None, R8, W8)
            Zh = P.sb([128, 16, 2, 128], F32)
            bZh = Buf("Zh")
            for ri in range(2):
                for gp in range(2):
                    P.ts(Zh[:, :, gp, :], B8[:, :, ri].rearrange("q p t c -> q p (t c)"), rowsel[:, gp:gp + 1], None, ALU.mult, None,
                         R8, [bZh])
                for p4 in range(8):
                    pzq, pzqb = bank(2 + p4 % 2)
                    for k_ in range(4):
                        idx = p4 * 4 + k_
                        P.tr(pzq[:, k_ * 128:(k_ + 1) * 128], Zh[:, idx // 2, idx % 2, :], identf[:], [bZh, bconst], [pzqb])
                    P.act(Bpad8[:, 2 * p4:2 * p4 + 2, :, ri, :], pzq.rearrange("q (a b c) -> q a b c", a=2, b=2), AF.Copy, [pzqb], [bBC])
                for gp in range(2):
                    sc_ = rowsel[:, gp:gp + 1] if ri == 0 else rowsel[:, 2 + gp:3 + gp]
                    P.ts(CgZ[:, :, gp, ri, :], C8[:, :, ri].rearrange("q p t c -> q p (t c)"), sc_, None, ALU.mult, None, R8, [bBC])
            tmpT = P.sb([128, 128], F32)
            btT = Buf("tmpT")
            pt_, ptb = bank(3)
            for g_ in range(32):
                p, gp = g_ // 2, g_ % 2
                rows = slice(64 * gp, 64 * gp + 64)
                for ri in range(2):
                    P.mm(pt_[:, 0:128], B8[rows, p, ri].rearrange("p t c -> p (t c)"), CQ[rows, p, ri].rearrange("p t c -> p (t c)"),
                         ri == 0, ri == 1, [b8], [ptb])
                P.tt(tmpT[:], pt_[:, 0:128], cmask8[:], ALU.mult, [ptb, bp], [btT])
                P.stt(Tg[:, g_, :], identf[:], dvec8[:, g_:g_ + 1], tmpT[:], ALU.mult, ALU.add, [bconst, bp, btT], [bBC])
            tA = P.sb([128, 16, TBc // 2], F32)
            tBb = P.sb([128, 16, TBc // 2], F32)
            P.memset(Ct[:, :, 0:1], 1.0, [btab])
            P.memset(Dt[:, :, 0:1], 0.0, [btab])
            RT, WT = [bp, bs, bpw, btab], [btab]
            for j in range(7):
                m = 1 << j
                cj = pw[:, j + 3, 0, :].unsqueeze(2).to_broadcast([128, 16, m])
                sj = pw[:, j + 3, 1, :].unsqueeze(2).to_broadcast([128, 16, m])
                P.tt(tA[:, :, 0:m], Ct[:, :, 0:m], cj, ALU.mult, RT, WT)
                P.tt(tBb[:, :, 0:m], Dt[:, :, 0:m], sj, ALU.mult, RT, WT)
                P.tt(Ct[:, :, m:2 * m], tA[:, :, 0:m], tBb[:, :, 0:m], ALU.subtract, RT, WT)
                P.tt(tA[:, :, 0:m], Dt[:, :, 0:m], cj, ALU.mult, RT, WT)
                P.tt(tBb[:, :, 0:m], Ct[:, :, 0:m], sj, ALU.mult, RT, WT)
                P.tt(Dt[:, :, m:2 * m], tA[:, :, 0:m], tBb[:, :, 0:m], ALU.add, RT, WT)
            stg = [P.sb([128, D], F32) for _ in range(2)]
            bstg = [Buf(), Buf()]
            k = 0
            for dst, src in ((wv, w_val), (wg, w_gate)):
                for j in range(4):
                    load_w_bf16(dst[:, j, :], src[j * 128:(j + 1) * 128, :], None, bwv, stg[k % 2][:], bstg[k % 2], conv_eng="dve")
                    k += 1
            P.barrier()
            P.off = mark_tmp
            Uk = [P.sb([128, 8, 512], BF16) for _ in range(2)]
            bUk = [Buf(), Buf()]
            U8d = [P.sb([128, 32, 128], BF16) for _ in range(2)]
            bU8d = [Buf("U8a"), Buf("U8b")]
            Uk2 = P.sb([128, 32, 128], BF16)
            bUk2 = Buf("Uk2")
            gmb = [P.sb([128, 8, 512], BF16) for _ in range(2)]
            bgmb = [Buf(), Buf()]
            lanes = []
            blw = []
            blinit = []
            for L_ in range(2):
                lanes.append((P.sb([128, TBc], F32), P.sb([128, TBc], F32), P.sb([128, TBc], F32), P.sb([128, TBc], F32),
                              P.sb([128, 2, TBc], F32), P.sb([128, 2, TBc], F32), P.sb([128, 2], F32), P.sb([128, 2], F32)))
                blw.append([Buf() for _ in range(6)])
                blinit.append(Buf())
            Hb2 = [[P.sb([128, 2, TBc + 2], BF16) for _ in range(16)] for _ in range(2)]
            bH2 = [[Buf() for _ in range(16)] for _ in range(2)]
            for q_ in range(2):
                for p in range(16):
                    P.memset(Hb2[q_][p][:], 0.0, [bH2[q_][p]])
            glast = P.sb([128, 16, 2], F32)
            bgl = [Buf() for _ in range(16)]
            P.memset(glast[:], 0.0, bgl)
            init = P.sb([128, 2], F32)
            binit = Buf()
            ti = P.sb([128, 2], F32)
            Ytok = P.sb([128, 8, 512], BF16)
            bY = Buf("Ytok")
            gy = P.sb([128, 4, 512], BF16)
            bgy = Buf()
            sgt = P.sb([128, 512], F32)
            bsg = Buf()
            ybt = P.sb([128, 512], F32)
            bybt = Buf()
            yb_sb = [P.sb([128, 8, 512], BF16) for _ in range(2)]
            byb = [Buf(), Buf()]
            psT = ps[0][:, 0:512].bitcast(BF16)
            bpsT = psb[0][0]
            NBK = T // (8 * TBc)

            def prep(blk):
                ub, bub = Uk[blk % 2], bUk[blk % 2]
                U8_, bU8_ = U8d[blk % 2], bU8d[blk % 2]
                P.dma("sp", ub[:].rearrange("p t c -> p (t c)"),
                      UTM[blk * 1024:(blk + 1) * 1024, :].rearrange("(k t) c -> k (t c)", t=8), writes=[bub])
                for hh in range(2):
                    P.act(Uk2[:, 16 * hh:16 * hh + 16, :].rearrange("p g (t c) -> p g t c", t=8),
                          ub[:, :, 256 * hh:256 * hh + 256].rearrange("p t (g c) -> p g t c", g=16), AF.Copy, [bub], [bUk2])
                for gb in range(4):
                    for gq in range(8):
                        g_ = gb * 8 + gq
                        P.tr(psT[:, gq * 128:(gq + 1) * 128], Uk2[:, g_, :], ident[:], [bUk2, bconst], [bpsT])
                    P.act(U8_[:, gb * 8:(gb + 1) * 8, :], psT.rearrange("p (g k) -> p g k", g=8), AF.Copy, [bpsT], [bU8_])

            def pair_steps(blk, p, L):
                U8_, bU8_ = U8d[blk % 2], bU8d[blk % 2]
                Hn, bHn = Hb2[blk % 2][p], bH2[blk % 2][p]
                Ho, bHo = Hb2[(blk + 1) % 2][p], bH2[(blk + 1) % 2][p]
                w1_, w2_, w3_, w4_, gin_, G_, init_, ti_ = lanes[L]
                bwL, binitL = blw[L], blinit[L]
                st = []
                sp_, spb = bank(1 + (p % 2))
                S = sp_.rearrange("p (a t) -> p a t", a=2)
                C_, D_ = Ct[:, p, :], Dt[:, p, :]
                ec, es = pw[:, 10, 0, p:p + 1], pw[:, 10, 1, p:p + 1]
                rb = r8[:, p:p + 1].to_broadcast([128, TBc])

                def s0():
                    for ri in range(2):
                        for gp in range(2):
                            P.mm(S[:, ri, 0:TBc], Bpad8[:, p, gp, ri, :], U8_[:, 2 * p + gp, :], gp == 0 and ri == 0, gp == 1, [bBC, bU8_], [spb])
                st.append(s0)
                st.append(lambda: P.tt(w1_[:], S[:, 0, 0:TBc], C_, ALU.mult, [spb, btab], [bwL[0]]))
                st.append(lambda: P.tt(w2_[:], S[:, 1, 0:TBc], D_, ALU.mult, [spb, btab], [bwL[1]]))
                st.append(lambda: P.tt(gin_[:, 0, :], w1_[:], w2_[:], ALU.add, [bwL[0], bwL[1]], [bwL[4]]))
                st.append(lambda: P.tt(w3_[:], S[:, 1, 0:TBc], C_, ALU.mult, [spb, btab], [bwL[2]]))
                st.append(lambda: P.tt(w4_[:], S[:, 0, 0:TBc], D_, ALU.mult, [spb, btab], [bwL[3]]))
                st.append(lambda: P.tt(gin_[:, 1, :], w3_[:], w4_[:], ALU.subtract, [bwL[2], bwL[3]], [bwL[4]]))
                st.append(lambda: P.ts(ti_[:, 0:1], glast[:, p, 1:2], es, None, ALU.mult, None, [bgl[p], bpw], [binitL]))
                st.append(lambda: P.stt(init_[:, 0:1], glast[:, p, 0:1], ec, ti_[:, 0:1], ALU.mult, ALU.subtract, [bgl[p], bpw, binitL], [binitL]))
                st.append(lambda: P.ts(ti_[:, 1:2], glast[:, p, 0:1], es, None, ALU.mult, None, [bgl[p], bpw], [binitL]))
                st.append(lambda: P.stt(init_[:, 1:2], glast[:, p, 1:2], ec, ti_[:, 1:2], ALU.mult, ALU.add, [bgl[p], bpw, binitL], [binitL]))
                for a in range(2):
                    st.append((lambda a=a: (lambda: P.op("dve", lambda e: e.tensor_tensor_scan(
                        out=G_[:, a, :], data0=rb, data1=gin_[:, a, :], initial=init_[:, a:a + 1],
                        op0=ALU.mult, op1=ALU.add), [bwL[4], binitL, btab], [bwL[5]])))())
                st.append(lambda: P.cp(glast[:, p, :], G_[:, :, TBc - 1], [bwL[5]], [bgl[p]]))
                st.append(lambda: P.cp(Hn[:, :, 0:1], Ho[:, :, TBc:TBc + 1], [bHo], [bHn]))
                st.append(lambda: P.tt(w1_[:], G_[:, 0, :], C_, ALU.mult, [bwL[5], btab], [bwL[0]]))
                st.append(lambda: P.tt(w2_[:], G_[:, 1, :], D_, ALU.mult, [bwL[5], btab], [bwL[1]]))
                st.append(lambda: P.tt(Hn[:, 0, 1:TBc + 1], w1_[:], w2_[:], ALU.subtract, [bwL[0], bwL[1]], [bHn]))
                st.append(lambda: P.tt(w3_[:], G_[:, 0, :], D_, ALU.mult, [bwL[5], btab], [bwL[2]]))
                st.append(lambda: P.tt(w4_[:], G_[:, 1, :], C_, ALU.mult, [bwL[5], btab], [bwL[3]]))
                st.append(lambda: P.tt(Hn[:, 1, 1:TBc + 1], w3_[:], w4_[:], ALU.add, [bwL[2], bwL[3]], [bHn]))
                return st

            def out_slices(blk):
                U8_, bU8_ = U8d[blk % 2], bU8d[blk % 2]
                Hs, bHs = Hb2[blk % 2], bH2[blk % 2]
                ybs = yb_sb[blk % 2]

                def y_part(gb):
                    py, pyb = bank(3 + gb % 2)
                    for gq in range(4):
                        g_ = gb * 4 + gq
                        p, gp = g_ // 2, g_ % 2
                        o_ = py[:, gq * 128:(gq + 1) * 128]
                        P.mm(o_, U8_[:, g_, :], Tg[:, g_, :], gq == 0, False, [bU8_, bBC], [pyb])
                        P.mm(o_, Hs[p][:, 0, 0:TBc], CgZ[:, p, gp, 0, :], False, False, [bHs[p], bBC], [pyb])
                        P.mm(o_, Hs[p][:, 1, 0:TBc], CgZ[:, p, gp, 1, :], False, gq == 3, [bHs[p], bBC], [pyb])
                    P.act(Ytok[:, :, gb * 64:(gb + 1) * 64].rearrange("p t (g c) -> p t g c", g=4),
                          py.rearrange("p (g t c) -> p t g c", g=4, t=8), AF.Copy, [pyb], [bY])

                def sel_part(cb):
                    pyt, pytb = bank(5)
                    for t_ in range(8):
                        P.mm(pyt[:, t_ * 64:(t_ + 1) * 64], Ytok[:, t_, cb * 128:(cb + 1) * 128], selc[:], t_ == 0, t_ == 7, [bY, bp], [pytb])
                    P.act(gy[:, cb, :].rearrange("p (j t) -> p j t", t=8), pyt.rearrange("p (t j) -> p j t", t=8),
                          AF.Gelu_apprx_tanh, [pytb], [bgy])

                def glu_part(oc):
                    pv_, pvb = bank(6)
                    pg_, pgb = bank(7)
                    for j in range(4):
                        P.mm(pv_, wv[:, j, oc * 128:(oc + 1) * 128], gy[:, j, :], j == 0, j == 3, [bwv, bgy], [pvb])
                    for j in range(4):
                        P.mm(pg_, wg[:, j, oc * 128:(oc + 1) * 128], gy[:, j, :], j == 0, j == 3, [bwv, bgy], [pgb])
                    P.act(sgt[:], pg_, AF.Sigmoid, [pgb], [bsg])
                    P.tt(ybt[:], pv_, sgt[:], ALU.mult, [pvb, bsg], [bybt])
                    P.tt(ybs[:, oc, :], ybt[:], gmb[blk % 2][:, oc, :], ALU.mult, [bybt, bgmb[blk % 2]], [byb[blk % 2]])

                def fin():
                    P.dma(nq(), YB[:, :, blk * 512:(blk + 1) * 512], ybs[:], reads=[byb[blk % 2]])
                sl = [[lambda gb=gb: y_part(gb) for gb in (2 * k_, 2 * k_ + 1)] for k_ in range(4)]
                sl.append([lambda cb=cb: sel_part(cb) for cb in (0, 1)])
                sl.append([lambda cb=cb: sel_part(cb) for cb in (2, 3)])
                sl.append([lambda oc=oc: glu_part(oc) for oc in range(4)])
                sl.append([lambda oc=oc: glu_part(oc) for oc in range(4, 8)] + [fin])
                return sl

            prep(0)
            for blk in range(NBK + 1):
                if blk >= 1:
                    ob = blk - 1
                    P.dma("sp", gmb[ob % 2][:], GM[:, 8:16, ob * 512:(ob + 1) * 512], writes=[bgmb[ob % 2]])
                osl = out_slices(blk - 1) if blk >= 1 else [[] for _ in range(8)]
                for pp in range(8):
                    if blk < NBK:
                        sa, sb_ = pair_steps(blk, 2 * pp, 0), pair_steps(blk, 2 * pp + 1, 1)
                        for fa, fb in zip(sa, sb_):
                            fa()
                            fb()
                    for fn_ in osl[pp]:
                        fn_()
                    if pp == 3 and blk + 1 < NBK:
                        prep(blk + 1)

        def phase2():
            P.off = persist_off
            bc2 = Buf("c2")
            kcT = P.sb([128, 512], BF16, "kcT")
            vca = P.sb([128, 4, 2, 128], BF16, "vca")
            bkc = Buf("kcT")
            bvca = Buf("vca")
            P.memset(vca[:], 1.0, [bvca])
            P.memset(kcT[:], 0.0, [bkc])
            ksT = P.sb([128, T], BF16, "ksT")
            kwT = P.sb([128, T], BF16, "kwT")
            vsw = P.sb([128, NT, 2, 2, 128], BF16, "vsw")
            we = P.sb([128, T], BF16, "we")
            bK, bV = Buf("K"), Buf("V")
            mark = P.off
            w1b = P.sb([128, 32, 256], BF16, "w1b")
            w2b = P.sb([128, 2, 64], BF16, "w2b")
            peT = P.sb([64, 32], F32, "peT")
            peTb = P.sb([64, 32], BF16, "peTb")
            P.dma("sp", peT[:], cmp_pe.rearrange("j d -> d j"), writes=[bc2], slow=True)
            P.cp(peTb[:], peT[:], [bc2], [bc2])
            raw = P.sb([128, T + 32], BF16, "raw")
            braw = Buf("raw")
            braw0, braw1 = Buf("raw0"), Buf("raw1")
            bsth = [[Buf(), Buf()], [Buf(), Buf()]]
            stg = [P.sb([128, 8, 256], F32, "stgc") for _ in range(2)]
            bstg = [Buf(), Buf()]
            stg2 = P.sb([128, 2, 64], F32, "stg2")
            bw1 = Buf("w1")
            hid = P.sb([128, 2, 512], BF16, "hid")
            bhid = Buf("hid")
            hbias = P.sb([128, 2], F32, "hbias")
            bhb_ = Buf("hbias")
            P.memset(raw[:, T:T + 32], 0.0, [braw])

            def resident_loads():
                P.dma("sp", ksT[:], KS, writes=[bK])
                P.dma("sp", kwT[:], KW, writes=[bK])
                for c4 in range(8):
                    P.dma(nq(), vsw[:, c4 * 8:(c4 + 1) * 8], VSW[c4 * 8:(c4 + 1) * 8].rearrange("t p s g d -> p t s g d"), writes=[bV])
                P.dma("sp", we[:], cd["we"], writes=[bc2])
            for kv in range(2):
                for jq in range(4):
                    for half in range(2):
                        P.dma(("sp", "pool")[half], stg[jq % 2][64 * half:64 * half + 64, :, :],
                              cw1[kv][jq * 512:(jq + 1) * 512, :].rearrange("(j d) h -> d j h", d=64), writes=[bsth[jq % 2][half]])
                    P.cp(w1b[:, jq * 8:(jq + 1) * 8, :], stg[jq % 2][:], bsth[jq % 2], [bw1] + bsth[jq % 2], eng="dve")
                P.dma("sp", stg2[:], cw2[kv].rearrange("(a p) d -> p a d", p=128), writes=[bc2])
                P.cp(w2b[:], stg2[:], [bc2], [bw1])
                P.dma("sp", raw[:, 0:T // 2], (KC if kv == 0 else VC)[:, 0:T // 2], writes=[braw0])
                P.dma("pool", raw[:, T // 2:T], (KC if kv == 0 else VC)[:, T // 2:T], writes=[braw1])
                if kv == 0:
                    resident_loads()
                pbi, pbib = bank(7)
                for hh in range(2):
                    for j in range(32):
                        P.mm(pbi[:, hh:hh + 1], w1b[0:64, j, hh * 128:(hh + 1) * 128], peTb[:, j:j + 1], j == 0, j == 31, [bw1, bc2], [pbib])
                P.cp(hbias[:], pbi[:, 0:2], [pbib], [bhb_])
                for g in range(2):
                    rows = slice(64 * g, 64 * g + 64)
                    for hh in range(2):
                        ph, phb = bank(2 + hh)
                        for j in range(32):
                            rhs = raw[rows, j:j + 16 * 512].rearrange("p (n s) -> p n s", s=16)[:, :, 0]
                            P.mm(ph, w1b[rows, j, hh * 128:(hh + 1) * 128], rhs, j == 0, j == 31, [bw1, braw, braw0, braw1], [phb])
                        P.act(hid[:, hh, :], ph, AF.Gelu_apprx_tanh, [phb, bhb_], [bhid], bias=hbias[:, hh:hh + 1])
                    if kv == 0:
                        po, pob = bank(4)
                        for hh in range(2):
                            P.mm(po[0:64, :], w2b[:, hh, :], hid[:, hh, :], hh == 0, hh == 1, [bw1, bhid], [pob])
                        P.cp(kcT[rows, 0:511], po[0:64, 0:511], [pob], [bkc])
                    else:
                        for nt in range(4):
                            po, pob = bank(4 + nt % 2)
                            for hh in range(2):
                                P.mm(po[:, 0:64], hid[:, hh, nt * 128:(nt + 1) * 128], w2b[:, hh, :], hh == 0, hh == 1, [bhid, bw1], [pob])
                            P.cp(vca[:, nt, g, 0:64], po[:, 0:64], [pob], [bvca])
            P.barrier()
            P.off = mark
            i4 = P.sb([128, 512], BF16, "i4")
            wmask = P.sb([128, 4, 128], BF16, "wmask")
            smask = P.sb([128, 2, 128], BF16, "smask")
            cbase = P.sb([128, 288], BF16, "cbase")
            ov = P.sb([128, 4, 128], BF16, "ov")
            wf = P.sb([128, 256], F32, "wf")
            ones = P.sb([128, 1], BF16, "ones")
            for dst, nm in ((i4, "i4"), (wmask, "wmask"), (smask, "smask"), (cbase, "cbase"), (ov, "ov"), (wf, "wf")):
                P.dma(nq(), dst[:], cd[nm], writes=[bc2])
            P.memset(ones[:], 1.0, [bc2])
            wat = P.sb([128, 4, D], BF16, "wat")
            wo = P.sb([128, 8, D], BF16, "wo")
            bwat = Buf("wat")
            mark2 = P.off
            stg = [P.sb([128, 4, D], F32, "stga") for _ in range(2)]
            bstg = [Buf(), Buf()]
            for g in range(2):
                P.dma(nq(), stg[0][64 * g:64 * g + 64, :, :],
                      w_attn[g * 256:(g + 1) * 256, :].rearrange("(hl d) o -> d hl o", d=64), writes=[bstg[0]])
            P.cp(wat[:], stg[0][:], [bstg[0]], [bwat], eng="dve")
            for hf in range(2):
                P.dma(nq(), stg[1 - hf][:], w_out[hf * 512:(hf + 1) * 512, :].rearrange("(c p) o -> p c o", p=128), writes=[bstg[1 - hf]])
                P.cp(wo[:, hf * 4:(hf + 1) * 4, :], stg[1 - hf][:], [bstg[1 - hf]], [bwat], eng="dve")
            P.barrier()
            P.off = mark2
            qr_t = [[P.sb([128, 4, 128], BF16, "qrt") for _g in range(2)] for _ in range(2)]
            qp_t = [[P.sb([128, 4, 128], BF16, "qpt") for _g in range(2)] for _ in range(2)]
            gbc_t = [P.sb([64, 2, 3, 4, 128], BF16, "gbc") for _ in range(2)]
            gm_t1 = P.sb([128, 8, 128], BF16, "gmt")
            yb_t1 = P.sb([128, 8, 128], BF16, "ybt")
            x_t1 = P.sb([128, D], F32, "xt2")
            gm_t, yb_t, x_t = [gm_t1] * 2, [yb_t1] * 2, [x_t1] * 2
            bq = [Buf(), Buf()]
            bq21 = Buf()
            bq2 = [bq21, bq21]
            NPT = 3
            PT = [P.sb([128, 512], BF16, "PT") for _ in range(NPT)]
            bPT = [Buf() for _ in range(NPT)]
            rdq = P.sb([128, 4], F32, "rdq")
            brdq = Buf()
            imp = P.sb([128, 128], F32, "imp")
            sc2 = P.sb([128, 128], F32, "sc2")
            m8 = P.sb([128, 16], F32, "m8")
            bimp = Buf()
            selb = P.sb([128, 128], BF16, "selb")
            bselb = Buf()
            selbT = [P.sb([128, 4, 128], BF16, "selbT") for _ in range(2)]
            selbs = [P.sb([128, 128], BF16, "selbs") for _ in range(2)]
            bselbs = [Buf(), Buf()]
            bselbT = [Buf(), Buf()]
            off_rden = P.off
            rden = P.sb([64, 512], F32, "rden")
            coef = P.sb([64, 512], F32, "coef")
            ctb = P.sb([64, 512], F32, "ctb")
            lnd = P.sb([64, 512], F32, "lnd")
            blnd = Buf()
            off_acc = P.off
            accT = [P.sb([64, 512], F32, "accT") for _ in range(2)]
            bacc = [Buf(), Buf()]
            bcomb = Buf()
            nsaT = P.sb([128, 4, 128], BF16, "nsaT")
            bnsa = Buf()
            m1 = P.sb([128, 8, 128], F32, "m1")
            bm1 = Buf()
            mT = P.sb([128, 8, 128], BF16, "mT")
            bmT = Buf()
            x2s = P.sb([128, D], F32, "x2s")
            bx2 = Buf()
            NGr = NG.rearrange("(g hl br) t -> g br hl t", g=2, hl=4, br=3)

            def loads(i):
                b2 = i % 2
                t0 = i * 128
                for g in range(2):
                    rw = slice(64 * g, 64 * g + 64)
                    P.dma("sp", qr_t[b2][g][rw], QR[rw, :, t0:t0 + 128], writes=[bq[b2]])
                    P.dma("sp", qp_t[b2][g][rw], QP[rw, :, t0:t0 + 128], writes=[bq[b2]])
                for g in range(2):
                    for br in range(3):
                        P.dma("sp", gbc_t[b2][:, g, br], NGr[g, br:br + 1, :, t0:t0 + 128].to_broadcast([64, 4, 128]),
                              writes=[bq[b2]], slow=True)

            def loads_e(i):
                b2 = i % 2
                t0 = i * 128
                P.dma("sp", gm_t[b2][:], GM[:, 0:8, t0:t0 + 128], writes=[bq2[b2]])
                P.dma("sp", yb_t[b2][:], YB[:, :, t0:t0 + 128], writes=[bq2[b2]])
                P.dma("sp", x_t[b2][:], xown[t0:t0 + 128, :], writes=[bq2[b2]])

            jobs = []
            oacc_banks = [2, 3, 7]
            oacc_i = [0]
            for i in range(NTO):
                nkc = (8 * (2 * i + 1) + 6) // 128 + 1
                k0 = max(0, 2 * i - 4)
                for br, kts in ((0, list(range(nkc))), (2, list(range(k0, 2 * i + 2))), (1, list(range(2 * i + 2)))):
                    for g in range(2):
                        oacc_i[0] += 1
                        ob = oacc_banks[oacc_i[0] % 3]
                        for n_, kt in enumerate(kts):
                            jobs.append(dict(i=i, g=g, br=br, kt=kt, first=(n_ == 0), last=(n_ == len(kts) - 1), ob=ob,
                                             tile_first=(br == 0 and g == 0 and n_ == 0),
                                             tile_last=(br == 1 and g == 1 and n_ == len(kts) - 1)))
            sbi = [0]
            pti = [0]

            def score(J):
                i, g, br, kt = J["i"], J["g"], J["br"], J["kt"]
                b2 = i % 2
                rows = slice(64 * g, 64 * g + 64)
                masks = []
                if br == 0:
                    lhsT, rk, q_ap = kcT[:, kt * 128:(kt + 1) * 128], [bkc], qr_t[b2][g][:, :, :]
                    Dv = 16 * i - 128 * kt
                    if Dv < 130:
                        s0 = 136 - Dv
                        masks.append((cbase[:, s0:s0 + 128], i4[:], [bc2]))
                elif br == 1:
                    lhsT, rk, q_ap = ksT[:, kt * 128:(kt + 1) * 128], [bK], qp_t[b2][g][:, :, :]
                    masks.append((we[:, kt * 128:(kt + 1) * 128], selbT[g][:].rearrange("p h q -> p (h q)"), [bc2, bselbT[g]]))
                    if kt - 2 * i in (0, 1):
                        masks.append((smask[:, kt - 2 * i, :], i4[:], [bc2]))
                else:
                    lhsT, rk, q_ap = kwT[:, kt * 128:(kt + 1) * 128], [bK], qp_t[b2][g][:, :, :]
                    pofs = {1: 0, 0: 1, -3: 2, -4: 3}
                    if kt - 2 * i in pofs:
                        masks.append((wmask[:, pofs[kt - 2 * i], :], i4[:], [bc2]))
                sbi[0] += 1
                s_ps, s_pb = bank(sbi[0] % 2)
                n = len(masks)
                P.mm(s_ps, lhsT, q_ap.rearrange("p h q -> p (h q)"), True, n == 0, rk + [bq[b2]], [s_pb])
                for mi, (ml, mr, mrd) in enumerate(masks):
                    P.mm(s_ps, ml, mr, False, mi == n - 1, mrd, [s_pb])
                pti[0] += 1
                k = pti[0] % NPT
                P.act(PT[k][:], s_ps, AF.Exp, [s_pb], [bPT[k]], scale=0.125)
                J["pt"] = (PT[k], bPT[k])

            def combine(J):
                i, g, br = J["i"], J["g"], J["br"]
                b2 = i % 2
                o_ps, o_pb = bank(J["ob"])
                if br == 0 and i == 0:
                    P.ts(rden[:], o_ps[64:128, :], 1e-30, None, ALU.add, None, [o_pb], [bcomb])
                    P.op("dve", lambda e: e.reciprocal(out=rden[:], in_=rden[:]), [bcomb], [bcomb])
                else:
                    P.act(lnd[:], o_ps[64:128, :], AF.Ln, [o_pb], [blnd])
                    P.act(rden[:], lnd[:], AF.Exp, [blnd], [bcomb], scale=-1.0)
                P.tt(coef[:], rden[:], gbc_t[b2][:, g, br].rearrange("p h q -> p (h q)"), ALU.mult, [bcomb, bq[b2]], [bcomb])
                if br == 0:
                    P.tt(accT[g][:], o_ps[0:64, :], coef[:], ALU.mult, [o_pb, bcomb], [bacc[g]])
                elif br == 2:
                    P.tt(ctb[:], o_ps[0:64, :], coef[:], ALU.mult, [o_pb, bcomb], [bcomb])
                    P.tt(accT[g][:], accT[g][:], ctb[:], ALU.add, [bcomb, bacc[g]], [bacc[g]])
                else:
                    P.tt(ctb[:], o_ps[0:64, :], coef[:], ALU.mult, [o_pb, bcomb], [bcomb])
                    P.tt(nsaT[64 * g:64 * g + 64, :, :], accT[g][:].rearrange("p (h q) -> p h q", h=4),
                         ctb[:].rearrange("p (h q) -> p h q", h=4), ALU.add, [bcomb, bacc[g]], [bnsa])

            def selection(J):
                i, g = J["i"], J["g"]
                imp_ps, imp_pb = bank(4 + g)
                dn_ps, dn_pb = bank(6)
                P.ts(rdq[:], dn_ps[:, 4 * g:4 * g + 4], 1e-30, None, ALU.add, None, [dn_pb], [brdq])
                P.op("dve", lambda e: e.reciprocal(out=rdq[:], in_=rdq[:]), [brdq], [brdq])
                P.ts(imp[:], imp_ps[:, 0:128], rdq[:, 0:1], None, ALU.mult, None, [imp_pb, brdq], [bimp])
                for hl in range(1, 4):
                    P.stt(imp[:], imp_ps[:, hl * 128:(hl + 1) * 128], rdq[:, hl:hl + 1], imp[:], ALU.mult, ALU.add,
                          [imp_pb, brdq, bimp], [bimp])
                P.tt(imp[:], imp[:], wf[:, 128 - 4 * i:256 - 4 * i], ALU.add, [bimp, bc2], [bimp])
                P.ts(imp[:, 0:1], imp[:, 0:1], 1000.0, None, ALU.add, None, [bimp], [bimp])
                P.op("dve", lambda e: e.max(out=m8[:, 0:8], in_=imp[:]), [bimp], [bimp])
                P.op("dve", lambda e: e.match_replace(out=sc2[:], in_to_replace=m8[:, 0:8], in_values=imp[:], imm_value=-3e38),
                     [bimp], [bimp])
                P.op("dve", lambda e: e.max(out=m8[:, 8:16], in_=sc2[:]), [bimp], [bimp])
                P.ts(selb[:], imp[:], m8[:, 15:16], NEG, ALU.is_lt, ALU.mult, [bimp], [bselb])
                P.cp(selbs[g][:], selb[:], [bselb], [bselbs[g]])

            def selection_b(i, g):
                dn_ps, dn_pb = bank(6)
                tpv = dn_ps.bitcast(BF16)[:, 256 + 128 * g:384 + 128 * g]
                P.tr(tpv, selbs[g][:], ident[:], [bselbs[g], bconst], [dn_pb])
                P.cp(selbT[g][:], tpv.unsqueeze(1).to_broadcast([128, 4, 128]), [dn_pb], [bselbT[g]])

            def pv(J):
                i, g, br, kt = J["i"], J["g"], J["br"], J["kt"]
                pt, bpt = J["pt"]
                o_ps, o_pb = bank(J["ob"])
                if br == 0:
                    vl, rv = vca[:, kt, g, :], [bvca]
                elif br == 1:
                    vl, rv = vsw[:, kt, 0, g, :], [bV]
                else:
                    vl, rv = vsw[:, kt, 1, g, :], [bV]
                P.mm(o_ps, vl, pt[:], J["first"], J["last"], rv + [bpt], [o_pb])
                if br == 0:
                    imp_ps, imp_pb = bank(4 + g)
                    dn_ps, dn_pb = bank(6)
                    for hl in range(4):
                        P.mm(imp_ps[:, hl * 128:(hl + 1) * 128], pt[:, hl * 128:(hl + 1) * 128], ov[:, kt, :],
                             J["first"] and hl == 0, J["last"], [bpt, bc2], [imp_pb])
                    for hl in range(4):
                        P.mm(dn_ps[:, 4 * g + hl:4 * g + hl + 1], pt[:, hl * 128:(hl + 1) * 128], ones[:],
                             J["first"] and hl == 0, J["last"], [bpt, bc2], [dn_pb])

            def epi_ya(i, h):
                b2 = i % 2
                ya, yab = bank(6)
                for oc4 in range(4):
                    ocn = 4 * h + oc4
                    for hl in range(4):
                        P.mm(ya[:, oc4 * 128:(oc4 + 1) * 128], wat[:, hl, ocn * 128:(ocn + 1) * 128], nsaT[:, hl, :],
                             hl == 0 and oc4 == 0, hl == 3, [bwat, bnsa], [yab])
                P.tt(m1[:, 4 * h:4 * h + 4, :], ya.rearrange("p (c q) -> p c q", c=4), gm_t[b2][:, 4 * h:4 * h + 4, :], ALU.mult,
                     [yab, bq2[b2]], [bm1])
                P.tt(mT[:, 4 * h:4 * h + 4, :], m1[:, 4 * h:4 * h + 4, :], yb_t[b2][:, 4 * h:4 * h + 4, :], ALU.add, [bm1, bq2[b2]], [bmT])

            def epi_xo(i, hf):
                b2 = i % 2
                t0 = i * 128
                xo, xob = bank(6)
                for kc in range(8):
                    P.mm(xo, mT[:, kc, :], wo[:, kc, hf * 512:(hf + 1) * 512], kc == 0, kc == 7, [bmT, bwat], [xob])
                P.tt(x2s[:, hf * 512:(hf + 1) * 512], xo, x_t[b2][:, hf * 512:(hf + 1) * 512], ALU.add, [xob, bq2[b2]], [bx2])
                if hf == 1:
                    P.dma("sp", X2[t0:t0 + 128, :], x2s[:], reads=[bx2])
                    if i + 1 < NTO:
                        loads_e(i + 1)

            for b2_ in range(2):
                for g_ in range(2):
                    ow = slice(64 * (1 - g_), 64 * (1 - g_) + 64)
                    P.memset(qr_t[b2_][g_][ow], 0.0, [bq[b2_]])
                    P.memset(qp_t[b2_][g_][ow], 0.0, [bq[b2_]])
            loads(0)
            loads_e(0)
            score(jobs[0])
            pend = []
            idxS = {}
            for j, J in enumerate(jobs):
                if J["br"] == 1 and J["first"]:
                    idxS[(J["i"], J["g"])] = j
            for j, J in enumerate(jobs):
                pend.sort(key=lambda t_: t_[0])
                while pend and pend[0][0] <= j:
                    _, fn_, ar_, h_ = pend.pop(0)
                    fn_(ar_, h_)
                if J["tile_first"] and J["i"] + 1 < NTO:
                    loads(J["i"] + 1)
                if j + 1 < len(jobs):
                    score(jobs[j + 1])
                pv(J)
                if J["last"]:
                    combine(J)
                    if J["br"] == 0:
                        selection(J)
                        tS = idxS[(J["i"], J["g"])]
                        pend.append((max(j + 1, min(j + 5, tS - 1)), selection_b, J["i"], J["g"]))
                if J["br"] == 2 and J["g"] == 0 and J["first"] and J["i"] > 0:
                    ip = J["i"] - 1
                    pend.extend([(j + 1, epi_ya, ip, 0), (j + 4, epi_ya, ip, 1), (j + 7, epi_xo, ip, 0), (j + 10, epi_xo, ip, 1)])
                if j == len(jobs) - 1:
                    pend.extend([(j, epi_ya, J["i"], 0), (j, epi_ya, J["i"], 1), (j, epi_xo, J["i"], 0), (j, epi_xo, J["i"], 1)])
                if J["tile_last"]:
                    pend.sort(key=lambda t_: (t_[1] is not selection_b, t_[0]))
                    keep = []
                    while pend:
                        it = pend.pop(0)
                        if it[1] is selection_b and it[2] != J["i"]:
                            keep.append(it)
                        else:
                            it[1](it[2], it[3])
                    pend = keep

        def phase3():
            P.off = persist_off
            wu = P.sb([128, 8, 4096], BF16, "wu")
            wd = P.sb([128, 32, D], BF16, "wd")
            bwu = Buf("wu")
            stg = [P.sb([128, 4096], F32, "stg3") for _ in range(2)]
            bstq = [[Buf() for _ in range(4)] for _ in range(2)]
            mark3 = None
            bstg = [Buf(), Buf()]
            k = 0
            for kc in range(8):
                for c_ in range(4):
                    P.dma(bulkq(), stg[k % 2][:, c_ * 1024:(c_ + 1) * 1024], w_up[kc * 128:(kc + 1) * 128, c_ * 1024:(c_ + 1) * 1024],
                          writes=[bstq[k % 2][c_]])
                P.cp(wu[:, kc, :], stg[k % 2][:], bstq[k % 2], [bwu] + bstq[k % 2], eng="dve")
                k += 1
            for c4 in range(8):
                for c_ in range(4):
                    P.dma(bulkq(), stg[k % 2][:, c_ * 1024:(c_ + 1) * 1024],
                          w_down[c4 * 512 + c_ * 128:c4 * 512 + (c_ + 1) * 128, :], writes=[bstq[k % 2][c_]])
                P.cp(wd[:, c4 * 4:(c4 + 1) * 4, :], stg[k % 2][:].rearrange("p (c o) -> p c o", c=4), bstq[k % 2], [bwu] + bstq[k % 2],
                     eng="dve")
                k += 1
            P.barrier()
            P.off -= 2 * 4096 * 4
            g2T = P.sb([128, 8], F32)
            gfb = P.sb([128, D], F32)
            bc3 = Buf()
            P.dma("sp", g2T[:], g_mlp.rearrange("(c p) -> p c", p=128), writes=[bc3], slow=True)
            P.dma("sp", gfb[:], g_fin.rearrange("(a d) -> a d", a=1).to_broadcast([128, D]), writes=[bc3], slow=True)
            xt2 = [P.sb([128, 2, D], F32) for _ in range(2)]
            bxt2 = [[Buf(), Buf()], [Buf(), Buf()]]
            junk = P.sb([128, D], BF16)
            bjunk = Buf()
            ssq = P.sb([128, 8], F32)
            bss = [Buf() for _ in range(4)]
            hb2 = [P.sb([128, D], BF16) for _ in range(2)]
            bhb2 = [Buf(), Buf()]
            hT2 = [P.sb([128, 8, 256], BF16) for _ in range(2)]
            bhT2 = [Buf(), Buf()]
            rl = P.sb([128, 256], F32)
            brl = Buf()
            hidT = P.sb([128, 32, 256], BF16)
            bhid = Buf()
            x3 = P.sb([128, D], F32)
            bx3 = Buf()
            junk2 = junk
            ss2 = P.sb([128, 2], F32)
            bss2 = Buf()
            ot1 = P.sb([128, D], F32)
            bot1 = Buf()
            psT2 = [ps[0][:, 0:512].bitcast(BF16), ps[0][:, 512:1024].bitcast(BF16)]
            bpsT2 = [psb[0][0], psb[0][1]]
            NB3 = NTO // 2

            def stats3(blk):
                xt = xt2[blk % 2]
                for tt in range(2):
                    tile = blk * 2 + tt
                    bx = bxt2[blk % 2][tt]
                    P.dma(nq(), xt[:, tt, :], X2[tile * 128:(tile + 1) * 128, :], writes=[bx])
                    P.act(junk[:], xt[:, tt, :], AF.Square, [bx], [bjunk, bss[tt]], accum=ssq[:, tt:tt + 1])
                    P.ts(ssq[:, 4 + tt:5 + tt], ssq[:, tt:tt + 1], 1.0 / D, 1e-6, ALU.mult, ALU.add, [bss[tt]], [bss[tt]])
                    P.act(ssq[:, 4 + tt:5 + tt], ssq[:, 4 + tt:5 + tt], AF.Sqrt, [bss[tt]], [bss[tt]])
                    P.op("dve", (lambda tt=tt: (lambda e: e.reciprocal(out=ssq[:, 4 + tt:5 + tt], in_=ssq[:, 4 + tt:5 + tt])))(), [bss[tt]], [bss[tt]])
                    P.act(hb2[tt][:], xt[:, tt, :], AF.Copy, [bx, bss[tt]], [bhb2[tt]], scale=ssq[:, 4 + tt:5 + tt])

            def trans3(blk):
                hT, bhT = hT2[blk % 2], bhT2[blk % 2]
                for tt in range(2):
                    pT, bpT = psT2[tt], bpsT2[tt]
                    for kc in range(8):
                        P.tr(pT[:, kc * 128:(kc + 1) * 128], hb2[tt][:, kc * 128:(kc + 1) * 128], ident[:], [bhb2[tt], bconst], [bpT])
                    P.tt(hT[:, :, tt * 128:(tt + 1) * 128], pT.rearrange("p (c t) -> p c t", c=8),
                         g2T[:].unsqueeze(2).to_broadcast([128, 8, 128]), ALU.mult, [bpT, bc3], [bhT])

            def up3(blk):
                hT, bhT = hT2[blk % 2], bhT2[blk % 2]
                for f in range(32):
                    pa, pb = bank(2 + f % 2)
                    for kc in range(8):
                        P.mm(pa[:, 0:256], wu[:, kc, f * 128:(f + 1) * 128], hT[:, kc, :], kc == 0, kc == 7, [bwu, bhT], [pb])
                    P.act(rl[:], pa[:, 0:256], AF.Relu, [pb], [brl])
                    P.tt(hidT[:, f, :], rl[:], rl[:], ALU.mult, [brl], [bhid], eng="dve")

            def down3(blk):
                res = []
                for tt in range(2):
                    po = ps[2 + tt % 2][:, :]
                    pob = psb[2 + tt % 2]
                    for hf in range(2):
                        for f in range(32):
                            P.mm(po[:, hf * 512:(hf + 1) * 512], hidT[:, f, tt * 128:(tt + 1) * 128], wd[:, f, hf * 512:(hf + 1) * 512],
                                 f == 0, f == 31, [bhid, bwu], [pob[hf]])

            def epi3(blk):
                xt = xt2[blk % 2]
                for tt in range(2):
                    tile = blk * 2 + tt
                    bx = bxt2[blk % 2][tt]
                    po = ps[2 + tt % 2][:, :]
                    pob = psb[2 + tt % 2]
                    P.tt(x3[:], po, xt[:, tt, :], ALU.add, [pob[0], pob[1], bx], [bx3])
                    P.act(junk2[:], x3[:], AF.Square, [bx3], [bss2, bjunk], accum=ss2[:, 0:1])
                    P.ts(ss2[:, 1:2], ss2[:, 0:1], 1.0 / D, 1e-6, ALU.mult, ALU.add, [bss2], [bss2])
                    P.act(ss2[:, 1:2], ss2[:, 1:2], AF.Sqrt, [bss2], [bss2])
                    P.op("dve", lambda e: e.reciprocal(out=ss2[:, 1:2], in_=ss2[:, 1:2]), [bss2], [bss2])
                    P.stt(ot1[:], x3[:], ss2[:, 1:2], gfb[:], ALU.mult, ALU.mult, [bx3, bss2, bc3], [bot1])
                    P.dma(nq(), out[tile * 128:(tile + 1) * 128, :], ot1[:], reads=[bot1])

            stats3(0)
            trans3(0)
            for blk in range(NB3):
                up3(blk)
                if blk + 1 < NB3:
                    stats3(blk + 1)
                down3(blk)
                if blk + 1 < NB3:
                    trans3(blk + 1)
                epi3(blk)

        phases = dbg.get("phases", "1a,1b,2,3") if dbg else "1a,1b,2,3"
        if "1a" in phases:
            phase1a()
            P.barrier()
        if "1b" in phases:
            phase1b()
            P.barrier()
        if "2" in phases:
            phase2()
            P.barrier()
        if "3" in phases:
            phase3()
        if dbg and "dump" in dbg:
            P.barrier()
            for nm in dbg["dump"]:
                src = {"QR": QR, "QP": QP, "KS": KS, "KW": KW, "KC": KC, "VC": VC, "VSW": VSW, "GM": GM, "UT": UT,
                       "NG": NG, "YB": YB, "X2": X2}[nm]
                dd = nc.dram_tensor("dump_" + nm, list(src.shape), src.dtype, kind="ExternalOutput").ap()
                P.dma("sp", dd, src)
        nw = P.finalize()
        print(f"[build] ops={len(P.ops)} waits={nw} sems={P.nsem}")
    return nc, consts


_CACHE = {}


def kernel(**inputs):
    if "nc" not in _CACHE:
        _CACHE["nc"] = build()
        _CACHE["consts"] = [host_consts(0), host_consts(1)]
    nc, _ = _CACHE["nc"]
    x = np.asarray(inputs["x"], dtype=np.float32)
    B = x.shape[0]
    shared = {}
    for k, v in inputs.items():
        if k == "x":
            continue
        a = np.ascontiguousarray(np.asarray(v, dtype=np.float32))
        if k != "norm_final_g":
            a = a[0]
        shared[k] = np.ascontiguousarray(a)
    in_maps = []
    for c in range(8):
        bidx, par = c // 2, c % 2
        m = dict(shared)
        for k, v in _CACHE["consts"][par].items():
            m["c_" + k] = v
        xb = x[bidx % B]
        m["x"] = np.ascontiguousarray(xb)
        m["xown"] = np.ascontiguousarray(xb.reshape(NTO, 2, 128, D)[:, par].reshape(TO, D))
        in_maps.append(m)
    res = run_bass_kernel_spmd(nc, in_maps, core_ids=list(range(8)))
    out = np.empty((B, T, D), np.float32)
    ov = out.reshape(B, NTO, 2, 128, D)
    for c in range(8):
        bidx, par = c // 2, c % 2
        if bidx < B:
            ov[bidx, :, par] = np.asarray(res.results[c]["out"], dtype=np.float32).reshape(NTO, 128, D)
    return out
```

```python
import math
import numpy as np
import ml_dtypes
import concourse.bass as bass
import concourse.mybir as mybir
from concourse.bass_utils import run_bass_kernel_spmd
from contextlib import ExitStack

F32 = mybir.dt.float32
BF16 = mybir.dt.bfloat16
ALU = mybir.AluOpType
AF = mybir.ActivationFunctionType

T = 8192
NT = T // 128
TO = T // 2
NTO = TO // 128
D = 1024
INW = 3864
QOFF, KVOFF, NGOFF, UOFF, MGOFF = 0, 512, 1280, 1304, 1816
NEG = -30000.0
NO_SAME_ENGINE_SYNC = False
DBG = {}


class Buf:
    __slots__ = ("name", "last_w", "readers")

    def __init__(self, name=""):
        self.name = name
        self.last_w = None
        self.readers = []


class Op:
    __slots__ = ("eng", "fn", "idx", "deps", "sig", "is_dma", "sem", "val", "prewait")


class Prog:
    ENGS = ("pe", "act", "dve", "pool", "sp")
    NDSEM = 12

    def __init__(self, nc, stack):
        self.nc = nc
        self.stack = stack
        self.ops = []
        self.engobj = {"pe": nc.tensor, "act": nc.scalar, "dve": nc.vector,
                       "pool": nc.gpsimd, "sp": nc.sync}
        self.nsem = 0
        self.last_on = {}
        self.dmas_since_barrier = []
        self.off = 16640
        self.ntens = 0

    def sb(self, shape, dt, name=None, at=None):
        esz = 4 if dt == F32 else 2
        n = 1
        for s in shape[1:]:
            n *= s
        nbytes = (n * esz + 63) // 64 * 64
        self.ntens += 1
        if at is not None:
            return self.nc.alloc_sbuf_tensor_at(f"{name or 't'}_{self.ntens}", list(shape), dt, offset=at)
        t = self.nc.alloc_sbuf_tensor_at(f"{name or 't'}_{self.ntens}", list(shape), dt, offset=self.off)
        self.off += nbytes
        assert self.off <= 226000, f"SBUF overflow {self.off}"
        return t

    def newsem(self, name):
        self.nsem += 1
        return self.stack.enter_context(self.nc.semaphore(f"{name}_{self.nsem}"))

    def _add(self, eng, fn, reads, writes, is_dma):
        o = Op()
        o.eng, o.fn, o.idx, o.deps, o.sig, o.is_dma = eng, fn, len(self.ops), set(), False, is_dma
        o.sem, o.val, o.prewait = None, 0, None
        for b in reads:
            if b.last_w is not None:
                o.deps.add(b.last_w)
        for b in writes:
            if b.last_w is not None:
                o.deps.add(b.last_w)
            o.deps.update(b.readers)
        for b in reads:
            b.readers.append(o.idx)
        for b in writes:
            b.last_w = o.idx
            b.readers = []
        o.deps.discard(o.idx)
        self.ops.append(o)
        if is_dma:
            self.dmas_since_barrier.append(o.idx)
        elif fn is not None:
            self.last_on[eng] = o.idx
        return o

    def op(self, eng, fn, reads=(), writes=()):
        return self._add(eng, fn, reads, writes, False)

    def dma(self, q, out, in_, reads=(), writes=(), slow=False):
        if slow:
            return self._add(q, lambda e: e.dma_start(out=out, in_=in_, allow_slow_non_contiguous=True), reads, writes, True)
        return self._add(q, lambda e: e.dma_start(out=out, in_=in_), reads, writes, True)

    def barrier(self):
        deps = set(self.last_on.values()) | set(self.dmas_since_barrier)
        self.dmas_since_barrier = []
        for e in self.ENGS:
            o = self._add(e, None, (), (), False)
            o.deps = set(d for d in deps)

    def mm(self, out, lhsT, rhs, start, stop, r, w):
        self.op("pe", lambda e: e.matmul(out, lhsT, rhs, start=start, stop=stop, skip_group_check=True), r, w)

    def tr(self, out, in_, ident, r, w):
        self.op("pe", lambda e: e.transpose(out, in_, ident), r, w)

    def act(self, out, in_, func, r, w, bias=None, scale=None, accum=None):
        kw = {}
        if bias is not None:
            kw["bias"] = bias
        if scale is not None:
            kw["scale"] = scale
        if accum is not None:
            kw["accum_out"] = accum
        self.op("act", lambda e: e.activation(out=out, in_=in_, func=func, **kw), r, w)

    def tt(self, out, in0, in1, op, r, w, eng="dve"):
        self.op(eng, lambda e: e.tensor_tensor(out=out, in0=in0, in1=in1, op=op), r, w)

    def ts(self, out, in0, s1, s2, op0, op1, r, w, eng="dve"):
        if op1 is None:
            self.op(eng, lambda e: e.tensor_scalar(out=out, in0=in0, scalar1=s1, scalar2=None, op0=op0), r, w)
        else:
            self.op(eng, lambda e: e.tensor_scalar(out=out, in0=in0, scalar1=s1, scalar2=s2, op0=op0, op1=op1), r, w)

    def stt(self, out, in0, scalar, in1, op0, op1, r, w, eng="dve"):
        self.op(eng, lambda e: e.scalar_tensor_tensor(out=out, in0=in0, scalar=scalar, in1=in1, op0=op0, op1=op1), r, w)

    def cp(self, out, in_, r, w, eng="dve"):
        self.op(eng, lambda e: e.tensor_copy(out=out, in_=in_), r, w)

    def memset(self, ap, v, w, eng="dve"):
        self.op(eng, lambda e: e.memset(ap, v), (), w)

    def finalize(self):
        ops = self.ops
        for o in ops:
            if o.eng == "pe" and o.fn is not None:
                o.deps = set(d for d in o.deps if not (ops[d].eng == "pe" and not ops[d].is_dma))
            if NO_SAME_ENGINE_SYNC and o.eng in ("dve", "act") and o.fn is not None and not o.is_dma:
                o.deps = set(d for d in o.deps if not (ops[d].eng == o.eng and not ops[d].is_dma))
            for d in o.deps:
                ops[d].sig = True
            if o.is_dma:
                o.sig = True
        ROLL = 12000
        cur, cnt = {}, {}
        dsem = {}
        dcount = {}
        for o in ops:
            if not o.sig:
                continue
            if o.is_dma:
                q = o.eng
                if q not in dsem:
                    dsem[q] = [self.newsem("d" + q) for _ in range(self.NDSEM)]
                    dcount[q] = 0
                k = dcount[q]
                dcount[q] += 1
                o.sem = dsem[q][k % self.NDSEM]
                o.val = 16 * (k // self.NDSEM + 1)
                if k >= self.NDSEM:
                    o.prewait = (o.sem, o.val - 16)
            else:
                e = o.eng
                if e not in cur or cnt[e] >= ROLL:
                    cur[e] = self.newsem(e)
                    cnt[e] = 0
                cnt[e] += 1
                o.sem, o.val = cur[e], cnt[e]
        waited = {e: {} for e in self.ENGS}
        nwait = 0
        for o in ops:
            e = self.engobj[o.eng]
            need = {}
            for d in o.deps:
                p = ops[d]
                k = id(p.sem)
                if k not in need or need[k][1] < p.val:
                    need[k] = (p.sem, p.val)
            if o.prewait is not None:
                k = id(o.prewait[0])
                if k not in need or need[k][1] < o.prewait[1]:
                    need[k] = o.prewait
            w = waited[o.eng]
            for k, (s, v) in need.items():
                if w.get(k, 0) >= v:
                    continue
                e.wait_ge(s, v)
                w[k] = v
                nwait += 1
            if o.fn is None:
                continue
            ins = o.fn(e)
            if o.sig:
                ins.then_inc(o.sem, 16 if o.is_dma else 1)
        fe = self.engobj["sp"]
        for q, sems in dsem.items():
            n = dcount[q]
            for j, s in enumerate(sems):
                c = (n - j + self.NDSEM - 1) // self.NDSEM if n > j else 0
                if c > 0:
                    fe.wait_ge(s, 16 * c)
        return nwait


def host_consts(par=0):
    c = {}
    bf = ml_dtypes.bfloat16
    c["ident"] = np.eye(128, dtype=np.float32).astype(bf)
    c["identf"] = np.eye(128, dtype=np.float32)
    c["i4"] = np.tile(np.eye(128, dtype=np.float32), (1, 4)).astype(bf)
    half = 8
    inv = 500000.0 ** (-(np.arange(half, dtype=np.float32) * 2.0) / 16.0)
    ang = np.arange(T, dtype=np.float32)[None, :] * inv[:, None].astype(np.float32)
    cosv, sinv = np.cos(ang).astype(np.float32), np.sin(ang).astype(np.float32)
    rc = np.ones((128, T), np.float32)
    rs = np.zeros((128, T), np.float32)
    for g in range(2):
        rc[64 * g:64 * g + 8] = cosv
        rc[64 * g + 8:64 * g + 16] = cosv
        rs[64 * g:64 * g + 8] = sinv
        rs[64 * g + 8:64 * g + 16] = sinv
    c["ropec"], c["ropes"] = rc, rs
    own_pos = (np.arange(TO) // 128 * 2 + par) * 128 + np.arange(TO) % 128
    c["ropeco"], c["ropeso"] = np.ascontiguousarray(rc[:, own_pos]), np.ascontiguousarray(rs[:, own_pos])
    pm = np.zeros((128, 128), np.float32)
    for g in range(2):
        for d in range(8):
            pm[64 * g + d + 8, 64 * g + d] = -1.0
            pm[64 * g + d, 64 * g + d + 8] = 1.0
    c["pm"] = pm.astype(bf)
    r = np.arange(128)[:, None]
    kl = np.arange(128)[None, :]
    def vis_mask(p, window):
        dk = kl - r
        v = dk <= 128 * (par - p)
        if window:
            v = v & (dk > 128 * (par - p) - 512)
        return np.where(v, 0.0, NEG).astype(np.float32)
    c["wmask"] = np.stack([vis_mask(p, True) for p in (1, 0, -3, -4)], axis=1).astype(bf)
    c["smask"] = np.stack([vis_mask(p, False) for p in (0, 1)], axis=1).astype(bf)
    m = np.arange(0, 288)[None, :] - 136 - 8 * par
    fl = np.floor((np.arange(128)[:, None] - 31) / 16.0)
    c["cbase"] = np.where(m <= fl, 0.0, NEG).astype(np.float32).astype(bf)
    c["sel01"] = np.tile(np.array([[1.0 - par, float(par)]], np.float32), (128, 1))
    n_ = np.arange(512)[:, None] * 16
    j_ = np.arange(128)[None, :] * 64
    ov = ((n_ < j_ + 64) & (n_ + 32 > j_)).astype(np.float32)
    ov[511] = 0
    c["ov"] = ov.reshape(4, 128, 128).transpose(1, 0, 2).copy().astype(bf)
    mm = np.arange(-128, 128)[None, :]
    hi = (np.arange(128)[:, None] >= 64).astype(np.int64) + 2 * par
    wf = np.zeros((128, 256), np.float32)
    wf[(mm == hi) | (mm == hi - 1)] = 1000.0
    wf[np.broadcast_to(mm > hi, wf.shape)] = -1e30
    c["wf"] = wf
    we = np.zeros((128, T), np.float32)
    we[np.arange(T) // 64, np.arange(T)] = 1.0
    c["we"] = we.astype(bf)
    sg = np.zeros((24, 24, 64), np.float32)
    for k in range(24):
        sg[k, k, :] = 1.0
    c["selg"] = sg.astype(bf)
    mb = np.zeros((128, 4, 8), np.float32)
    mc = np.zeros((128, 4, 2), np.float32)
    for q in range(4):
        for gp in range(2):
            mb[64 * gp:64 * gp + 64, q, 2 * q + gp] = 1.0
            mc[32 * q + 16 * gp:32 * q + 16 * gp + 16, q, gp] = 1.0
    c["mb"], c["mc"] = mb, mc
    rowsel = np.zeros((128, 4), np.float32)
    rowsel[0:64, 0] = 1.0
    rowsel[64:128, 1] = 1.0
    rowsel[:, 2:4] = -rowsel[:, 0:2]
    c["rowsel"] = rowsel
    tq = np.arange(128) // 16
    c["cmask8"] = (tq[None, :] >= tq[:, None]).astype(np.float32)
    selc = np.zeros((128, 64), np.float32)
    for j in range(64):
        selc[(2 * (j // 16) + par) * 16 + j % 16, j] = 1.0
    c["selc"] = selc.astype(bf)
    return c


CONST_SPECS = None


def build(dbg=None):
    nc = bass.Bass("TRN2", target_bir_lowering=False)
    consts = host_consts()
    dram_in = {}

    def din(name, shape, dt):
        dram_in[name] = nc.dram_tensor(name, list(shape), dt, kind="ExternalInput").ap()
        return dram_in[name]

    x = din("x", [T, D], F32)
    xown = din("xown", [TO, D], F32)
    w_in = din("w_in", [D, INW], F32)
    g_mix = din("norm_mix_g", [D], F32)
    g_mlp = din("norm_mlp_g", [D], F32)
    g_fin = din("norm_final_g", [D], F32)
    cmp_pe = din("cmp_pe", [32, 64], F32)
    cw1 = [din("cmp_k_w1", [2048, 256], F32), din("cmp_v_w1", [2048, 256], F32)]
    cw2 = [din("cmp_k_w2", [256, 64], F32), din("cmp_v_w2", [256, 64], F32)]
    lam_re = din("ssm_lam_re", [32, 64], F32)
    lam_im = din("ssm_lam_im", [32, 64], F32)
    log_step = din("ssm_log_step", [32], F32)
    b_re = din("ssm_b_re", [32, 64, 16], F32)
    b_im = din("ssm_b_im", [32, 64, 16], F32)
    c_re = din("ssm_c_re", [32, 16, 64], F32)
    c_im = din("ssm_c_im", [32, 16, 64], F32)
    ssm_d = din("ssm_d", [512], F32)
    w_attn = din("w_attn_branch", [512, D], F32)
    w_val = din("w_ssm_val", [512, D], F32)
    w_gate = din("w_ssm_gate", [512, D], F32)
    w_out = din("w_out", [D, D], F32)
    w_up = din("w_up", [D, 4096], F32)
    w_down = din("w_down", [4096, D], F32)
    cd = {}
    for k, v in consts.items():
        cd[k] = din("c_" + k, v.shape, BF16 if v.dtype == ml_dtypes.bfloat16 else F32)
    out = nc.dram_tensor("out", [TO, D], F32, kind="ExternalOutput").ap()

    def scratch(name, shape, dt):
        return nc.dram_tensor("s_" + name, list(shape), dt, kind="Internal").ap()

    QR = scratch("qr", [128, 4, TO], BF16)
    QP = scratch("qp", [128, 4, TO], BF16)
    KS = scratch("ks", [128, T], BF16)
    KW = scratch("kw", [128, T], BF16)
    KC = scratch("kc", [128, T], BF16)
    VC = scratch("vc", [128, T], BF16)
    VSW = scratch("vsw", [NT, 128, 2, 2, 128], BF16)
    GM = scratch("gm", [128, 16, TO], BF16)
    UT = scratch("ut", [128, 4, T], BF16)
    UTM = scratch("utm", [T, 512], BF16)
    NG = scratch("ng", [24, TO], BF16)
    YB = scratch("yb", [128, 8, TO], BF16)
    X2 = scratch("x2", [TO, D], F32)

    with ExitStack() as st:
        P = Prog(nc, st)
        ps = [st.enter_context(nc.psum_tensor(f"ps{i}", [128, 1024], F32)) for i in range(4)]
        psb = [[Buf(f"ps{i}a"), Buf(f"ps{i}b")] for i in range(4)]

        def bank(i):
            return ps[i // 2][:, (i % 2) * 512:(i % 2) * 512 + 512], psb[i // 2][i % 2]

        dq = ["sp", "sp"]
        dqi = [0]
        bq_i = [0]

        def bulkq():
            bq_i[0] += 1
            return ("sp", "pool")[bq_i[0] % 2]

        def nq():
            dqi[0] += 1
            return dq[dqi[0] % 2]

        ident = P.sb([128, 128], BF16, "ident")
        identf = P.sb([128, 128], F32, "identf")
        bconst = Buf("const")
        P.dma("sp", ident[:], cd["ident"], writes=[bconst])
        P.dma("sp", identf[:], cd["identf"], writes=[bconst])
        persist_off = P.off

        def load_w_bf16(dst, src_ap, shape, bdst, stage, bstage, conv_eng="dve"):
            n = src_ap.shape[-1]
            if len(src_ap.shape) == 2 and n > 1024:
                nch = (n + 965) // 966
                for c in range(nch):
                    lo, hi = c * 966, min(n, (c + 1) * 966)
                    P.dma(nq(), stage[:, lo:hi], src_ap[:, lo:hi], writes=[bstage])
            else:
                P.dma(nq(), stage, src_ap, writes=[bstage])
            P.cp(dst, stage, [bstage], [bdst], eng=conv_eng)

        def phase1a():
            P.off = persist_off
            win = P.sb([128, 8, INW], BF16, "win")
            bwin = Buf("win")
            stg = [P.sb([128, INW], F32, "stg") for _ in range(2)]
            bstg = [Buf("stg0"), Buf("stg1")]
            for kc in range(8):
                load_w_bf16(win[:, kc, :], w_in[kc * 128:(kc + 1) * 128, :], None, bwin, stg[kc % 2][:], bstg[kc % 2],
                            conv_eng="dve")
            gT = P.sb([128, 8], F32, "gT")
            P.dma("sp", gT[:], g_mix.rearrange("(c p) -> p c", p=128), writes=[bconst], slow=True)
            pm = P.sb([128, 128], BF16, "pm")
            P.dma("sp", pm[:], cd["pm"], writes=[bconst])
            xt = P.sb([128, 4, D], F32, "xt")
            bxt = [Buf() for _ in range(4)]
            junk = P.sb([128, D], BF16, "junk")
            bjunk = Buf()
            ssq = P.sb([128, 4], F32, "ssq")
            rstd = P.sb([128, 4], F32, "rstd")
            bss = [Buf() for _ in range(4)]
            hb4 = [P.sb([128, D], BF16, "hb") for _ in range(4)]
            bhb4 = [Buf() for _ in range(4)]
            hT2 = [P.sb([128, 8, 512], BF16, "hT") for _ in range(2)]
            bhT2 = [Buf(), Buf()]
            cur = {"hT": hT2[0], "bhT": bhT2[0]}
            ropc = P.sb([128, 512], F32, "ropc")
            rops = P.sb([128, 512], F32, "rops")
            brope = Buf()
            qr_sb = P.sb([128, 4, 512], BF16, "qr")
            qp_sb = P.sb([128, 4, 512], BF16, "qp")
            bqr, bqp = Buf(), Buf()
            kraw = P.sb([128, 512], BF16, "kraw")
            bkraw = Buf()
            kout = P.sb([128, 4, 512], BF16, "kout")
            bko = [Buf() for _ in range(4)]
            t1 = P.sb([128, 512], F32, "t1")
            t2 = P.sb([128, 512], F32, "t2")
            bt1, bt2 = Buf(), Buf()
            gm_sb = P.sb([128, 16, 512], BF16, "gm")
            bgm = Buf()
            u_sb = P.sb([128, 4, 512], BF16, "u")
            bu = Buf()
            ng_sb = P.sb([24, 512], BF16, "ng")
            bng = Buf()
            v_sb = P.sb([128, 2, 2, 128], BF16, "v")
            bv = Buf()
            P.memset(v_sb[:], 1.0, [bv])
            psT = ps[0][:, 0:512].bitcast(BF16)
            bpsT = psb[0][0]
            acc_banks = [2, 3, 4, 5]
            acc_i = [0]

            def next_acc():
                acc_i[0] += 1
                return bank(acc_banks[acc_i[0] % 4])

            def proj(cols, M):
                pa, pb = next_acc()
                for kc in range(8):
                    P.mm(pa[0:M, :], win[:, kc, cols:cols + M], cur["hT"][:, kc, :], kc == 0, kc == 7, [bwin, cur["bhT"]], [pb])
                return pa, pb

            def rope(src_sb, bsrc, dst_sb, bdst):
                pp, pbp = bank(6)
                P.mm(pp, pm[:], src_sb, True, True, [bconst, bsrc], [pbp])
                P.tt(t1[:], src_sb, ropc[:], ALU.mult, [bsrc, brope], [bt1])
                P.tt(t2[:], pp, rops[:], ALU.mult, [pbp, brope], [bt2])
                P.tt(dst_sb, t1[:], t2[:], ALU.add, [bt1, bt2], [bdst])

            psT2 = [ps[0][:, 0:512].bitcast(BF16), ps[0][:, 512:1024].bitcast(BF16)]
            bpsT2 = [psb[0][0], psb[0][1]]

            def stats(xsrc, blk):
                for tt in range(4):
                    tile = blk * 4 + tt
                    P.dma(nq(), xt[:, tt, :], xsrc[tile * 128:(tile + 1) * 128, :], writes=[bxt[tt]])
                    P.act(junk[:], xt[:, tt, :], AF.Square, [bxt[tt]], [bjunk, bss[tt]], accum=ssq[:, tt:tt + 1])
                    P.ts(rstd[:, tt:tt + 1], ssq[:, tt:tt + 1], 1.0 / D, 1e-6, ALU.mult, ALU.add, [bss[tt]], [bss[tt]])
                    P.act(rstd[:, tt:tt + 1], rstd[:, tt:tt + 1], AF.Sqrt, [bss[tt]], [bss[tt]])
                    P.op("dve", (lambda tt=tt: (lambda e: e.reciprocal(out=rstd[:, tt:tt + 1], in_=rstd[:, tt:tt + 1])))(), [bss[tt]], [bss[tt]])
                    P.act(hb4[tt][:], xt[:, tt, :], AF.Copy, [bxt[tt], bss[tt]], [bhb4[tt]], scale=rstd[:, tt:tt + 1])

            def trans(par_):
                hT_, bhT_ = hT2[par_], bhT2[par_]
                for tt in range(4):
                    pT, bpT = psT2[tt % 2], bpsT2[tt % 2]
                    for kc in range(8):
                        P.tr(pT[:, kc * 128:(kc + 1) * 128], hb4[tt][:, kc * 128:(kc + 1) * 128], ident[:], [bhb4[tt], bconst], [bpT])
                    P.tt(hT_[:, :, tt * 128:(tt + 1) * 128], pT.rearrange("p (c t) -> p c t", c=8),
                         gT[:].unsqueeze(2).to_broadcast([128, 8, 128]), ALU.mult, [bpT, bconst], [bhT_])

            def projA(blk):
                hT = cur["hT"]
                bhT = cur["bhT"]
                tok0 = blk * 512
                P.dma("sp", ropc[:], cd["ropec"][:, tok0:tok0 + 512], writes=[brope])
                P.dma("sp", rops[:], cd["ropes"][:, tok0:tok0 + 512], writes=[brope])
                for j, (kvi, dst) in enumerate([(0, KC), (1, VC)]):
                    pa, pb = proj(KVOFF + 128 * kvi, 128)
                    P.act(kout[:, 2 + j, :], pa, AF.Copy, [pb], [bko[2 + j]])
                    P.dma(nq(), dst[:, tok0:tok0 + 512], kout[:, 2 + j, :], reads=[bko[2 + j]])
                for j, (kvi, dst) in enumerate([(2, KS), (4, KW)]):
                    pa, pb = proj(KVOFF + 128 * kvi, 128)
                    P.act(kraw[:], pa, AF.Copy, [pb], [bkraw])
                    rope(kraw[:], bkraw, kout[:, j, :], bko[j])
                    P.dma(nq(), dst[:, tok0:tok0 + 512], kout[:, j, :], reads=[bko[j]])
                for tt in range(4):
                    tile = blk * 4 + tt
                    pu, pub = bank(6)
                    for kc in range(8):
                        P.mm(pu, hT[:, kc, tt * 128:(tt + 1) * 128], win[:, kc, UOFF:UOFF + 512], kc == 0, kc == 7, [bhT, bwin], [pub])
                    P.act(u_sb[:, tt, :], pu, AF.Copy, [pub], [bu])
                    P.dma(nq(), UTM[tile * 128:(tile + 1) * 128, :], u_sb[:, tt, :], reads=[bu])
                for tt in range(4):
                    tile = blk * 4 + tt
                    pv, pvb = bank(7)
                    for sw, kvi in enumerate([3, 5]):
                        for kc in range(8):
                            P.mm(pv[:, sw * 128:(sw + 1) * 128], hT[:, kc, tt * 128:(tt + 1) * 128],
                                 win[:, kc, KVOFF + 128 * kvi:KVOFF + 128 * kvi + 128], kc == 0, kc == 7, [bhT, bwin], [pvb])
                    P.cp(v_sb[:, :, :, 0:64], pv[:, 0:256].rearrange("p (s g d) -> p s g d", s=2, g=2), [pvb], [bv])
                    P.dma(nq(), VSW[tile], v_sb[:], reads=[bv])

            def projB(blk):
                tok0 = blk * 512
                P.dma("sp", ropc[:], cd["ropeco"][:, tok0:tok0 + 512], writes=[brope])
                P.dma("sp", rops[:], cd["ropeso"][:, tok0:tok0 + 512], writes=[brope])
                for m in range(4):
                    pa, pb = proj(QOFF + 128 * m, 128)
                    g = m // 2
                    hl = 2 * (m % 2)
                    P.act(qr_sb[64 * g:64 * g + 64, hl, :], pa[0:64, :], AF.Copy, [pb], [bqr])
                    P.cp(qr_sb[64 * g:64 * g + 64, hl + 1, :], pa[64:128, :], [pb], [bqr])
                for hl in range(4):
                    rope(qr_sb[:, hl, :], bqr, qp_sb[:, hl, :], bqp)
                P.dma(nq(), QR[:, :, tok0:tok0 + 512], qr_sb[:], reads=[bqr])
                P.dma(nq(), QP[:, :, tok0:tok0 + 512], qp_sb[:], reads=[bqp])
                for m in range(16):
                    pa, pb = proj(MGOFF + 128 * m, 128)
                    P.act(gm_sb[:, m, :], pa, AF.Sigmoid, [pb], [bgm])
                P.dma(nq(), GM[:, :, tok0:tok0 + 512], gm_sb[:], reads=[bgm])
                pa, pb = proj(NGOFF, 24)
                P.act(ng_sb[:], pa[0:24, :], AF.Sigmoid, [pb], [bng])
                P.dma(nq(), NG[:, tok0:tok0 + 512], ng_sb[:], reads=[bng])

            blocks = [("A", b_) for b_ in range(NT // 4)] + [("B", b_) for b_ in range(NTO // 4)]
            srcs = {"A": x, "B": xown}
            stats(srcs[blocks[0][0]], blocks[0][1])
            trans(0)
            for bi, (kind, b_) in enumerate(blocks):
                cur["hT"], cur["bhT"] = hT2[bi % 2], bhT2[bi % 2]
                if bi + 1 < len(blocks):
                    stats(srcs[blocks[bi + 1][0]], blocks[bi + 1][1])
                (projA if kind == "A" else projB)(b_)
                if bi + 1 < len(blocks):
                    trans((bi + 1) % 2)

        def phase1b():
            P.off = persist_off
            TB = 256
            NBK = T // TB
            bp = Buf("ssmparams")
            lre = P.sb([128, 16], F32)
            lim = P.sb([128, 16], F32)
            stp = P.sb([128, 16], F32)
            P.dma("sp", lre[:], lam_re.rearrange("(p gp) n -> (gp n) p", gp=2), writes=[bp], slow=True)
            P.dma("sp", lim[:], lam_im.rearrange("(p gp) n -> (gp n) p", gp=2), writes=[bp], slow=True)
            ls2 = log_step.rearrange("(p gp) -> gp p", gp=2)
            for gp in range(2):
                P.dma("sp", stp[64 * gp:64 * gp + 64, :], ls2[gp:gp + 1, :].to_broadcast([64, 16]), writes=[bp], slow=True)
            bre = P.sb([128, 16, 16], F32)
            bim = P.sb([128, 16, 16], F32)
            P.dma("sp", bre[:], b_re.rearrange("(p gp) n c -> (gp n) p c", gp=2), writes=[bp])
            P.dma("sp", bim[:], b_im.rearrange("(p gp) n c -> (gp n) p c", gp=2), writes=[bp])
            cct = P.sb([128, 2, 4, 64], F32)
            P.dma("sp", cct[:, 0, :, :], c_re.rearrange("(j gl) c n -> (gl c) j n", j=4), writes=[bp])
            P.dma("sp", cct[:, 1, :, :], c_im.rearrange("(j gl) c n -> (gl c) j n", j=4), writes=[bp])
            dsk = P.sb([128, 4], F32)
            P.dma("sp", dsk[:], ssm_d.rearrange("(j p) -> p j", p=128), writes=[bp], slow=True)
            mb = P.sb([128, 4, 8], F32)
            mc = P.sb([128, 4, 2], F32)
            P.dma("sp", mb[:], cd["mb"], writes=[bp])
            P.dma("sp", mc[:], cd["mc"], writes=[bp])
            sc = [P.sb([128, 16], F32) for _ in range(12)]
            bs = Buf("ssmscratch")
            R, W = [bp, bs], [bs]
            step, r_, th, s8, s16, pc, psn, tmpa, tmpb, cre, cim, den = sc
            P.act(step[:], stp[:], AF.Exp, [bp], W)
            P.tt(tmpa[:], step[:], lre[:], ALU.mult, R, W)
            P.act(r_[:], tmpa[:], AF.Exp, R, W)
            P.tt(th[:], step[:], lim[:], ALU.mult, R, W)
            P.act(s8[:], th[:], AF.Sin, R, W, scale=1.0 / 8)
            P.act(s16[:], th[:], AF.Sin, R, W, scale=1.0 / 16)
            P.tt(tmpa[:], s16[:], s16[:], ALU.mult, R, W)
            P.ts(pc[:], tmpa[:], -2.0, 1.0, ALU.mult, ALU.add, R, W)
            P.cp(psn[:], s8[:], R, W)
            pw = P.sb([128, 11, 2, 16], F32)
            bpw = Buf("pw")

            def csq(oc, os_, ic, is_):
                P.tt(tmpa[:], ic, ic, ALU.mult, R + [bpw], W)
                P.tt(tmpb[:], is_, is_, ALU.mult, R + [bpw], W)
                P.tt(cim[:], ic, is_, ALU.mult, R + [bpw], W)
                P.tt(oc, tmpa[:], tmpb[:], ALU.subtract, R + [bpw], W + [bpw])
                P.ts(os_, cim[:], 2.0, None, ALU.mult, None, R + [bpw], W + [bpw])
            csq(cre[:], den[:], pc[:], psn[:])
            csq(pc[:], psn[:], cre[:], den[:])
            csq(pw[:, 0, 0, :], pw[:, 0, 1, :], pc[:], psn[:])
            for j in range(1, 11):
                csq(pw[:, j, 0, :], pw[:, j, 1, :], pw[:, j - 1, 0, :], pw[:, j - 1, 1, :])
            lmr = P.sb([128, 16], F32)
            lmi = P.sb([128, 16], F32)
            P.tt(lmr[:], r_[:], pw[:, 0, 0, :], ALU.mult, R + [bpw], W)
            P.tt(lmi[:], r_[:], pw[:, 0, 1, :], ALU.mult, R + [bpw], W)
            P.ts(tmpa[:], lmr[:], -1.0, None, ALU.add, None, R, W)
            P.tt(den[:], lre[:], lre[:], ALU.mult, R, W)
            P.tt(tmpb[:], lim[:], lim[:], ALU.mult, R, W)
            P.tt(den[:], den[:], tmpb[:], ALU.add, R, W)
            P.op("dve", lambda e: e.reciprocal(out=den[:], in_=den[:]), R, W)
            P.tt(cre[:], tmpa[:], lre[:], ALU.mult, R, W)
            P.tt(tmpb[:], lmi[:], lim[:], ALU.mult, R, W)
            P.tt(cre[:], cre[:], tmpb[:], ALU.add, R, W)
            P.tt(cre[:], cre[:], den[:], ALU.mult, R, W)
            P.tt(cim[:], lmi[:], lre[:], ALU.mult, R, W)
            P.tt(tmpb[:], tmpa[:], lim[:], ALU.mult, R, W)
            P.tt(cim[:], cim[:], tmpb[:], ALU.subtract, R, W)
            P.tt(cim[:], cim[:], den[:], ALU.mult, R, W)
            bbr = P.sb([128, 16, 16], F32)
            bbi = P.sb([128, 16, 16], F32)
            tb3 = P.sb([128, 16, 16], F32)
            creb = cre[:].unsqueeze(2).to_broadcast([128, 16, 16])
            cimb = cim[:].unsqueeze(2).to_broadcast([128, 16, 16])
            P.tt(bbr[:], bre[:], creb, ALU.mult, R, W)
            P.tt(tb3[:], bim[:], cimb, ALU.mult, R, W)
            P.tt(bbr[:], bbr[:], tb3[:], ALU.subtract, R, W)
            P.tt(bbi[:], bim[:], creb, ALU.mult, R, W)
            P.tt(tb3[:], bre[:], cimb, ALU.mult, R, W)
            P.tt(bbi[:], bbi[:], tb3[:], ALU.add, R, W)
            TBc = 128
            rowsel = P.sb([128, 4], F32)
            cmask8 = P.sb([128, 128], F32)
            selc = P.sb([128, 64], BF16)
            dvec8 = P.sb([128, 32], F32)
            P.dma("sp", rowsel[:], cd["rowsel"], writes=[bp])
            P.dma("sp", cmask8[:], cd["cmask8"], writes=[bp])
            P.dma("sp", selc[:], cd["selc"], writes=[bp])
            dsrc = ssm_d.rearrange("(g c) -> c g", c=16)
            for t_ in range(8):
                P.dma("sp", dvec8[16 * t_:16 * t_ + 16, :], dsrc, writes=[bp], slow=True)
            Bpad8 = P.sb([128, 16, 2, 2, 128], BF16)
            CgZ = P.sb([128, 16, 2, 2, 128], BF16)
            Tg = P.sb([128, 32, 128], BF16)
            Ct = P.sb([128, 16, TBc], F32)
            Dt = P.sb([128, 16, TBc], F32)
            r8 = P.sb([128, 16], F32)
            bBC = Buf("BC")
            btab = Buf("tab")
            wv = P.sb([128, 4, D], BF16)
            wg = P.sb([128, 4, D], BF16)
            bwv = Buf("wv")
            mark_tmp = P.off
            lp = P.sb([128, 9, 2, 16], F32)
            blp = Buf("lp")
            RL, WL = [bp, bs, bpw, blp], [blp, bs]
            P.memset(lp[:, 0, 0, :], 1.0, [blp])
            P.memset(lp[:, 0, 1, :], 0.0, [blp])
            for m_ in range(8):
                a_r, a_i = lp[:, m_, 0, :], lp[:, m_, 1, :]
                P.tt(tmpa[:], a_r, lmr[:], ALU.mult, RL, WL)
                P.tt(tmpb[:], a_i, lmi[:], ALU.mult, RL, WL)
                P.tt(lp[:, m_ + 1, 0, :], tmpa[:], tmpb[:], ALU.subtract, RL, WL)
                P.tt(tmpa[:], a_r, lmi[:], ALU.mult, RL, WL)
                P.tt(tmpb[:], a_i, lmr[:], ALU.mult, RL, WL)
                P.tt(lp[:, m_ + 1, 1, :], tmpa[:], tmpb[:], ALU.add, RL, WL)
            inv8 = P.sb([128, 2, 16], F32)
            P.tt(tmpa[:], r_[:], r_[:], ALU.mult, RL, WL)
            P.tt(tmpa[:], tmpa[:], tmpa[:], ALU.mult, RL, WL)
            P.tt(r8[:], tmpa[:], tmpa[:], ALU.mult, RL, WL + [btab])
            P.tt(tmpb[:], r8[:], r8[:], ALU.mult, RL + [btab], WL)
            P.op("dve", lambda e: e.reciprocal(out=tmpb[:], in_=tmpb[:]), RL, WL)
            P.tt(inv8[:, 0, :], lp[:, 8, 0, :], tmpb[:], ALU.mult, RL, WL)
            P.tt(tmpa[:], lp[:, 8, 1, :], tmpb[:], ALU.mult, RL, WL)
            P.ts(inv8[:, 1, :], tmpa[:], -1.0, None, ALU.mult, None, RL, WL)
            ccTp = P.sb([128, 16, 2, 16], F32)
            bcc = Buf("ccTp")
            pz, pzb = bank(2)
            for ri in range(2):
                for j in range(4):
                    P.tr(pz[0:64, 0:128], cct[:, ri, j, :], identf[:], [bp, bconst], [pzb])
                    for gp in range(2):
                        P.cp(ccTp[64 * gp:64 * gp + 64, 4 * j:4 * j + 4, ri, :],
                             pz[0:64, 0:128].rearrange("n (q g c) -> n q g c", q=4, g=2)[:, :, gp, :], [pzb], [bcc])
            B8 = P.sb([128, 16, 2, 8, 16], F32)
            C8 = P.sb([128, 16, 2, 8, 16], F32)
            CQ = P.sb([128, 16, 2, 8, 16], F32)
            tb3b = P.sb([128, 16, 16], F32)
            b8 = Buf("B8")
            R8, W8 = [bp, bs, blp, bcc, b8], [b8]

            def cmul_bc(out_r, out_i, a_r, a_i, s_r, s_i, shape):
                sr = s_r.unsqueeze(2).to_broadcast(shape)
                si = s_i.unsqueeze(2).to_broadcast(shape)
                P.tt(out_r, a_r, sr, ALU.mult, R8, W8)
                P.tt(tb3[:], a_i, si, ALU.mult, R8, W8)
                P.tt(out_r, out_r, tb3[:], ALU.subtract, R8, W8)
                P.tt(out_i, a_i, sr, ALU.mult, R8, W8)
                P.tt(tb3[:], a_r, si, ALU.mult, R8, W8)
                P.tt(out_i, out_i, tb3[:], ALU.add, R8, W8)
            lpr = P.sb([128, 8, 2, 16], F32)
            for t_ in range(8):
                P.cp(lpr[:, t_, :, :], lp[:, 7 - t_, :, :], R8, W8)
            tb4 = P.sb([128, 16, 8, 16], F32)
            SH4 = [128, 16, 8, 16]

            def cmul4(out_r, out_i, a_r, a_i, s_r, s_i):
                P.tt(out_r, a_r, s_r, ALU.mult, R8, W8)
                P.tt(tb4[:], a_i, s_i, ALU.mult, R8, W8)
                P.tt(out_r, out_r, tb4[:], ALU.subtract, R8, W8)
                P.tt(out_i, a_i, s_r, ALU.mult, R8, W8)
                P.tt(tb4[:], a_r, s_i, ALU.mult, R8, W8)
                P.tt(out_i, out_i, tb4[:], ALU.add, R8, W8)

            def bc_t(a3):
                return a3.unsqueeze(2).to_broadcast(SH4)

            def bc_c(tab):
                return tab.rearrange("q t p -> q p t").unsqueeze(3).to_broadcast(SH4)
            cmul4(B8[:, :, 0], B8[:, :, 1], bc_t(bbr[:]), bc_t(bbi[:]), bc_c(lpr[:, :, 0, :]), bc_c(lpr[:, :, 1, :]))
            cmul4(C8[:, :, 0], C8[:, :, 1], bc_t(ccTp[:, :, 0, :]), bc_t(ccTp[:, :, 1, :]),
                  bc_c(lp[:, 1:9, 0, :]), bc_c(lp[:, 1:9, 1, :]))
            i8r = inv8[:, 0, :].unsqueeze(2).unsqueeze(3).to_broadcast(SH4)
            i8i = inv8[:, 1, :].unsqueeze(2).unsqueeze(3).to_broadcast(SH4)
            cmul4(CQ[:, :, 0], CQ[:, :, 1], C8[:, :, 0], C8[:, :, 1], i8r, i8i)
            P.ts(CQ[:, :, 1, :, :], CQ[:, :, 1, :, :], -1.0, None, ALU.mult, None, R8, W8)
            Zh = P.sb([128, 16, 2, 128], F32)
            bZh = Buf("Zh")
            for ri in range(2):
                for gp in range(2):
                    P.ts(Zh[:, :, gp, :], B8[:, :, ri].rearrange("q p t c -> q p (t c)"), rowsel[:, gp:gp + 1], None, ALU.mult, None,
                         R8, [bZh])
                for p4 in range(8):
                    pzq, pzqb = bank(2 + p4 % 2)
                    for k_ in range(4):
                        idx = p4 * 4 + k_
                        P.tr(pzq[:, k_ * 128:(k_ + 1) * 128], Zh[:, idx // 2, idx % 2, :], identf[:], [bZh, bconst], [pzqb])
                    P.act(Bpad8[:, 2 * p4:2 * p4 + 2, :, ri, :], pzq.rearrange("q (a b c) -> q a b c", a=2, b=2), AF.Copy, [pzqb], [bBC])
                for gp in range(2):
                    sc_ = rowsel[:, gp:gp + 1] if ri == 0 else rowsel[:, 2 + gp:3 + gp]
                    P.ts(CgZ[:, :, gp, ri, :], C8[:, :, ri].rearrange("q p t c -> q p (t c)"), sc_, None, ALU.mult, None, R8, [bBC])
            tmpT = P.sb([128, 128], F32)
            btT = Buf("tmpT")
            pt_, ptb = bank(3)
            for g_ in range(32):
                p, gp = g_ // 2, g_ % 2
                rows = slice(64 * gp, 64 * gp + 64)
                for ri in range(2):
                    P.mm(pt_[:, 0:128], B8[rows, p, ri].rearrange("p t c -> p (t c)"), CQ[rows, p, ri].rearrange("p t c -> p (t c)"),
                         ri == 0, ri == 1, [b8], [ptb])
                P.tt(tmpT[:], pt_[:, 0:128], cmask8[:], ALU.mult, [ptb, bp], [btT])
                P.stt(Tg[:, g_, :], identf[:], dvec8[:, g_:g_ + 1], tmpT[:], ALU.mult, ALU.add, [bconst, bp, btT], [bBC])
            tA = P.sb([128, 16, TBc // 2], F32)
            tBb = P.sb([128, 16, TBc // 2], F32)
            P.memset(Ct[:, :, 0:1], 1.0, [btab])
            P.memset(Dt[:, :, 0:1], 0.0, [btab])
            RT, WT = [bp, bs, bpw, btab], [btab]
            for j in range(7):
                m = 1 << j
                cj = pw[:, j + 3, 0, :].unsqueeze(2).to_broadcast([128, 16, m])
                sj = pw[:, j + 3, 1, :].unsqueeze(2).to_broadcast([128, 16, m])
                P.tt(tA[:, :, 0:m], Ct[:, :, 0:m], cj, ALU.mult, RT, WT)
                P.tt(tBb[:, :, 0:m], Dt[:, :, 0:m], sj, ALU.mult, RT, WT)
                P.tt(Ct[:, :, m:2 * m], tA[:, :, 0:m], tBb[:, :, 0:m], ALU.subtract, RT, WT)
                P.tt(tA[:, :, 0:m], Dt[:, :, 0:m], cj, ALU.mult, RT, WT)
                P.tt(tBb[:, :, 0:m], Ct[:, :, 0:m], sj, ALU.mult, RT, WT)
                P.tt(Dt[:, :, m:2 * m], tA[:, :, 0:m], tBb[:, :, 0:m], ALU.add, RT, WT)
            stg = [P.sb([128, D], F32) for _ in range(2)]
            bstg = [Buf(), Buf()]
            k = 0
            for dst, src in ((wv, w_val), (wg, w_gate)):
                for j in range(4):
                    load_w_bf16(dst[:, j, :], src[j * 128:(j + 1) * 128, :], None, bwv, stg[k % 2][:], bstg[k % 2], conv_eng="dve")
                    k += 1
            P.barrier()
            P.off = mark_tmp
            Uk = [P.sb([128, 8, 512], BF16) for _ in range(2)]
            bUk = [Buf(), Buf()]
            U8d = [P.sb([128, 32, 128], BF16) for _ in range(2)]
            bU8d = [Buf("U8a"), Buf("U8b")]
            Uk2 = P.sb([128, 32, 128], BF16)
            bUk2 = Buf("Uk2")
            gmb = [P.sb([128, 8, 512], BF16) for _ in range(2)]
            bgmb = [Buf(), Buf()]
            lanes = []
            blw = []
            blinit = []
            for L_ in range(2):
                lanes.append((P.sb([128, TBc], F32), P.sb([128, TBc], F32), P.sb([128, TBc], F32), P.sb([128, TBc], F32),
                              P.sb([128, 2, TBc], F32), P.sb([128, 2, TBc], F32), P.sb([128, 2], F32), P.sb([128, 2], F32)))
                blw.append([Buf() for _ in range(6)])
                blinit.append(Buf())
            Hb2 = [[P.sb([128, 2, TBc + 2], BF16) for _ in range(16)] for _ in range(2)]
            bH2 = [[Buf() for _ in range(16)] for _ in range(2)]
            for q_ in range(2):
                for p in range(16):
                    P.memset(Hb2[q_][p][:], 0.0, [bH2[q_][p]])
            glast = P.sb([128, 16, 2], F32)
            bgl = [Buf() for _ in range(16)]
            P.memset(glast[:], 0.0, bgl)
            init = P.sb([128, 2], F32)
            binit = Buf()
            ti = P.sb([128, 2], F32)
            Ytok = P.sb([128, 8, 512], BF16)
            bY = Buf("Ytok")
            gy = P.sb([128, 4, 512], BF16)
            bgy = Buf()
            sgt = P.sb([128, 512], F32)
            bsg = Buf()
            ybt = P.sb([128, 512], F32)
            bybt = Buf()
            yb_sb = [P.sb([128, 8, 512], BF16) for _ in range(2)]
            byb = [Buf(), Buf()]
            psT = ps[0][:, 0:512].bitcast(BF16)
            bpsT = psb[0][0]
            NBK = T // (8 * TBc)

            def prep(blk):
                ub, bub = Uk[blk % 2], bUk[blk % 2]
                U8_, bU8_ = U8d[blk % 2], bU8d[blk % 2]
                P.dma("sp", ub[:].rearrange("p t c -> p (t c)"),
                      UTM[blk * 1024:(blk + 1) * 1024, :].rearrange("(k t) c -> k (t c)", t=8), writes=[bub])
                for hh in range(2):
                    P.act(Uk2[:, 16 * hh:16 * hh + 16, :].rearrange("p g (t c) -> p g t c", t=8),
                          ub[:, :, 256 * hh:256 * hh + 256].rearrange("p t (g c) -> p g t c", g=16), AF.Copy, [bub], [bUk2])
                for gb in range(4):
                    for gq in range(8):
                        g_ = gb * 8 + gq
                        P.tr(psT[:, gq * 128:(gq + 1) * 128], Uk2[:, g_, :], ident[:], [bUk2, bconst], [bpsT])
                    P.act(U8_[:, gb * 8:(gb + 1) * 8, :], psT.rearrange("p (g k) -> p g k", g=8), AF.Copy, [bpsT], [bU8_])

            def pair_steps(blk, p, L):
                U8_, bU8_ = U8d[blk % 2], bU8d[blk % 2]
                Hn, bHn = Hb2[blk % 2][p], bH2[blk % 2][p]
                Ho, bHo = Hb2[(blk + 1) % 2][p], bH2[(blk + 1) % 2][p]
                w1_, w2_, w3_, w4_, gin_, G_, init_, ti_ = lanes[L]
                bwL, binitL = blw[L], blinit[L]
                st = []
                sp_, spb = bank(1 + (p % 2))
                S = sp_.rearrange("p (a t) -> p a t", a=2)
                C_, D_ = Ct[:, p, :], Dt[:, p, :]
                ec, es = pw[:, 10, 0, p:p + 1], pw[:, 10, 1, p:p + 1]
                rb = r8[:, p:p + 1].to_broadcast([128, TBc])

                def s0():
                    for ri in range(2):
                        for gp in range(2):
                            P.mm(S[:, ri, 0:TBc], Bpad8[:, p, gp, ri, :], U8_[:, 2 * p + gp, :], gp == 0 and ri == 0, gp == 1, [bBC, bU8_], [spb])
                st.append(s0)
                st.append(lambda: P.tt(w1_[:], S[:, 0, 0:TBc], C_, ALU.mult, [spb, btab], [bwL[0]]))
                st.append(lambda: P.tt(w2_[:], S[:, 1, 0:TBc], D_, ALU.mult, [spb, btab], [bwL[1]]))
                st.append(lambda: P.tt(gin_[:, 0, :], w1_[:], w2_[:], ALU.add, [bwL[0], bwL[1]], [bwL[4]]))
                st.append(lambda: P.tt(w3_[:], S[:, 1, 0:TBc], C_, ALU.mult, [spb, btab], [bwL[2]]))
                st.append(lambda: P.tt(w4_[:], S[:, 0, 0:TBc], D_, ALU.mult, [spb, btab], [bwL[3]]))
                st.append(lambda: P.tt(gin_[:, 1, :], w3_[:], w4_[:], ALU.subtract, [bwL[2], bwL[3]], [bwL[4]]))
                st.append(lambda: P.ts(ti_[:, 0:1], glast[:, p, 1:2], es, None, ALU.mult, None, [bgl[p], bpw], [binitL]))
                st.append(lambda: P.stt(init_[:, 0:1], glast[:, p, 0:1], ec, ti_[:, 0:1], ALU.mult, ALU.subtract, [bgl[p], bpw, binitL], [binitL]))
                st.append(lambda: P.ts(ti_[:, 1:2], glast[:, p, 0:1], es, None, ALU.mult, None, [bgl[p], bpw], [binitL]))
                st.append(lambda: P.stt(init_[:, 1:2], glast[:, p, 1:2], ec, ti_[:, 1:2], ALU.mult, ALU.add, [bgl[p], bpw, binitL], [binitL]))
                for a in range(2):
                    st.append((lambda a=a: (lambda: P.op("dve", lambda e: e.tensor_tensor_scan(
                        out=G_[:, a, :], data0=rb, data1=gin_[:, a, :], initial=init_[:, a:a + 1],
                        op0=ALU.mult, op1=ALU.add), [bwL[4], binitL, btab], [bwL[5]])))())
                st.append(lambda: P.cp(glast[:, p, :], G_[:, :, TBc - 1], [bwL[5]], [bgl[p]]))
                st.append(lambda: P.cp(Hn[:, :, 0:1], Ho[:, :, TBc:TBc + 1], [bHo], [bHn]))
                st.append(lambda: P.tt(w1_[:], G_[:, 0, :], C_, ALU.mult, [bwL[5], btab], [bwL[0]]))
                st.append(lambda: P.tt(w2_[:], G_[:, 1, :], D_, ALU.mult, [bwL[5], btab], [bwL[1]]))
                st.append(lambda: P.tt(Hn[:, 0, 1:TBc + 1], w1_[:], w2_[:], ALU.subtract, [bwL[0], bwL[1]], [bHn]))
                st.append(lambda: P.tt(w3_[:], G_[:, 0, :], D_, ALU.mult, [bwL[5], btab], [bwL[2]]))
                st.append(lambda: P.tt(w4_[:], G_[:, 1, :], C_, ALU.mult, [bwL[5], btab], [bwL[3]]))
                st.append(lambda: P.tt(Hn[:, 1, 1:TBc + 1], w3_[:], w4_[:], ALU.add, [bwL[2], bwL[3]], [bHn]))
                return st

            def out_slices(blk):
                U8_, bU8_ = U8d[blk % 2], bU8d[blk % 2]
                Hs, bHs = Hb2[blk % 2], bH2[blk % 2]
                ybs = yb_sb[blk % 2]

                def y_part(gb):
                    py, pyb = bank(3 + gb % 2)
                    for gq in range(4):
                        g_ = gb * 4 + gq
                        p, gp = g_ // 2, g_ % 2
                        o_ = py[:, gq * 128:(gq + 1) * 128]
                        P.mm(o_, U8_[:, g_, :], Tg[:, g_, :], gq == 0, False, [bU8_, bBC], [pyb])
                        P.mm(o_, Hs[p][:, 0, 0:TBc], CgZ[:, p, gp, 0, :], False, False, [bHs[p], bBC], [pyb])
                        P.mm(o_, Hs[p][:, 1, 0:TBc], CgZ[:, p, gp, 1, :], False, gq == 3, [bHs[p], bBC], [pyb])
                    P.act(Ytok[:, :, gb * 64:(gb + 1) * 64].rearrange("p t (g c) -> p t g c", g=4),
                          py.rearrange("p (g t c) -> p t g c", g=4, t=8), AF.Copy, [pyb], [bY])

                def sel_part(cb):
                    pyt, pytb = bank(5)
                    for t_ in range(8):
                        P.mm(pyt[:, t_ * 64:(t_ + 1) * 64], Ytok[:, t_, cb * 128:(cb + 1) * 128], selc[:], t_ == 0, t_ == 7, [bY, bp], [pytb])
                    P.act(gy[:, cb, :].rearrange("p (j t) -> p j t", t=8), pyt.rearrange("p (t j) -> p j t", t=8),
                          AF.Gelu_apprx_tanh, [pytb], [bgy])

                def glu_part(oc):
                    pv_, pvb = bank(6)
                    pg_, pgb = bank(7)
                    for j in range(4):
                        P.mm(pv_, wv[:, j, oc * 128:(oc + 1) * 128], gy[:, j, :], j == 0, j == 3, [bwv, bgy], [pvb])
                    for j in range(4):
                        P.mm(pg_, wg[:, j, oc * 128:(oc + 1) * 128], gy[:, j, :], j == 0, j == 3, [bwv, bgy], [pgb])
                    P.act(sgt[:], pg_, AF.Sigmoid, [pgb], [bsg])
                    P.tt(ybt[:], pv_, sgt[:], ALU.mult, [pvb, bsg], [bybt])
                    P.tt(ybs[:, oc, :], ybt[:], gmb[blk % 2][:, oc, :], ALU.mult, [bybt, bgmb[blk % 2]], [byb[blk % 2]])

                def fin():
                    P.dma(nq(), YB[:, :, blk * 512:(blk + 1) * 512], ybs[:], reads=[byb[blk % 2]])
                sl = [[lambda gb=gb: y_part(gb) for gb in (2 * k_, 2 * k_ + 1)] for k_ in range(4)]
                sl.append([lambda cb=cb: sel_part(cb) for cb in (0, 1)])
                sl.append([lambda cb=cb: sel_part(cb) for cb in (2, 3)])
                sl.append([lambda oc=oc: glu_part(oc) for oc in range(4)])
                sl.append([lambda oc=oc: glu_part(oc) for oc in range(4, 8)] + [fin])
                return sl

            prep(0)
            for blk in range(NBK + 1):
                if blk >= 1:
                    ob = blk - 1
                    P.dma("sp", gmb[ob % 2][:], GM[:, 8:16, ob * 512:(ob + 1) * 512], writes=[bgmb[ob % 2]])
                osl = out_slices(blk - 1) if blk >= 1 else [[] for _ in range(8)]
                for pp in range(8):
                    if blk < NBK:
                        sa, sb_ = pair_steps(blk, 2 * pp, 0), pair_steps(blk, 2 * pp + 1, 1)
                        for fa, fb in zip(sa, sb_):
                            fa()
                            fb()
                    for fn_ in osl[pp]:
                        fn_()
                    if pp == 3 and blk + 1 < NBK:
                        prep(blk + 1)

        def phase2():
            P.off = persist_off
            bc2 = Buf("c2")
            kcT = P.sb([128, 512], BF16, "kcT")
            vca = P.sb([128, 4, 2, 128], BF16, "vca")
            bkc = Buf("kcT")
            bvca = Buf("vca")
            P.memset(vca[:], 1.0, [bvca])
            P.memset(kcT[:], 0.0, [bkc])
            ksT = P.sb([128, T], BF16, "ksT")
            kwT = P.sb([128, T], BF16, "kwT")
            vsw = P.sb([128, NT, 2, 2, 128], BF16, "vsw")
            we = P.sb([128, T], BF16, "we")
            bK, bV = Buf("K"), Buf("V")
            mark = P.off
            w1b = P.sb([128, 32, 256], BF16, "w1b")
            w2b = P.sb([128, 2, 64], BF16, "w2b")
            peT = P.sb([64, 32], F32, "peT")
            peTb = P.sb([64, 32], BF16, "peTb")
            P.dma("sp", peT[:], cmp_pe.rearrange("j d -> d j"), writes=[bc2], slow=True)
            P.cp(peTb[:], peT[:], [bc2], [bc2])
            raw = P.sb([128, T + 32], BF16, "raw")
            braw = Buf("raw")
            stg = [P.sb([128, 8, 256], F32, "stgc") for _ in range(2)]
            bstg = [Buf(), Buf()]
            stg2 = P.sb([128, 2, 64], F32, "stg2")
            bw1 = Buf("w1")
            hid = P.sb([128, 2, 512], BF16, "hid")
            bhid = Buf("hid")
            hbias = P.sb([128, 2], F32, "hbias")
            bhb_ = Buf("hbias")
            P.memset(raw[:, T:T + 32], 0.0, [braw])

            def resident_loads():
                P.dma("sp", ksT[:], KS, writes=[bK])
                P.dma("sp", kwT[:], KW, writes=[bK])
                for c4 in range(8):
                    P.dma(nq(), vsw[:, c4 * 8:(c4 + 1) * 8], VSW[c4 * 8:(c4 + 1) * 8].rearrange("t p s g d -> p t s g d"), writes=[bV])
                P.dma("sp", we[:], cd["we"], writes=[bc2])
            for kv in range(2):
                for jq in range(4):
                    for half in range(2):
                        P.dma(nq(), stg[jq % 2][64 * half:64 * half + 64, :, :],
                              cw1[kv][jq * 512:(jq + 1) * 512, :].rearrange("(j d) h -> d j h", d=64), writes=[bstg[jq % 2]])
                    P.cp(w1b[:, jq * 8:(jq + 1) * 8, :], stg[jq % 2][:], [bstg[jq % 2]], [bw1], eng="dve")
                P.dma("sp", stg2[:], cw2[kv].rearrange("(a p) d -> p a d", p=128), writes=[bc2])
                P.cp(w2b[:], stg2[:], [bc2], [bw1])
                P.dma("sp", raw[:, 0:T // 2], (KC if kv == 0 else VC)[:, 0:T // 2], writes=[braw])
                P.dma("sp", raw[:, T // 2:T], (KC if kv == 0 else VC)[:, T // 2:T], writes=[braw])
                if kv == 0:
                    resident_loads()
                pbi, pbib = bank(7)
                for hh in range(2):
                    for j in range(32):
                        P.mm(pbi[:, hh:hh + 1], w1b[0:64, j, hh * 128:(hh + 1) * 128], peTb[:, j:j + 1], j == 0, j == 31, [bw1, bc2], [pbib])
                P.cp(hbias[:], pbi[:, 0:2], [pbib], [bhb_])
                for g in range(2):
                    rows = slice(64 * g, 64 * g + 64)
                    for hh in range(2):
                        ph, phb = bank(2 + hh)
                        for j in range(32):
                            rhs = raw[rows, j:j + 16 * 512].rearrange("p (n s) -> p n s", s=16)[:, :, 0]
                            P.mm(ph, w1b[rows, j, hh * 128:(hh + 1) * 128], rhs, j == 0, j == 31, [bw1, braw], [phb])
                        P.act(hid[:, hh, :], ph, AF.Gelu_apprx_tanh, [phb, bhb_], [bhid], bias=hbias[:, hh:hh + 1])
                    if kv == 0:
                        po, pob = bank(4)
                        for hh in range(2):
                            P.mm(po[0:64, :], w2b[:, hh, :], hid[:, hh, :], hh == 0, hh == 1, [bw1, bhid], [pob])
                        P.cp(kcT[rows, 0:511], po[0:64, 0:511], [pob], [bkc])
                    else:
                        for nt in range(4):
                            po, pob = bank(4 + nt % 2)
                            for hh in range(2):
                                P.mm(po[:, 0:64], hid[:, hh, nt * 128:(nt + 1) * 128], w2b[:, hh, :], hh == 0, hh == 1, [bhid, bw1], [pob])
                            P.cp(vca[:, nt, g, 0:64], po[:, 0:64], [pob], [bvca])
            P.barrier()
            P.off = mark
            i4 = P.sb([128, 512], BF16, "i4")
            wmask = P.sb([128, 4, 128], BF16, "wmask")
            smask = P.sb([128, 2, 128], BF16, "smask")
            cbase = P.sb([128, 288], BF16, "cbase")
            ov = P.sb([128, 4, 128], BF16, "ov")
            wf = P.sb([128, 256], F32, "wf")
            ones = P.sb([128, 1], BF16, "ones")
            for dst, nm in ((i4, "i4"), (wmask, "wmask"), (smask, "smask"), (cbase, "cbase"), (ov, "ov"), (wf, "wf")):
                P.dma(nq(), dst[:], cd[nm], writes=[bc2])
            P.memset(ones[:], 1.0, [bc2])
            wat = P.sb([128, 4, D], BF16, "wat")
            wo = P.sb([128, 8, D], BF16, "wo")
            bwat = Buf("wat")
            mark2 = P.off
            stg = [P.sb([128, 4, D], F32, "stga") for _ in range(2)]
            bstg = [Buf(), Buf()]
            for g in range(2):
                P.dma(nq(), stg[0][64 * g:64 * g + 64, :, :],
                      w_attn[g * 256:(g + 1) * 256, :].rearrange("(hl d) o -> d hl o", d=64), writes=[bstg[0]])
            P.cp(wat[:], stg[0][:], [bstg[0]], [bwat], eng="dve")
            for hf in range(2):
                P.dma(nq(), stg[1 - hf][:], w_out[hf * 512:(hf + 1) * 512, :].rearrange("(c p) o -> p c o", p=128), writes=[bstg[1 - hf]])
                P.cp(wo[:, hf * 4:(hf + 1) * 4, :], stg[1 - hf][:], [bstg[1 - hf]], [bwat], eng="dve")
            P.barrier()
            P.off = mark2
            qr_t = [[P.sb([128, 4, 128], BF16, "qrt") for _g in range(2)] for _ in range(2)]
            qp_t = [[P.sb([128, 4, 128], BF16, "qpt") for _g in range(2)] for _ in range(2)]
            gbc_t = [P.sb([64, 2, 3, 4, 128], BF16, "gbc") for _ in range(2)]
            gm_t1 = P.sb([128, 8, 128], BF16, "gmt")
            yb_t1 = P.sb([128, 8, 128], BF16, "ybt")
            x_t1 = P.sb([128, D], F32, "xt2")
            gm_t, yb_t, x_t = [gm_t1] * 2, [yb_t1] * 2, [x_t1] * 2
            bq = [Buf(), Buf()]
            bq21 = Buf()
            bq2 = [bq21, bq21]
            NPT = 4
            PT = [P.sb([128, 512], BF16, "PT") for _ in range(NPT)]
            bPT = [Buf() for _ in range(NPT)]
            rdq = P.sb([128, 4], F32, "rdq")
            brdq = Buf()
            imp = P.sb([128, 128], F32, "imp")
            sc2 = P.sb([128, 128], F32, "sc2")
            m8 = P.sb([128, 16], F32, "m8")
            bimp = Buf()
            selb = P.sb([128, 128], BF16, "selb")
            bselb = Buf()
            selbT = [P.sb([128, 4, 128], BF16, "selbT") for _ in range(2)]
            selbs = [P.sb([128, 128], BF16, "selbs") for _ in range(2)]
            bselbs = [Buf(), Buf()]
            bselbT = [Buf(), Buf()]
            off_rden = P.off
            rden = P.sb([64, 512], F32, "rden")
            coef = P.sb([64, 512], F32, "coef")
            ctb = P.sb([64, 512], F32, "ctb")
            lnd = P.sb([64, 512], F32, "lnd")
            blnd = Buf()
            off_acc = P.off
            accT = [P.sb([64, 512], F32, "accT") for _ in range(2)]
            bacc = [Buf(), Buf()]
            bcomb = Buf()
            nsaT = P.sb([128, 4, 128], BF16, "nsaT")
            bnsa = Buf()
            m1 = P.sb([128, 8, 128], F32, "m1")
            bm1 = Buf()
            mT = P.sb([128, 8, 128], BF16, "mT")
            bmT = Buf()
            x2s = P.sb([128, D], F32, "x2s")
            bx2 = Buf()
            NGr = NG.rearrange("(g hl br) t -> g br hl t", g=2, hl=4, br=3)

            def loads(i):
                b2 = i % 2
                t0 = i * 128
                for g in range(2):
                    rw = slice(64 * g, 64 * g + 64)
                    P.dma("sp", qr_t[b2][g][rw], QR[rw, :, t0:t0 + 128], writes=[bq[b2]])
                    P.dma("sp", qp_t[b2][g][rw], QP[rw, :, t0:t0 + 128], writes=[bq[b2]])
                for g in range(2):
                    for br in range(3):
                        P.dma("sp", gbc_t[b2][:, g, br], NGr[g, br:br + 1, :, t0:t0 + 128].to_broadcast([64, 4, 128]),
                              writes=[bq[b2]], slow=True)

            def loads_e(i):
                b2 = i % 2
                t0 = i * 128
                P.dma("sp", gm_t[b2][:], GM[:, 0:8, t0:t0 + 128], writes=[bq2[b2]])
                P.dma("sp", yb_t[b2][:], YB[:, :, t0:t0 + 128], writes=[bq2[b2]])
                P.dma("sp", x_t[b2][:], xown[t0:t0 + 128, :], writes=[bq2[b2]])

            jobs = []
            oacc_banks = [2, 3, 7]
            oacc_i = [0]
            for i in range(NTO):
                nkc = (8 * (2 * i + 1) + 6) // 128 + 1
                k0 = max(0, 2 * i - 4)
                for br, kts in ((0, list(range(nkc))), (2, list(range(k0, 2 * i + 2))), (1, list(range(2 * i + 2)))):
                    for g in range(2):
                        oacc_i[0] += 1
                        ob = oacc_banks[oacc_i[0] % 3]
                        for n_, kt in enumerate(kts):
                            jobs.append(dict(i=i, g=g, br=br, kt=kt, first=(n_ == 0), last=(n_ == len(kts) - 1), ob=ob,
                                             tile_first=(br == 0 and g == 0 and n_ == 0),
                                             tile_last=(br == 1 and g == 1 and n_ == len(kts) - 1)))
            sbi = [0]
            pti = [0]

            def score(J):
                i, g, br, kt = J["i"], J["g"], J["br"], J["kt"]
                b2 = i % 2
                rows = slice(64 * g, 64 * g + 64)
                masks = []
                if br == 0:
                    lhsT, rk, q_ap = kcT[:, kt * 128:(kt + 1) * 128], [bkc], qr_t[b2][g][:, :, :]
                    Dv = 16 * i - 128 * kt
                    if Dv < 130:
                        s0 = 136 - Dv
                        masks.append((cbase[:, s0:s0 + 128], i4[:], [bc2]))
                elif br == 1:
                    lhsT, rk, q_ap = ksT[:, kt * 128:(kt + 1) * 128], [bK], qp_t[b2][g][:, :, :]
                    masks.append((we[:, kt * 128:(kt + 1) * 128], selbT[g][:].rearrange("p h q -> p (h q)"), [bc2, bselbT[g]]))
                    if kt - 2 * i in (0, 1):
                        masks.append((smask[:, kt - 2 * i, :], i4[:], [bc2]))
                else:
                    lhsT, rk, q_ap = kwT[:, kt * 128:(kt + 1) * 128], [bK], qp_t[b2][g][:, :, :]
                    pofs = {1: 0, 0: 1, -3: 2, -4: 3}
                    if kt - 2 * i in pofs:
                        masks.append((wmask[:, pofs[kt - 2 * i], :], i4[:], [bc2]))
                sbi[0] += 1
                s_ps, s_pb = bank((0, 1, 5)[sbi[0] % 3])
                n = len(masks)
                P.mm(s_ps, lhsT, q_ap.rearrange("p h q -> p (h q)"), True, n == 0, rk + [bq[b2]], [s_pb])
                for mi, (ml, mr, mrd) in enumerate(masks):
                    P.mm(s_ps, ml, mr, False, mi == n - 1, mrd, [s_pb])
                pti[0] += 1
                k = pti[0] % NPT
                P.act(PT[k][:], s_ps, AF.Exp, [s_pb], [bPT[k]], scale=0.125)
                J["pt"] = (PT[k], bPT[k])

            def combine(J):
                i, g, br = J["i"], J["g"], J["br"]
                b2 = i % 2
                o_ps, o_pb = bank(J["ob"])
                if br == 0 and i == 0:
                    P.ts(rden[:], o_ps[64:128, :], 1e-30, None, ALU.add, None, [o_pb], [bcomb])
                    P.op("dve", lambda e: e.reciprocal(out=rden[:], in_=rden[:]), [bcomb], [bcomb])
                else:
                    P.act(lnd[:], o_ps[64:128, :], AF.Ln, [o_pb], [blnd])
                    P.act(rden[:], lnd[:], AF.Exp, [blnd], [bcomb], scale=-1.0)
                P.tt(coef[:], rden[:], gbc_t[b2][:, g, br].rearrange("p h q -> p (h q)"), ALU.mult, [bcomb, bq[b2]], [bcomb])
                if br == 0:
                    P.tt(accT[g][:], o_ps[0:64, :], coef[:], ALU.mult, [o_pb, bcomb], [bacc[g]])
                elif br == 2:
                    P.tt(ctb[:], o_ps[0:64, :], coef[:], ALU.mult, [o_pb, bcomb], [bcomb])
                    P.tt(accT[g][:], accT[g][:], ctb[:], ALU.add, [bcomb, bacc[g]], [bacc[g]])
                else:
                    P.tt(ctb[:], o_ps[0:64, :], coef[:], ALU.mult, [o_pb, bcomb], [bcomb])
                    P.tt(nsaT[64 * g:64 * g + 64, :, :], accT[g][:].rearrange("p (h q) -> p h q", h=4),
                         ctb[:].rearrange("p (h q) -> p h q", h=4), ALU.add, [bcomb, bacc[g]], [bnsa])

            def selection(J):
                i, g = J["i"], J["g"]
                imp_ps, imp_pb = bank(4)
                dn_ps, dn_pb = bank(6)
                P.ts(rdq[:], dn_ps[:, 4 * g:4 * g + 4], 1e-30, None, ALU.add, None, [dn_pb], [brdq])
                P.op("dve", lambda e: e.reciprocal(out=rdq[:], in_=rdq[:]), [brdq], [brdq])
                P.ts(imp[:], imp_ps[:, 0:128], rdq[:, 0:1], None, ALU.mult, None, [imp_pb, brdq], [bimp])
                for hl in range(1, 4):
                    P.stt(imp[:], imp_ps[:, hl * 128:(hl + 1) * 128], rdq[:, hl:hl + 1], imp[:], ALU.mult, ALU.add,
                          [imp_pb, brdq, bimp], [bimp])
                P.tt(imp[:], imp[:], wf[:, 128 - 4 * i:256 - 4 * i], ALU.add, [bimp, bc2], [bimp])
                P.ts(imp[:, 0:1], imp[:, 0:1], 1000.0, None, ALU.add, None, [bimp], [bimp])
                P.op("dve", lambda e: e.max(out=m8[:, 0:8], in_=imp[:]), [bimp], [bimp])
                P.op("dve", lambda e: e.match_replace(out=sc2[:], in_to_replace=m8[:, 0:8], in_values=imp[:], imm_value=-3e38),
                     [bimp], [bimp])
                P.op("dve", lambda e: e.max(out=m8[:, 8:16], in_=sc2[:]), [bimp], [bimp])
                P.ts(selb[:], imp[:], m8[:, 15:16], NEG, ALU.is_lt, ALU.mult, [bimp], [bselb])
                P.cp(selbs[g][:], selb[:], [bselb], [bselbs[g]])

            def selection_b(i, g):
                dn_ps, dn_pb = bank(6)
                tpv = dn_ps.bitcast(BF16)[:, 256 + 128 * g:384 + 128 * g]
                P.tr(tpv, selbs[g][:], ident[:], [bselbs[g], bconst], [dn_pb])
                P.cp(selbT[g][:], tpv.unsqueeze(1).to_broadcast([128, 4, 128]), [dn_pb], [bselbT[g]])

            def pv(J):
                i, g, br, kt = J["i"], J["g"], J["br"], J["kt"]
                pt, bpt = J["pt"]
                o_ps, o_pb = bank(J["ob"])
                if br == 0:
                    vl, rv = vca[:, kt, g, :], [bvca]
                elif br == 1:
                    vl, rv = vsw[:, kt, 0, g, :], [bV]
                else:
                    vl, rv = vsw[:, kt, 1, g, :], [bV]
                P.mm(o_ps, vl, pt[:], J["first"], J["last"], rv + [bpt], [o_pb])
                if br == 0:
                    imp_ps, imp_pb = bank(4)
                    dn_ps, dn_pb = bank(6)
                    for hl in range(4):
                        P.mm(imp_ps[:, hl * 128:(hl + 1) * 128], pt[:, hl * 128:(hl + 1) * 128], ov[:, kt, :],
                             J["first"] and hl == 0, J["last"], [bpt, bc2], [imp_pb])
                    for hl in range(4):
                        P.mm(dn_ps[:, 4 * g + hl:4 * g + hl + 1], pt[:, hl * 128:(hl + 1) * 128], ones[:],
                             J["first"] and hl == 0, J["last"], [bpt, bc2], [dn_pb])

            def epi_ya(i, h):
                b2 = i % 2
                ya, yab = bank(6)
                for oc4 in range(4):
                    ocn = 4 * h + oc4
                    for hl in range(4):
                        P.mm(ya[:, oc4 * 128:(oc4 + 1) * 128], wat[:, hl, ocn * 128:(ocn + 1) * 128], nsaT[:, hl, :],
                             hl == 0 and oc4 == 0, hl == 3, [bwat, bnsa], [yab])
                P.tt(m1[:, 4 * h:4 * h + 4, :], ya.rearrange("p (c q) -> p c q", c=4), gm_t[b2][:, 4 * h:4 * h + 4, :], ALU.mult,
                     [yab, bq2[b2]], [bm1])
                P.tt(mT[:, 4 * h:4 * h + 4, :], m1[:, 4 * h:4 * h + 4, :], yb_t[b2][:, 4 * h:4 * h + 4, :], ALU.add, [bm1, bq2[b2]], [bmT])

            def epi_xo(i, hf):
                b2 = i % 2
                t0 = i * 128
                xo, xob = bank(6)
                for kc in range(8):
                    P.mm(xo, mT[:, kc, :], wo[:, kc, hf * 512:(hf + 1) * 512], kc == 0, kc == 7, [bmT, bwat], [xob])
                P.tt(x2s[:, hf * 512:(hf + 1) * 512], xo, x_t[b2][:, hf * 512:(hf + 1) * 512], ALU.add, [xob, bq2[b2]], [bx2])
                if hf == 1:
                    P.dma("sp", X2[t0:t0 + 128, :], x2s[:], reads=[bx2])
                    if i + 1 < NTO:
                        loads_e(i + 1)

            for b2_ in range(2):
                for g_ in range(2):
                    ow = slice(64 * (1 - g_), 64 * (1 - g_) + 64)
                    P.memset(qr_t[b2_][g_][ow], 0.0, [bq[b2_]])
                    P.memset(qp_t[b2_][g_][ow], 0.0, [bq[b2_]])
            loads(0)
            loads_e(0)
            score(jobs[0])
            score(jobs[1])
            pend = []
            idxS = {}
            for j, J in enumerate(jobs):
                if J["br"] == 1 and J["first"]:
                    idxS[(J["i"], J["g"])] = j
            for j, J in enumerate(jobs):
                pend.sort(key=lambda t_: t_[0])
                while pend and pend[0][0] <= j:
                    _, fn_, ar_, h_ = pend.pop(0)
                    fn_(ar_, h_)
                if J["tile_first"] and J["i"] + 1 < NTO:
                    loads(J["i"] + 1)
                if j + 2 < len(jobs):
                    score(jobs[j + 2])
                pv(J)
                if J["last"]:
                    combine(J)
                    if J["br"] == 0:
                        selection(J)
                        tS = idxS[(J["i"], J["g"])]
                        pend.append((max(j + 1, min(j + 5, tS - 2)), selection_b, J["i"], J["g"]))
                if J["br"] == 2 and J["g"] == 0 and J["first"] and J["i"] > 0:
                    ip = J["i"] - 1
                    pend.extend([(j + 1, epi_ya, ip, 0), (j + 4, epi_ya, ip, 1), (j + 7, epi_xo, ip, 0), (j + 10, epi_xo, ip, 1)])
                if j == len(jobs) - 1:
                    pend.extend([(j, epi_ya, J["i"], 0), (j, epi_ya, J["i"], 1), (j, epi_xo, J["i"], 0), (j, epi_xo, J["i"], 1)])
                if J["tile_last"]:
                    pend.sort(key=lambda t_: (t_[1] is not selection_b, t_[0]))
                    keep = []
                    while pend:
                        it = pend.pop(0)
                        if it[1] is selection_b and it[2] != J["i"]:
                            keep.append(it)
                        else:
                            it[1](it[2], it[3])
                    pend = keep

        def phase3():
            P.off = persist_off
            wu = P.sb([128, 8, 4096], BF16, "wu")
            wd = P.sb([128, 32, D], BF16, "wd")
            bwu = Buf("wu")
            stg = [P.sb([128, 4096], F32, "stg3") for _ in range(2)]
            bstq = [[Buf() for _ in range(4)] for _ in range(2)]
            mark3 = None
            bstg = [Buf(), Buf()]
            k = 0
            for kc in range(8):
                for c_ in range(4):
                    P.dma(bulkq(), stg[k % 2][:, c_ * 1024:(c_ + 1) * 1024], w_up[kc * 128:(kc + 1) * 128, c_ * 1024:(c_ + 1) * 1024],
                          writes=[bstq[k % 2][c_]])
                P.cp(wu[:, kc, :], stg[k % 2][:], bstq[k % 2], [bwu] + bstq[k % 2], eng="dve")
                k += 1
            for c4 in range(8):
                for c_ in range(4):
                    P.dma(bulkq(), stg[k % 2][:, c_ * 1024:(c_ + 1) * 1024],
                          w_down[c4 * 512 + c_ * 128:c4 * 512 + (c_ + 1) * 128, :], writes=[bstq[k % 2][c_]])
                P.cp(wd[:, c4 * 4:(c4 + 1) * 4, :], stg[k % 2][:].rearrange("p (c o) -> p c o", c=4), bstq[k % 2], [bwu] + bstq[k % 2],
                     eng="dve")
                k += 1
            P.barrier()
            P.off -= 2 * 4096 * 4
            g2T = P.sb([128, 8], F32)
            gfb = P.sb([128, D], F32)
            bc3 = Buf()
            P.dma("sp", g2T[:], g_mlp.rearrange("(c p) -> p c", p=128), writes=[bc3], slow=True)
            P.dma("sp", gfb[:], g_fin.rearrange("(a d) -> a d", a=1).to_broadcast([128, D]), writes=[bc3], slow=True)
            xt2 = [P.sb([128, 2, D], F32) for _ in range(2)]
            bxt2 = [[Buf(), Buf()], [Buf(), Buf()]]
            junk = P.sb([128, D], BF16)
            bjunk = Buf()
            ssq = P.sb([128, 8], F32)
            bss = [Buf() for _ in range(4)]
            hb2 = [P.sb([128, D], BF16) for _ in range(2)]
            bhb2 = [Buf(), Buf()]
            hT2 = [P.sb([128, 8, 256], BF16) for _ in range(2)]
            bhT2 = [Buf(), Buf()]
            rl = P.sb([128, 256], F32)
            brl = Buf()
            hidT = P.sb([128, 32, 256], BF16)
            bhid = Buf()
            x3 = P.sb([128, D], F32)
            bx3 = Buf()
            junk2 = junk
            ss2 = P.sb([128, 2], F32)
            bss2 = Buf()
            ot1 = P.sb([128, D], F32)
            bot1 = Buf()
            psT2 = [ps[0][:, 0:512].bitcast(BF16), ps[0][:, 512:1024].bitcast(BF16)]
            bpsT2 = [psb[0][0], psb[0][1]]
            NB3 = NTO // 2

            def stats3(blk):
                xt = xt2[blk % 2]
                for tt in range(2):
                    tile = blk * 2 + tt
                    bx = bxt2[blk % 2][tt]
                    P.dma(nq(), xt[:, tt, :], X2[tile * 128:(tile + 1) * 128, :], writes=[bx])
                    P.act(junk[:], xt[:, tt, :], AF.Square, [bx], [bjunk, bss[tt]], accum=ssq[:, tt:tt + 1])
                    P.ts(ssq[:, 4 + tt:5 + tt], ssq[:, tt:tt + 1], 1.0 / D, 1e-6, ALU.mult, ALU.add, [bss[tt]], [bss[tt]])
                    P.act(ssq[:, 4 + tt:5 + tt], ssq[:, 4 + tt:5 + tt], AF.Sqrt, [bss[tt]], [bss[tt]])
                    P.op("dve", (lambda tt=tt: (lambda e: e.reciprocal(out=ssq[:, 4 + tt:5 + tt], in_=ssq[:, 4 + tt:5 + tt])))(), [bss[tt]], [bss[tt]])
                    P.act(hb2[tt][:], xt[:, tt, :], AF.Copy, [bx, bss[tt]], [bhb2[tt]], scale=ssq[:, 4 + tt:5 + tt])

            def trans3(blk):
                hT, bhT = hT2[blk % 2], bhT2[blk % 2]
                for tt in range(2):
                    pT, bpT = psT2[tt], bpsT2[tt]
                    for kc in range(8):
                        P.tr(pT[:, kc * 128:(kc + 1) * 128], hb2[tt][:, kc * 128:(kc + 1) * 128], ident[:], [bhb2[tt], bconst], [bpT])
                    P.tt(hT[:, :, tt * 128:(tt + 1) * 128], pT.rearrange("p (c t) -> p c t", c=8),
                         g2T[:].unsqueeze(2).to_broadcast([128, 8, 128]), ALU.mult, [bpT, bc3], [bhT])

            def up3(blk):
                hT, bhT = hT2[blk % 2], bhT2[blk % 2]
                for f in range(32):
                    pa, pb = bank(2 + f % 2)
                    for kc in range(8):
                        P.mm(pa[:, 0:256], wu[:, kc, f * 128:(f + 1) * 128], hT[:, kc, :], kc == 0, kc == 7, [bwu, bhT], [pb])
                    P.act(rl[:], pa[:, 0:256], AF.Relu, [pb], [brl])
                    P.tt(hidT[:, f, :], rl[:], rl[:], ALU.mult, [brl], [bhid], eng="dve")

            def down3(blk):
                res = []
                for tt in range(2):
                    po = ps[2 + tt % 2][:, :]
                    pob = psb[2 + tt % 2]
                    for hf in range(2):
                        for f in range(32):
                            P.mm(po[:, hf * 512:(hf + 1) * 512], hidT[:, f, tt * 128:(tt + 1) * 128], wd[:, f, hf * 512:(hf + 1) * 512],
                                 f == 0, f == 31, [bhid, bwu], [pob[hf]])

            def epi3(blk):
                xt = xt2[blk % 2]
                for tt in range(2):
                    tile = blk * 2 + tt
                    bx = bxt2[blk % 2][tt]
                    po = ps[2 + tt % 2][:, :]
                    pob = psb[2 + tt % 2]
                    P.tt(x3[:], po, xt[:, tt, :], ALU.add, [pob[0], pob[1], bx], [bx3])
                    P.act(junk2[:], x3[:], AF.Square, [bx3], [bss2, bjunk], accum=ss2[:, 0:1])
                    P.ts(ss2[:, 1:2], ss2[:, 0:1], 1.0 / D, 1e-6, ALU.mult, ALU.add, [bss2], [bss2])
                    P.act(ss2[:, 1:2], ss2[:, 1:2], AF.Sqrt, [bss2], [bss2])
                    P.op("dve", lambda e: e.reciprocal(out=ss2[:, 1:2], in_=ss2[:, 1:2]), [bss2], [bss2])
                    P.stt(ot1[:], x3[:], ss2[:, 1:2], gfb[:], ALU.mult, ALU.mult, [bx3, bss2, bc3], [bot1])
                    P.dma(nq(), out[tile * 128:(tile + 1) * 128, :], ot1[:], reads=[bot1])

            stats3(0)
            trans3(0)
            for blk in range(NB3):
                up3(blk)
                if blk + 1 < NB3:
                    stats3(blk + 1)
                down3(blk)
                if blk + 1 < NB3:
                    trans3(blk + 1)
                epi3(blk)

        phases = dbg.get("phases", "1a,1b,2,3") if dbg else "1a,1b,2,3"
        if "1a" in phases:
            phase1a()
            P.barrier()
        if "1b" in phases:
            phase1b()
            P.barrier()
        if "2" in phases:
            phase2()
            P.barrier()
        if "3" in phases:
            phase3()
        if dbg and "dump" in dbg:
            P.barrier()
            for nm in dbg["dump"]:
                src = {"QR": QR, "QP": QP, "KS": KS, "KW": KW, "KC": KC, "VC": VC, "VSW": VSW, "GM": GM, "UT": UT,
                       "NG": NG, "YB": YB, "X2": X2}[nm]
                dd = nc.dram_tensor("dump_" + nm, list(src.shape), src.dtype, kind="ExternalOutput").ap()
                P.dma("sp", dd, src)
        nw = P.finalize()
        print(f"[build] ops={len(P.ops)} waits={nw} sems={P.nsem}")
    return nc, consts


_CACHE = {}


def kernel(**inputs):
    if "nc" not in _CACHE:
        _CACHE["nc"] = build()
        _CACHE["consts"] = [host_consts(0), host_consts(1)]
    nc, _ = _CACHE["nc"]
    x = np.asarray(inputs["x"], dtype=np.float32)
    B = x.shape[0]
    shared = {}
    for k, v in inputs.items():
        if k == "x":
            continue
        a = np.ascontiguousarray(np.asarray(v, dtype=np.float32))
        if k != "norm_final_g":
            a = a[0]
        shared[k] = np.ascontiguousarray(a)
    in_maps = []
    for c in range(8):
        bidx, par = c // 2, c % 2
        m = dict(shared)
        for k, v in _CACHE["consts"][par].items():
            m["c_" + k] = v
        xb = x[bidx % B]
        m["x"] = np.ascontiguousarray(xb)
        m["xown"] = np.ascontiguousarray(xb.reshape(NTO, 2, 128, D)[:, par].reshape(TO, D))
        in_maps.append(m)
    res = run_bass_kernel_spmd(nc, in_maps, core_ids=list(range(8)))
    out = np.empty((B, T, D), np.float32)
    ov = out.reshape(B, NTO, 2, 128, D)
    for c in range(8):
        bidx, par = c // 2, c % 2
        if bidx < B:
            ov[bidx, :, par] = np.asarray(res.results[c]["out"], dtype=np.float32).reshape(NTO, 128, D)
    return out
```

```python
import math
import numpy as np
import ml_dtypes
import concourse.bass as bass
import concourse.mybir as mybir
from concourse.bass_utils import run_bass_kernel_spmd
from contextlib import ExitStack

F32 = mybir.dt.float32
BF16 = mybir.dt.bfloat16
ALU = mybir.AluOpType
AF = mybir.ActivationFunctionType

T = 8192
NT = T // 128
TO = T // 2
NTO = TO // 128
D = 1024
INW = 3864
QOFF, KVOFF, NGOFF, UOFF, MGOFF = 0, 512, 1280, 1304, 1816
NEG = -30000.0
NO_SAME_ENGINE_SYNC = False
DBG = {}


class Buf:
    __slots__ = ("name", "last_w", "readers")

    def __init__(self, name=""):
        self.name = name
        self.last_w = None
        self.readers = []


class Op:
    __slots__ = ("eng", "fn", "idx", "deps", "sig", "is_dma", "sem", "val", "prewait")


class Prog:
    ENGS = ("pe", "act", "dve", "pool", "sp")
    NDSEM = 12

    def __init__(self, nc, stack):
        self.nc = nc
        self.stack = stack
        self.ops = []
        self.engobj = {"pe": nc.tensor, "act": nc.scalar, "dve": nc.vector,
                       "pool": nc.gpsimd, "sp": nc.sync}
        self.nsem = 0
        self.last_on = {}
        self.dmas_since_barrier = []
        self.off = 16640
        self.ntens = 0

    def sb(self, shape, dt, name=None, at=None):
        esz = 4 if dt == F32 else 2
        n = 1
        for s in shape[1:]:
            n *= s
        nbytes = (n * esz + 63) // 64 * 64
        self.ntens += 1
        if at is not None:
            return self.nc.alloc_sbuf_tensor_at(f"{name or 't'}_{self.ntens}", list(shape), dt, offset=at)
        t = self.nc.alloc_sbuf_tensor_at(f"{name or 't'}_{self.ntens}", list(shape), dt, offset=self.off)
        self.off += nbytes
        assert self.off <= 226000, f"SBUF overflow {self.off}"
        return t

    def newsem(self, name):
        self.nsem += 1
        return self.stack.enter_context(self.nc.semaphore(f"{name}_{self.nsem}"))

    def _add(self, eng, fn, reads, writes, is_dma):
        o = Op()
        o.eng, o.fn, o.idx, o.deps, o.sig, o.is_dma = eng, fn, len(self.ops), set(), False, is_dma
        o.sem, o.val, o.prewait = None, 0, None
        for b in reads:
            if b.last_w is not None:
                o.deps.add(b.last_w)
        for b in writes:
            if b.last_w is not None:
                o.deps.add(b.last_w)
            o.deps.update(b.readers)
        for b in reads:
            b.readers.append(o.idx)
        for b in writes:
            b.last_w = o.idx
            b.readers = []
        o.deps.discard(o.idx)
        self.ops.append(o)
        if is_dma:
            self.dmas_since_barrier.append(o.idx)
        elif fn is not None:
            self.last_on[eng] = o.idx
        return o

    def op(self, eng, fn, reads=(), writes=()):
        return self._add(eng, fn, reads, writes, False)

    def dma(self, q, out, in_, reads=(), writes=(), slow=False):
        if slow:
            return self._add(q, lambda e: e.dma_start(out=out, in_=in_, allow_slow_non_contiguous=True), reads, writes, True)
        return self._add(q, lambda e: e.dma_start(out=out, in_=in_), reads, writes, True)

    def barrier(self):
        deps = set(self.last_on.values()) | set(self.dmas_since_barrier)
        self.dmas_since_barrier = []
        for e in self.ENGS:
            o = self._add(e, None, (), (), False)
            o.deps = set(d for d in deps)

    def mm(self, out, lhsT, rhs, start, stop, r, w):
        self.op("pe", lambda e: e.matmul(out, lhsT, rhs, start=start, stop=stop, skip_group_check=True), r, w)

    def tr(self, out, in_, ident, r, w):
        self.op("pe", lambda e: e.transpose(out, in_, ident), r, w)

    def act(self, out, in_, func, r, w, bias=None, scale=None, accum=None):
        kw = {}
        if bias is not None:
            kw["bias"] = bias
        if scale is not None:
            kw["scale"] = scale
        if accum is not None:
            kw["accum_out"] = accum
        self.op("act", lambda e: e.activation(out=out, in_=in_, func=func, **kw), r, w)

    def tt(self, out, in0, in1, op, r, w, eng="dve"):
        self.op(eng, lambda e: e.tensor_tensor(out=out, in0=in0, in1=in1, op=op), r, w)

    def ts(self, out, in0, s1, s2, op0, op1, r, w, eng="dve"):
        if op1 is None:
            self.op(eng, lambda e: e.tensor_scalar(out=out, in0=in0, scalar1=s1, scalar2=None, op0=op0), r, w)
        else:
            self.op(eng, lambda e: e.tensor_scalar(out=out, in0=in0, scalar1=s1, scalar2=s2, op0=op0, op1=op1), r, w)

    def stt(self, out, in0, scalar, in1, op0, op1, r, w, eng="dve"):
        self.op(eng, lambda e: e.scalar_tensor_tensor(out=out, in0=in0, scalar=scalar, in1=in1, op0=op0, op1=op1), r, w)

    def cp(self, out, in_, r, w, eng="dve"):
        self.op(eng, lambda e: e.tensor_copy(out=out, in_=in_), r, w)

    def memset(self, ap, v, w, eng="dve"):
        self.op(eng, lambda e: e.memset(ap, v), (), w)

    def finalize(self):
        ops = self.ops
        for o in ops:
            if o.eng == "pe" and o.fn is not None:
                o.deps = set(d for d in o.deps if not (ops[d].eng == "pe" and not ops[d].is_dma))
            if NO_SAME_ENGINE_SYNC and o.eng in ("dve", "act") and o.fn is not None and not o.is_dma:
                o.deps = set(d for d in o.deps if not (ops[d].eng == o.eng and not ops[d].is_dma))
            for d in o.deps:
                ops[d].sig = True
            if o.is_dma:
                o.sig = True
        ROLL = 12000
        cur, cnt = {}, {}
        dsem = {}
        dcount = {}
        for o in ops:
            if not o.sig:
                continue
            if o.is_dma:
                q = o.eng
                if q not in dsem:
                    dsem[q] = [self.newsem("d" + q) for _ in range(self.NDSEM)]
                    dcount[q] = 0
                k = dcount[q]
                dcount[q] += 1
                o.sem = dsem[q][k % self.NDSEM]
                o.val = 16 * (k // self.NDSEM + 1)
                if k >= self.NDSEM:
                    o.prewait = (o.sem, o.val - 16)
            else:
                e = o.eng
                if e not in cur or cnt[e] >= ROLL:
                    cur[e] = self.newsem(e)
                    cnt[e] = 0
                cnt[e] += 1
                o.sem, o.val = cur[e], cnt[e]
        waited = {e: {} for e in self.ENGS}
        nwait = 0
        for o in ops:
            e = self.engobj[o.eng]
            need = {}
            for d in o.deps:
                p = ops[d]
                k = id(p.sem)
                if k not in need or need[k][1] < p.val:
                    need[k] = (p.sem, p.val)
            if o.prewait is not None:
                k = id(o.prewait[0])
                if k not in need or need[k][1] < o.prewait[1]:
                    need[k] = o.prewait
            w = waited[o.eng]
            for k, (s, v) in need.items():
                if w.get(k, 0) >= v:
                    continue
                e.wait_ge(s, v)
                w[k] = v
                nwait += 1
            if o.fn is None:
                continue
            ins = o.fn(e)
            if o.sig:
                ins.then_inc(o.sem, 16 if o.is_dma else 1)
        fe = self.engobj["sp"]
        for q, sems in dsem.items():
            n = dcount[q]
            for j, s in enumerate(sems):
                c = (n - j + self.NDSEM - 1) // self.NDSEM if n > j else 0
                if c > 0:
                    fe.wait_ge(s, 16 * c)
        return nwait


def host_consts(par=0):
    c = {}
    bf = ml_dtypes.bfloat16
    c["ident"] = np.eye(128, dtype=np.float32).astype(bf)
    c["identf"] = np.eye(128, dtype=np.float32)
    c["i4"] = np.tile(np.eye(128, dtype=np.float32), (1, 4)).astype(bf)
    half = 8
    inv = 500000.0 ** (-(np.arange(half, dtype=np.float32) * 2.0) / 16.0)
    ang = np.arange(T, dtype=np.float32)[None, :] * inv[:, None].astype(np.float32)
    cosv, sinv = np.cos(ang).astype(np.float32), np.sin(ang).astype(np.float32)
    rc = np.ones((128, T), np.float32)
    rs = np.zeros((128, T), np.float32)
    for g in range(2):
        rc[64 * g:64 * g + 8] = cosv
        rc[64 * g + 8:64 * g + 16] = cosv
        rs[64 * g:64 * g + 8] = sinv
        rs[64 * g + 8:64 * g + 16] = sinv
    c["ropec"], c["ropes"] = rc, rs
    own_pos = (np.arange(TO) // 128 * 2 + par) * 128 + np.arange(TO) % 128
    c["ropeco"], c["ropeso"] = np.ascontiguousarray(rc[:, own_pos]), np.ascontiguousarray(rs[:, own_pos])
    pm = np.zeros((128, 128), np.float32)
    for g in range(2):
        for d in range(8):
            pm[64 * g + d + 8, 64 * g + d] = -1.0
            pm[64 * g + d, 64 * g + d + 8] = 1.0
    c["pm"] = pm.astype(bf)
    r = np.arange(128)[:, None]
    kl = np.arange(128)[None, :]
    def vis_mask(p, window):
        dk = kl - r
        v = dk <= 128 * (par - p)
        if window:
            v = v & (dk > 128 * (par - p) - 512)
        return np.where(v, 0.0, NEG).astype(np.float32)
    c["wmask"] = np.stack([vis_mask(p, True) for p in (1, 0, -3, -4)], axis=1).astype(bf)
    c["smask"] = np.stack([vis_mask(p, False) for p in (0, 1)], axis=1).astype(bf)
    m = np.arange(0, 288)[None, :] - 136 - 8 * par
    fl = np.floor((np.arange(128)[:, None] - 31) / 16.0)
    c["cbase"] = np.where(m <= fl, 0.0, NEG).astype(np.float32).astype(bf)
    c["sel01"] = np.tile(np.array([[1.0 - par, float(par)]], np.float32), (128, 1))
    n_ = np.arange(512)[:, None] * 16
    j_ = np.arange(128)[None, :] * 64
    ov = ((n_ < j_ + 64) & (n_ + 32 > j_)).astype(np.float32)
    ov[511] = 0
    c["ov"] = ov.reshape(4, 128, 128).transpose(1, 0, 2).copy().astype(bf)
    mm = np.arange(-128, 128)[None, :]
    hi = (np.arange(128)[:, None] >= 64).astype(np.int64) + 2 * par
    wf = np.zeros((128, 256), np.float32)
    wf[(mm == hi) | (mm == hi - 1)] = 1000.0
    wf[np.broadcast_to(mm > hi, wf.shape)] = -1e30
    c["wf"] = wf
    we = np.zeros((128, T), np.float32)
    we[np.arange(T) // 64, np.arange(T)] = 1.0
    c["we"] = we.astype(bf)
    sg = np.zeros((24, 24, 64), np.float32)
    for k in range(24):
        sg[k, k, :] = 1.0
    c["selg"] = sg.astype(bf)
    mb = np.zeros((128, 4, 8), np.float32)
    mc = np.zeros((128, 4, 2), np.float32)
    for q in range(4):
        for gp in range(2):
            mb[64 * gp:64 * gp + 64, q, 2 * q + gp] = 1.0
            mc[32 * q + 16 * gp:32 * q + 16 * gp + 16, q, gp] = 1.0
    c["mb"], c["mc"] = mb, mc
    rowsel = np.zeros((128, 4), np.float32)
    rowsel[0:64, 0] = 1.0
    rowsel[64:128, 1] = 1.0
    rowsel[:, 2:4] = -rowsel[:, 0:2]
    c["rowsel"] = rowsel
    tq = np.arange(128) // 16
    c["cmask8"] = (tq[None, :] >= tq[:, None]).astype(np.float32)
    selc = np.zeros((128, 64), np.float32)
    for j in range(64):
        selc[(2 * (j // 16) + par) * 16 + j % 16, j] = 1.0
    c["selc"] = selc.astype(bf)
    return c


CONST_SPECS = None


def build(dbg=None):
    nc = bass.Bass("TRN2", target_bir_lowering=False)
    consts = host_consts()
    dram_in = {}

    def din(name, shape, dt):
        dram_in[name] = nc.dram_tensor(name, list(shape), dt, kind="ExternalInput").ap()
        return dram_in[name]

    x = din("x", [T, D], F32)
    xown = din("xown", [TO, D], F32)
    w_in = din("w_in", [D, INW], F32)
    g_mix = din("norm_mix_g", [D], F32)
    g_mlp = din("norm_mlp_g", [D], F32)
    g_fin = din("norm_final_g", [D], F32)
    cmp_pe = din("cmp_pe", [32, 64], F32)
    cw1 = [din("cmp_k_w1", [2048, 256], F32), din("cmp_v_w1", [2048, 256], F32)]
    cw2 = [din("cmp_k_w2", [256, 64], F32), din("cmp_v_w2", [256, 64], F32)]
    lam_re = din("ssm_lam_re", [32, 64], F32)
    lam_im = din("ssm_lam_im", [32, 64], F32)
    log_step = din("ssm_log_step", [32], F32)
    b_re = din("ssm_b_re", [32, 64, 16], F32)
    b_im = din("ssm_b_im", [32, 64, 16], F32)
    c_re = din("ssm_c_re", [32, 16, 64], F32)
    c_im = din("ssm_c_im", [32, 16, 64], F32)
    ssm_d = din("ssm_d", [512], F32)
    w_attn = din("w_attn_branch", [512, D], F32)
    w_val = din("w_ssm_val", [512, D], F32)
    w_gate = din("w_ssm_gate", [512, D], F32)
    w_out = din("w_out", [D, D], F32)
    w_up = din("w_up", [D, 4096], F32)
    w_down = din("w_down", [4096, D], F32)
    cd = {}
    for k, v in consts.items():
        cd[k] = din("c_" + k, v.shape, BF16 if v.dtype == ml_dtypes.bfloat16 else F32)
    out = nc.dram_tensor("out", [TO, D], F32, kind="ExternalOutput").ap()

    def scratch(name, shape, dt):
        return nc.dram_tensor("s_" + name, list(shape), dt, kind="Internal").ap()

    QR = scratch("qr", [128, 4, TO], BF16)
    QP = scratch("qp", [128, 4, TO], BF16)
    KS = scratch("ks", [128, T], BF16)
    KW = scratch("kw", [128, T], BF16)
    KC = scratch("kc", [128, T], BF16)
    VC = scratch("vc", [128, T], BF16)
    VSW = scratch("vsw", [NT, 128, 2, 2, 128], BF16)
    GM = scratch("gm", [128, 16, TO], BF16)
    UT = scratch("ut", [128, 4, T], BF16)
    UTM = scratch("utm", [T, 512], BF16)
    NG = scratch("ng", [24, TO], BF16)
    YB = scratch("yb", [128, 8, TO], BF16)
    X2 = scratch("x2", [TO, D], F32)

    with ExitStack() as st:
        P = Prog(nc, st)
        ps = [st.enter_context(nc.psum_tensor(f"ps{i}", [128, 1024], F32)) for i in range(4)]
        psb = [[Buf(f"ps{i}a"), Buf(f"ps{i}b")] for i in range(4)]

        def bank(i):
            return ps[i // 2][:, (i % 2) * 512:(i % 2) * 512 + 512], psb[i // 2][i % 2]

        dq = ["sp", "sp"]
        dqi = [0]
        bq_i = [0]

        def bulkq():
            bq_i[0] += 1
            return ("sp", "pool")[bq_i[0] % 2]

        def nq():
            dqi[0] += 1
            return dq[dqi[0] % 2]

        ident = P.sb([128, 128], BF16, "ident")
        identf = P.sb([128, 128], F32, "identf")
        bconst = Buf("const")
        P.dma("sp", ident[:], cd["ident"], writes=[bconst])
        P.dma("sp", identf[:], cd["identf"], writes=[bconst])
        persist_off = P.off

        def load_w_bf16(dst, src_ap, shape, bdst, stage, bstage, conv_eng="dve"):
            n = src_ap.shape[-1]
            if len(src_ap.shape) == 2 and n > 1024:
                nch = (n + 965) // 966
                for c in range(nch):
                    lo, hi = c * 966, min(n, (c + 1) * 966)
                    P.dma(nq(), stage[:, lo:hi], src_ap[:, lo:hi], writes=[bstage])
            else:
                P.dma(nq(), stage, src_ap, writes=[bstage])
            P.cp(dst, stage, [bstage], [bdst], eng=conv_eng)

        def phase1a():
            P.off = persist_off
            win = P.sb([128, 8, INW], BF16, "win")
            bwin = Buf("win")
            stg = [P.sb([128, INW], F32, "stg") for _ in range(2)]
            bstg = [Buf("stg0"), Buf("stg1")]
            for kc in range(8):
                load_w_bf16(win[:, kc, :], w_in[kc * 128:(kc + 1) * 128, :], None, bwin, stg[kc % 2][:], bstg[kc % 2],
                            conv_eng="dve")
            gT = P.sb([128, 8], F32, "gT")
            P.dma("sp", gT[:], g_mix.rearrange("(c p) -> p c", p=128), writes=[bconst], slow=True)
            pm = P.sb([128, 128], BF16, "pm")
            P.dma("sp", pm[:], cd["pm"], writes=[bconst])
            xt = P.sb([128, 4, D], F32, "xt")
            bxt = [Buf() for _ in range(4)]
            junk = P.sb([128, D], BF16, "junk")
            bjunk = Buf()
            ssq = P.sb([128, 4], F32, "ssq")
            rstd = P.sb([128, 4], F32, "rstd")
            bss = [Buf() for _ in range(4)]
            hb4 = [P.sb([128, D], BF16, "hb") for _ in range(4)]
            bhb4 = [Buf() for _ in range(4)]
            hT2 = [P.sb([128, 8, 512], BF16, "hT") for _ in range(2)]
            bhT2 = [Buf(), Buf()]
            cur = {"hT": hT2[0], "bhT": bhT2[0]}
            ropc = P.sb([128, 512], F32, "ropc")
            rops = P.sb([128, 512], F32, "rops")
            brope = Buf()
            qr_sb = P.sb([128, 4, 512], BF16, "qr")
            qp_sb = P.sb([128, 4, 512], BF16, "qp")
            bqr, bqp = Buf(), Buf()
            kraw = P.sb([128, 512], BF16, "kraw")
            bkraw = Buf()
            kout = P.sb([128, 4, 512], BF16, "kout")
            bko = [Buf() for _ in range(4)]
            t1 = P.sb([128, 512], F32, "t1")
            t2 = P.sb([128, 512], F32, "t2")
            bt1, bt2 = Buf(), Buf()
            gm_sb = P.sb([128, 16, 512], BF16, "gm")
            bgm = Buf()
            u_sb = P.sb([128, 4, 512], BF16, "u")
            bu = Buf()
            ng_sb = P.sb([24, 512], BF16, "ng")
            bng = Buf()
            v_sb = P.sb([128, 2, 2, 128], BF16, "v")
            bv = Buf()
            P.memset(v_sb[:], 1.0, [bv])
            psT = ps[0][:, 0:512].bitcast(BF16)
            bpsT = psb[0][0]
            acc_banks = [2, 3, 4, 5]
            acc_i = [0]

            def next_acc():
                acc_i[0] += 1
                return bank(acc_banks[acc_i[0] % 4])

            def proj(cols, M):
                pa, pb = next_acc()
                for kc in range(8):
                    P.mm(pa[0:M, :], win[:, kc, cols:cols + M], cur["hT"][:, kc, :], kc == 0, kc == 7, [bwin, cur["bhT"]], [pb])
                return pa, pb

            def rope(src_sb, bsrc, dst_sb, bdst):
                pp, pbp = bank(6)
                P.mm(pp, pm[:], src_sb, True, True, [bconst, bsrc], [pbp])
                P.tt(t1[:], src_sb, ropc[:], ALU.mult, [bsrc, brope], [bt1])
                P.tt(t2[:], pp, rops[:], ALU.mult, [pbp, brope], [bt2])
                P.tt(dst_sb, t1[:], t2[:], ALU.add, [bt1, bt2], [bdst])

            psT2 = [ps[0][:, 0:512].bitcast(BF16), ps[0][:, 512:1024].bitcast(BF16)]
            bpsT2 = [psb[0][0], psb[0][1]]

            def stats(xsrc, blk):
                for tt in range(4):
                    tile = blk * 4 + tt
                    P.dma(nq(), xt[:, tt, :], xsrc[tile * 128:(tile + 1) * 128, :], writes=[bxt[tt]])
                    P.act(junk[:], xt[:, tt, :], AF.Square, [bxt[tt]], [bjunk, bss[tt]], accum=ssq[:, tt:tt + 1])
                    P.ts(rstd[:, tt:tt + 1], ssq[:, tt:tt + 1], 1.0 / D, 1e-6, ALU.mult, ALU.add, [bss[tt]], [bss[tt]])
                    P.act(rstd[:, tt:tt + 1], rstd[:, tt:tt + 1], AF.Sqrt, [bss[tt]], [bss[tt]])
                    P.op("dve", (lambda tt=tt: (lambda e: e.reciprocal(out=rstd[:, tt:tt + 1], in_=rstd[:, tt:tt + 1])))(), [bss[tt]], [bss[tt]])
                    P.act(hb4[tt][:], xt[:, tt, :], AF.Copy, [bxt[tt], bss[tt]], [bhb4[tt]], scale=rstd[:, tt:tt + 1])

            def trans(par_):
                hT_, bhT_ = hT2[par_], bhT2[par_]
                for tt in range(4):
                    pT, bpT = psT2[tt % 2], bpsT2[tt % 2]
                    for kc in range(8):
                        P.tr(pT[:, kc * 128:(kc + 1) * 128], hb4[tt][:, kc * 128:(kc + 1) * 128], ident[:], [bhb4[tt], bconst], [bpT])
                    P.tt(hT_[:, :, tt * 128:(tt + 1) * 128], pT.rearrange("p (c t) -> p c t", c=8),
                         gT[:].unsqueeze(2).to_broadcast([128, 8, 128]), ALU.mult, [bpT, bconst], [bhT_])

            def projA(blk):
                hT = cur["hT"]
                bhT = cur["bhT"]
                tok0 = blk * 512
                P.dma("sp", ropc[:], cd["ropec"][:, tok0:tok0 + 512], writes=[brope])
                P.dma("sp", rops[:], cd["ropes"][:, tok0:tok0 + 512], writes=[brope])
                for j, (kvi, dst) in enumerate([(0, KC), (1, VC)]):
                    pa, pb = proj(KVOFF + 128 * kvi, 128)
                    P.act(kout[:, 2 + j, :], pa, AF.Copy, [pb], [bko[2 + j]])
                    P.dma(nq(), dst[:, tok0:tok0 + 512], kout[:, 2 + j, :], reads=[bko[2 + j]])
                for j, (kvi, dst) in enumerate([(2, KS), (4, KW)]):
                    pa, pb = proj(KVOFF + 128 * kvi, 128)
                    P.act(kraw[:], pa, AF.Copy, [pb], [bkraw])
                    rope(kraw[:], bkraw, kout[:, j, :], bko[j])
                    P.dma(nq(), dst[:, tok0:tok0 + 512], kout[:, j, :], reads=[bko[j]])
                for tt in range(4):
                    tile = blk * 4 + tt
                    pu, pub = bank(6)
                    for kc in range(8):
                        P.mm(pu, hT[:, kc, tt * 128:(tt + 1) * 128], win[:, kc, UOFF:UOFF + 512], kc == 0, kc == 7, [bhT, bwin], [pub])
                    P.act(u_sb[:, tt, :], pu, AF.Copy, [pub], [bu])
                    P.dma(nq(), UTM[tile * 128:(tile + 1) * 128, :], u_sb[:, tt, :], reads=[bu])
                for tt in range(4):
                    tile = blk * 4 + tt
                    pv, pvb = bank(7)
                    for sw, kvi in enumerate([3, 5]):
                        for kc in range(8):
                            P.mm(pv[:, sw * 128:(sw + 1) * 128], hT[:, kc, tt * 128:(tt + 1) * 128],
                                 win[:, kc, KVOFF + 128 * kvi:KVOFF + 128 * kvi + 128], kc == 0, kc == 7, [bhT, bwin], [pvb])
                    P.cp(v_sb[:, :, :, 0:64], pv[:, 0:256].rearrange("p (s g d) -> p s g d", s=2, g=2), [pvb], [bv])
                    P.dma(nq(), VSW[tile], v_sb[:], reads=[bv])

            def projB(blk):
                tok0 = blk * 512
                P.dma("sp", ropc[:], cd["ropeco"][:, tok0:tok0 + 512], writes=[brope])
                P.dma("sp", rops[:], cd["ropeso"][:, tok0:tok0 + 512], writes=[brope])
                for m in range(4):
                    pa, pb = proj(QOFF + 128 * m, 128)
                    g = m // 2
                    hl = 2 * (m % 2)
                    P.act(qr_sb[64 * g:64 * g + 64, hl, :], pa[0:64, :], AF.Copy, [pb], [bqr])
                    P.cp(qr_sb[64 * g:64 * g + 64, hl + 1, :], pa[64:128, :], [pb], [bqr])
                for hl in range(4):
                    rope(qr_sb[:, hl, :], bqr, qp_sb[:, hl, :], bqp)
                P.dma(nq(), QR[:, :, tok0:tok0 + 512], qr_sb[:], reads=[bqr])
                P.dma(nq(), QP[:, :, tok0:tok0 + 512], qp_sb[:], reads=[bqp])
                for m in range(16):
                    pa, pb = proj(MGOFF + 128 * m, 128)
                    P.act(gm_sb[:, m, :], pa, AF.Sigmoid, [pb], [bgm])
                P.dma(nq(), GM[:, :, tok0:tok0 + 512], gm_sb[:], reads=[bgm])
                pa, pb = proj(NGOFF, 24)
                P.act(ng_sb[:], pa[0:24, :], AF.Sigmoid, [pb], [bng])
                P.dma(nq(), NG[:, tok0:tok0 + 512], ng_sb[:], reads=[bng])

            blocks = [("A", b_) for b_ in range(NT // 4)] + [("B", b_) for b_ in range(NTO // 4)]
            srcs = {"A": x, "B": xown}
            stats(srcs[blocks[0][0]], blocks[0][1])
            trans(0)
            for bi, (kind, b_) in enumerate(blocks):
                cur["hT"], cur["bhT"] = hT2[bi % 2], bhT2[bi % 2]
                if bi + 1 < len(blocks):
                    stats(srcs[blocks[bi + 1][0]], blocks[bi + 1][1])
                (projA if kind == "A" else projB)(b_)
                if bi + 1 < len(blocks):
                    trans((bi + 1) % 2)

        def phase1b():
            P.off = persist_off
            TB = 256
            NBK = T // TB
            bp = Buf("ssmparams")
            lre = P.sb([128, 16], F32)
            lim = P.sb([128, 16], F32)
            stp = P.sb([128, 16], F32)
            P.dma("sp", lre[:], lam_re.rearrange("(p gp) n -> (gp n) p", gp=2), writes=[bp], slow=True)
            P.dma("sp", lim[:], lam_im.rearrange("(p gp) n -> (gp n) p", gp=2), writes=[bp], slow=True)
            ls2 = log_step.rearrange("(p gp) -> gp p", gp=2)
            for gp in range(2):
                P.dma("sp", stp[64 * gp:64 * gp + 64, :], ls2[gp:gp + 1, :].to_broadcast([64, 16]), writes=[bp], slow=True)
            bre = P.sb([128, 16, 16], F32)
            bim = P.sb([128, 16, 16], F32)
            P.dma("sp", bre[:], b_re.rearrange("(p gp) n c -> (gp n) p c", gp=2), writes=[bp])
            P.dma("sp", bim[:], b_im.rearrange("(p gp) n c -> (gp n) p c", gp=2), writes=[bp])
            cct = P.sb([128, 2, 4, 64], F32)
            P.dma("sp", cct[:, 0, :, :], c_re.rearrange("(j gl) c n -> (gl c) j n", j=4), writes=[bp])
            P.dma("sp", cct[:, 1, :, :], c_im.rearrange("(j gl) c n -> (gl c) j n", j=4), writes=[bp])
            dsk = P.sb([128, 4], F32)
            P.dma("sp", dsk[:], ssm_d.rearrange("(j p) -> p j", p=128), writes=[bp], slow=True)
            mb = P.sb([128, 4, 8], F32)
            mc = P.sb([128, 4, 2], F32)
            P.dma("sp", mb[:], cd["mb"], writes=[bp])
            P.dma("sp", mc[:], cd["mc"], writes=[bp])
            sc = [P.sb([128, 16], F32) for _ in range(12)]
            bs = Buf("ssmscratch")
            R, W = [bp, bs], [bs]
            step, r_, th, s8, s16, pc, psn, tmpa, tmpb, cre, cim, den = sc
            P.act(step[:], stp[:], AF.Exp, [bp], W)
            P.tt(tmpa[:], step[:], lre[:], ALU.mult, R, W)
            P.act(r_[:], tmpa[:], AF.Exp, R, W)
            P.tt(th[:], step[:], lim[:], ALU.mult, R, W)
            P.act(s8[:], th[:], AF.Sin, R, W, scale=1.0 / 8)
            P.act(s16[:], th[:], AF.Sin, R, W, scale=1.0 / 16)
            P.tt(tmpa[:], s16[:], s16[:], ALU.mult, R, W)
            P.ts(pc[:], tmpa[:], -2.0, 1.0, ALU.mult, ALU.add, R, W)
            P.cp(psn[:], s8[:], R, W)
            pw = P.sb([128, 11, 2, 16], F32)
            bpw = Buf("pw")

            def csq(oc, os_, ic, is_):
                P.tt(tmpa[:], ic, ic, ALU.mult, R + [bpw], W)
                P.tt(tmpb[:], is_, is_, ALU.mult, R + [bpw], W)
                P.tt(cim[:], ic, is_, ALU.mult, R + [bpw], W)
                P.tt(oc, tmpa[:], tmpb[:], ALU.subtract, R + [bpw], W + [bpw])
                P.ts(os_, cim[:], 2.0, None, ALU.mult, None, R + [bpw], W + [bpw])
            csq(cre[:], den[:], pc[:], psn[:])
            csq(pc[:], psn[:], cre[:], den[:])
            csq(pw[:, 0, 0, :], pw[:, 0, 1, :], pc[:], psn[:])
            for j in range(1, 11):
                csq(pw[:, j, 0, :], pw[:, j, 1, :], pw[:, j - 1, 0, :], pw[:, j - 1, 1, :])
            lmr = P.sb([128, 16], F32)
            lmi = P.sb([128, 16], F32)
            P.tt(lmr[:], r_[:], pw[:, 0, 0, :], ALU.mult, R + [bpw], W)
            P.tt(lmi[:], r_[:], pw[:, 0, 1, :], ALU.mult, R + [bpw], W)
            P.ts(tmpa[:], lmr[:], -1.0, None, ALU.add, None, R, W)
            P.tt(den[:], lre[:], lre[:], ALU.mult, R, W)
            P.tt(tmpb[:], lim[:], lim[:], ALU.mult, R, W)
            P.tt(den[:], den[:], tmpb[:], ALU.add, R, W)
            P.op("dve", lambda e: e.reciprocal(out=den[:], in_=den[:]), R, W)
            P.tt(cre[:], tmpa[:], lre[:], ALU.mult, R, W)
            P.tt(tmpb[:], lmi[:], lim[:], ALU.mult, R, W)
            P.tt(cre[:], cre[:], tmpb[:], ALU.add, R, W)
            P.tt(cre[:], cre[:], den[:], ALU.mult, R, W)
            P.tt(cim[:], lmi[:], lre[:], ALU.mult, R, W)
            P.tt(tmpb[:], tmpa[:], lim[:], ALU.mult, R, W)
            P.tt(cim[:], cim[:], tmpb[:], ALU.subtract, R, W)
            P.tt(cim[:], cim[:], den[:], ALU.mult, R, W)
            bbr = P.sb([128, 16, 16], F32)
            bbi = P.sb([128, 16, 16], F32)
            tb3 = P.sb([128, 16, 16], F32)
            creb = cre[:].unsqueeze(2).to_broadcast([128, 16, 16])
            cimb = cim[:].unsqueeze(2).to_broadcast([128, 16, 16])
            P.tt(bbr[:], bre[:], creb, ALU.mult, R, W)
            P.tt(tb3[:], bim[:], cimb, ALU.mult, R, W)
            P.tt(bbr[:], bbr[:], tb3[:], ALU.subtract, R, W)
            P.tt(bbi[:], bim[:], creb, ALU.mult, R, W)
            P.tt(tb3[:], bre[:], cimb, ALU.mult, R, W)
            P.tt(bbi[:], bbi[:], tb3[:], ALU.add, R, W)
            TBc = 128
            rowsel = P.sb([128, 4], F32)
            cmask8 = P.sb([128, 128], F32)
            selc = P.sb([128, 64], BF16)
            dvec8 = P.sb([128, 32], F32)
            P.dma("sp", rowsel[:], cd["rowsel"], writes=[bp])
            P.dma("sp", cmask8[:], cd["cmask8"], writes=[bp])
            P.dma("sp", selc[:], cd["selc"], writes=[bp])
            dsrc = ssm_d.rearrange("(g c) -> c g", c=16)
            for t_ in range(8):
                P.dma("sp", dvec8[16 * t_:16 * t_ + 16, :], dsrc, writes=[bp], slow=True)
            Bpad8 = P.sb([128, 16, 2, 2, 128], BF16)
            CgZ = P.sb([128, 16, 2, 2, 128], BF16)
            Tg = P.sb([128, 32, 128], BF16)
            Ct = P.sb([128, 16, TBc], F32)
            Dt = P.sb([128, 16, TBc], F32)
            r8 = P.sb([128, 16], F32)
            bBC = Buf("BC")
            btab = Buf("tab")
            wv = P.sb([128, 4, D], BF16)
            wg = P.sb([128, 4, D], BF16)
            bwv = Buf("wv")
            mark_tmp = P.off
            lp = P.sb([128, 9, 2, 16], F32)
            blp = Buf("lp")
            RL, WL = [bp, bs, bpw, blp], [blp, bs]
            P.memset(lp[:, 0, 0, :], 1.0, [blp])
            P.memset(lp[:, 0, 1, :], 0.0, [blp])
            for m_ in range(8):
                a_r, a_i = lp[:, m_, 0, :], lp[:, m_, 1, :]
                P.tt(tmpa[:], a_r, lmr[:], ALU.mult, RL, WL)
                P.tt(tmpb[:], a_i, lmi[:], ALU.mult, RL, WL)
                P.tt(lp[:, m_ + 1, 0, :], tmpa[:], tmpb[:], ALU.subtract, RL, WL)
                P.tt(tmpa[:], a_r, lmi[:], ALU.mult, RL, WL)
                P.tt(tmpb[:], a_i, lmr[:], ALU.mult, RL, WL)
                P.tt(lp[:, m_ + 1, 1, :], tmpa[:], tmpb[:], ALU.add, RL, WL)
            inv8 = P.sb([128, 2, 16], F32)
            P.tt(tmpa[:], r_[:], r_[:], ALU.mult, RL, WL)
            P.tt(tmpa[:], tmpa[:], tmpa[:], ALU.mult, RL, WL)
            P.tt(r8[:], tmpa[:], tmpa[:], ALU.mult, RL, WL + [btab])
            P.tt(tmpb[:], r8[:], r8[:], ALU.mult, RL + [btab], WL)
            P.op("dve", lambda e: e.reciprocal(out=tmpb[:], in_=tmpb[:]), RL, WL)
            P.tt(inv8[:, 0, :], lp[:, 8, 0, :], tmpb[:], ALU.mult, RL, WL)
            P.tt(tmpa[:], lp[:, 8, 1, :], tmpb[:], ALU.mult, RL, WL)
            P.ts(inv8[:, 1, :], tmpa[:], -1.0, None, ALU.mult, None, RL, WL)
            ccTp = P.sb([128, 16, 2, 16], F32)
            bcc = Buf("ccTp")
            pz, pzb = bank(2)
            for ri in range(2):
                for j in range(4):
                    P.tr(pz[0:64, 0:128], cct[:, ri, j, :], identf[:], [bp, bconst], [pzb])
                    for gp in range(2):
                        P.cp(ccTp[64 * gp:64 * gp + 64, 4 * j:4 * j + 4, ri, :],
                             pz[0:64, 0:128].rearrange("n (q g c) -> n q g c", q=4, g=2)[:, :, gp, :], [pzb], [bcc])
            B8 = P.sb([128, 16, 2, 8, 16], F32)
            C8 = P.sb([128, 16, 2, 8, 16], F32)
            CQ = P.sb([128, 16, 2, 8, 16], F32)
            tb3b = P.sb([128, 16, 16], F32)
            b8 = Buf("B8")
            R8, W8 = [bp, bs, blp, bcc, b8], [b8]

            def cmul_bc(out_r, out_i, a_r, a_i, s_r, s_i, shape):
                sr = s_r.unsqueeze(2).to_broadcast(shape)
                si = s_i.unsqueeze(2).to_broadcast(shape)
                P.tt(out_r, a_r, sr, ALU.mult, R8, W8)
                P.tt(tb3[:], a_i, si, ALU.mult, R8, W8)
                P.tt(out_r, out_r, tb3[:], ALU.subtract, R8, W8)
                P.tt(out_i, a_i, sr, ALU.mult, R8, W8)
                P.tt(tb3[:], a_r, si, ALU.mult, R8, W8)
                P.tt(out_i, out_i, tb3[:], ALU.add, R8, W8)
            lpr = P.sb([128, 8, 2, 16], F32)
            for t_ in range(8):
                P.cp(lpr[:, t_, :, :], lp[:, 7 - t_, :, :], R8, W8)
            tb4 = P.sb([128, 16, 8, 16], F32)
            SH4 = [128, 16, 8, 16]

            def cmul4(out_r, out_i, a_r, a_i, s_r, s_i):
                P.tt(out_r, a_r, s_r, ALU.mult, R8, W8)
                P.tt(tb4[:], a_i, s_i, ALU.mult, R8, W8)
                P.tt(out_r, out_r, tb4[:], ALU.subtract, R8, W8)
                P.tt(out_i, a_i, s_r, ALU.mult, R8, W8)
                P.tt(tb4[:], a_r, s_i, ALU.mult, R8, W8)
                P.tt(out_i, out_i, tb4[:], ALU.add, R8, W8)

            def bc_t(a3):
                return a3.unsqueeze(2).to_broadcast(SH4)

            def bc_c(tab):
                return tab.rearrange("q t p -> q p t").unsqueeze(3).to_broadcast(SH4)
            cmul4(B8[:, :, 0], B8[:, :, 1], bc_t(bbr[:]), bc_t(bbi[:]), bc_c(lpr[:, :, 0, :]), bc_c(lpr[:, :, 1, :]))
            cmul4(C8[:, :, 0], C8[:, :, 1], bc_t(ccTp[:, :, 0, :]), bc_t(ccTp[:, :, 1, :]),
                  bc_c(lp[:, 1:9, 0, :]), bc_c(lp[:, 1:9, 1, :]))
            i8r = inv8[:, 0, :].unsqueeze(2).unsqueeze(3).to_broadcast(SH4)
            i8i = inv8[:, 1, :].unsqueeze(2).unsqueeze(3).to_broadcast(SH4)
            cmul4(CQ[:, :, 0], CQ[:, :, 1], C8[:, :, 0], C8[:, :, 1], i8r, i8i)
            P.ts(CQ[:, :, 1, :, :], CQ[:, :, 1, :, :], -1.0, None, ALU.mult, None, R8, W8)
            Zh = P.sb([128, 16, 2, 128], F32)
            bZh = Buf("Zh")
            for ri in range(2):
                for gp in range(2):
                    P.ts(Zh[:, :, gp, :], B8[:, :, ri].rearrange("q p t c -> q p (t c)"), rowsel[:, gp:gp + 1], None, ALU.mult, None,
                         R8, [bZh])
                for p4 in range(8):
                    pzq, pzqb = bank(2 + p4 % 2)
                    for k_ in range(4):
                        idx = p4 * 4 + k_
                        P.tr(pzq[:, k_ * 128:(k_ + 1) * 128], Zh[:, idx // 2, idx % 2, :], identf[:], [bZh, bconst], [pzqb])
                    P.act(Bpad8[:, 2 * p4:2 * p4 + 2, :, ri, :], pzq.rearrange("q (a b c) -> q a b c", a=2, b=2), AF.Copy, [pzqb], [bBC])
                for gp in range(2):
                    sc_ = rowsel[:, gp:gp + 1] if ri == 0 else rowsel[:, 2 + gp:3 + gp]
                    P.ts(CgZ[:, :, gp, ri, :], C8[:, :, ri].rearrange("q p t c -> q p (t c)"), sc_, None, ALU.mult, None, R8, [bBC])
            tmpT = P.sb([128, 128], F32)
            btT = Buf("tmpT")
            pt_, ptb = bank(3)
            for g_ in range(32):
                p, gp = g_ // 2, g_ % 2
                rows = slice(64 * gp, 64 * gp + 64)
                for ri in range(2):
                    P.mm(pt_[:, 0:128], B8[rows, p, ri].rearrange("p t c -> p (t c)"), CQ[rows, p, ri].rearrange("p t c -> p (t c)"),
                         ri == 0, ri == 1, [b8], [ptb])
                P.tt(tmpT[:], pt_[:, 0:128], cmask8[:], ALU.mult, [ptb, bp], [btT])
                P.stt(Tg[:, g_, :], identf[:], dvec8[:, g_:g_ + 1], tmpT[:], ALU.mult, ALU.add, [bconst, bp, btT], [bBC])
            tA = P.sb([128, 16, TBc // 2], F32)
            tBb = P.sb([128, 16, TBc // 2], F32)
            P.memset(Ct[:, :, 0:1], 1.0, [btab])
            P.memset(Dt[:, :, 0:1], 0.0, [btab])
            RT, WT = [bp, bs, bpw, btab], [btab]
            for j in range(7):
                m = 1 << j
                cj = pw[:, j + 3, 0, :].unsqueeze(2).to_broadcast([128, 16, m])
                sj = pw[:, j + 3, 1, :].unsqueeze(2).to_broadcast([128, 16, m])
                P.tt(tA[:, :, 0:m], Ct[:, :, 0:m], cj, ALU.mult, RT, WT)
                P.tt(tBb[:, :, 0:m], Dt[:, :, 0:m], sj, ALU.mult, RT, WT)
                P.tt(Ct[:, :, m:2 * m], tA[:, :, 0:m], tBb[:, :, 0:m], ALU.subtract, RT, WT)
                P.tt(tA[:, :, 0:m], Dt[:, :, 0:m], cj, ALU.mult, RT, WT)
                P.tt(tBb[:, :, 0:m], Ct[:, :, 0:m], sj, ALU.mult, RT, WT)
                P.tt(Dt[:, :, m:2 * m], tA[:, :, 0:m], tBb[:, :, 0:m], ALU.add, RT, WT)
            stg = [P.sb([128, D], F32) for _ in range(2)]
            bstg = [Buf(), Buf()]
            k = 0
            for dst, src in ((wv, w_val), (wg, w_gate)):
                for j in range(4):
                    load_w_bf16(dst[:, j, :], src[j * 128:(j + 1) * 128, :], None, bwv, stg[k % 2][:], bstg[k % 2], conv_eng="dve")
                    k += 1
            P.barrier()
            P.off = mark_tmp
            Uk = [P.sb([128, 8, 512], BF16) for _ in range(2)]
            bUk = [Buf(), Buf()]
            U8d = [P.sb([128, 32, 128], BF16) for _ in range(2)]
            bU8d = [Buf("U8a"), Buf("U8b")]
            Uk2 = P.sb([128, 32, 128], BF16)
            bUk2 = Buf("Uk2")
            gmb = [P.sb([128, 8, 512], BF16) for _ in range(2)]
            bgmb = [Buf(), Buf()]
            lanes = []
            blw = []
            blinit = []
            for L_ in range(2):
                lanes.append((P.sb([128, TBc], F32), P.sb([128, TBc], F32), P.sb([128, TBc], F32), P.sb([128, TBc], F32),
                              P.sb([128, 2, TBc], F32), P.sb([128, 2, TBc], F32), P.sb([128, 2], F32), P.sb([128, 2], F32)))
                blw.append([Buf() for _ in range(6)])
                blinit.append(Buf())
            Hb2 = [[P.sb([128, 2, TBc + 2], BF16) for _ in range(16)] for _ in range(2)]
            bH2 = [[Buf() for _ in range(16)] for _ in range(2)]
            for q_ in range(2):
                for p in range(16):
                    P.memset(Hb2[q_][p][:], 0.0, [bH2[q_][p]])
            glast = P.sb([128, 16, 2], F32)
            bgl = [Buf() for _ in range(16)]
            P.memset(glast[:], 0.0, bgl)
            init = P.sb([128, 2], F32)
            binit = Buf()
            ti = P.sb([128, 2], F32)
            Ytok = P.sb([128, 8, 512], BF16)
            bY = Buf("Ytok")
            gy = P.sb([128, 4, 512], BF16)
            bgy = Buf()
            sgt = P.sb([128, 512], F32)
            bsg = Buf()
            ybt = P.sb([128, 512], F32)
            bybt = Buf()
            yb_sb = [P.sb([128, 8, 512], BF16) for _ in range(2)]
            byb = [Buf(), Buf()]
            psT = ps[0][:, 0:512].bitcast(BF16)
            bpsT = psb[0][0]
            NBK = T // (8 * TBc)

            def prep(blk):
                ub, bub = Uk[blk % 2], bUk[blk % 2]
                U8_, bU8_ = U8d[blk % 2], bU8d[blk % 2]
                P.dma("sp", ub[:].rearrange("p t c -> p (t c)"),
                      UTM[blk * 1024:(blk + 1) * 1024, :].rearrange("(k t) c -> k (t c)", t=8), writes=[bub])
                for hh in range(2):
                    P.act(Uk2[:, 16 * hh:16 * hh + 16, :].rearrange("p g (t c) -> p g t c", t=8),
                          ub[:, :, 256 * hh:256 * hh + 256].rearrange("p t (g c) -> p g t c", g=16), AF.Copy, [bub], [bUk2])
                for gb in range(4):
                    for gq in range(8):
                        g_ = gb * 8 + gq
                        P.tr(psT[:, gq * 128:(gq + 1) * 128], Uk2[:, g_, :], ident[:], [bUk2, bconst], [bpsT])
                    P.act(U8_[:, gb * 8:(gb + 1) * 8, :], psT.rearrange("p (g k) -> p g k", g=8), AF.Copy, [bpsT], [bU8_])

            def pair_steps(blk, p, L):
                U8_, bU8_ = U8d[blk % 2], bU8d[blk % 2]
                Hn, bHn = Hb2[blk % 2][p], bH2[blk % 2][p]
                Ho, bHo = Hb2[(blk + 1) % 2][p], bH2[(blk + 1) % 2][p]
                w1_, w2_, w3_, w4_, gin_, G_, init_, ti_ = lanes[L]
                bwL, binitL = blw[L], blinit[L]
                st = []
                sp_, spb = bank(1 + (p % 2))
                S = sp_.rearrange("p (a t) -> p a t", a=2)
                C_, D_ = Ct[:, p, :], Dt[:, p, :]
                ec, es = pw[:, 10, 0, p:p + 1], pw[:, 10, 1, p:p + 1]
                rb = r8[:, p:p + 1].to_broadcast([128, TBc])

                def s0():
                    for ri in range(2):
                        for gp in range(2):
                            P.mm(S[:, ri, 0:TBc], Bpad8[:, p, gp, ri, :], U8_[:, 2 * p + gp, :], gp == 0 and ri == 0, gp == 1, [bBC, bU8_], [spb])
                st.append(s0)
                st.append(lambda: P.tt(w1_[:], S[:, 0, 0:TBc], C_, ALU.mult, [spb, btab], [bwL[0]]))
                st.append(lambda: P.tt(w2_[:], S[:, 1, 0:TBc], D_, ALU.mult, [spb, btab], [bwL[1]]))
                st.append(lambda: P.tt(gin_[:, 0, :], w1_[:], w2_[:], ALU.add, [bwL[0], bwL[1]], [bwL[4]]))
                st.append(lambda: P.tt(w3_[:], S[:, 1, 0:TBc], C_, ALU.mult, [spb, btab], [bwL[2]]))
                st.append(lambda: P.tt(w4_[:], S[:, 0, 0:TBc], D_, ALU.mult, [spb, btab], [bwL[3]]))
                st.append(lambda: P.tt(gin_[:, 1, :], w3_[:], w4_[:], ALU.subtract, [bwL[2], bwL[3]], [bwL[4]]))
                st.append(lambda: P.ts(ti_[:, 0:1], glast[:, p, 1:2], es, None, ALU.mult, None, [bgl[p], bpw], [binitL]))
                st.append(lambda: P.stt(init_[:, 0:1], glast[:, p, 0:1], ec, ti_[:, 0:1], ALU.mult, ALU.subtract, [bgl[p], bpw, binitL], [binitL]))
                st.append(lambda: P.ts(ti_[:, 1:2], glast[:, p, 0:1], es, None, ALU.mult, None, [bgl[p], bpw], [binitL]))
                st.append(lambda: P.stt(init_[:, 1:2], glast[:, p, 1:2], ec, ti_[:, 1:2], ALU.mult, ALU.add, [bgl[p], bpw, binitL], [binitL]))
                for a in range(2):
                    st.append((lambda a=a: (lambda: P.op("dve", lambda e: e.tensor_tensor_scan(
                        out=G_[:, a, :], data0=rb, data1=gin_[:, a, :], initial=init_[:, a:a + 1],
                        op0=ALU.mult, op1=ALU.add), [bwL[4], binitL, btab], [bwL[5]])))())
                st.append(lambda: P.cp(glast[:, p, :], G_[:, :, TBc - 1], [bwL[5]], [bgl[p]]))
                st.append(lambda: P.cp(Hn[:, :, 0:1], Ho[:, :, TBc:TBc + 1], [bHo], [bHn]))
                st.append(lambda: P.tt(w1_[:], G_[:, 0, :], C_, ALU.mult, [bwL[5], btab], [bwL[0]]))
                st.append(lambda: P.tt(w2_[:], G_[:, 1, :], D_, ALU.mult, [bwL[5], btab], [bwL[1]]))
                st.append(lambda: P.tt(Hn[:, 0, 1:TBc + 1], w1_[:], w2_[:], ALU.subtract, [bwL[0], bwL[1]], [bHn]))
                st.append(lambda: P.tt(w3_[:], G_[:, 0, :], D_, ALU.mult, [bwL[5], btab], [bwL[2]]))
                st.append(lambda: P.tt(w4_[:], G_[:, 1, :], C_, ALU.mult, [bwL[5], btab], [bwL[3]]))
                st.append(lambda: P.tt(Hn[:, 1, 1:TBc + 1], w3_[:], w4_[:], ALU.add, [bwL[2], bwL[3]], [bHn]))
                return st

            def out_slices(blk):
                U8_, bU8_ = U8d[blk % 2], bU8d[blk % 2]
                Hs, bHs = Hb2[blk % 2], bH2[blk % 2]
                ybs = yb_sb[blk % 2]

                def y_part(gb):
                    py, pyb = bank(3 + gb % 2)
                    for gq in range(4):
                        g_ = gb * 4 + gq
                        p, gp = g_ // 2, g_ % 2
                        o_ = py[:, gq * 128:(gq + 1) * 128]
                        P.mm(o_, U8_[:, g_, :], Tg[:, g_, :], gq == 0, False, [bU8_, bBC], [pyb])
                        P.mm(o_, Hs[p][:, 0, 0:TBc], CgZ[:, p, gp, 0, :], False, False, [bHs[p], bBC], [pyb])
                        P.mm(o_, Hs[p][:, 1, 0:TBc], CgZ[:, p, gp, 1, :], False, gq == 3, [bHs[p], bBC], [pyb])
                    P.act(Ytok[:, :, gb * 64:(gb + 1) * 64].rearrange("p t (g c) -> p t g c", g=4),
                          py.rearrange("p (g t c) -> p t g c", g=4, t=8), AF.Copy, [pyb], [bY])

                def sel_part(cb):
                    pyt, pytb = bank(5)
                    for t_ in range(8):
                        P.mm(pyt[:, t_ * 64:(t_ + 1) * 64], Ytok[:, t_, cb * 128:(cb + 1) * 128], selc[:], t_ == 0, t_ == 7, [bY, bp], [pytb])
                    P.act(gy[:, cb, :].rearrange("p (j t) -> p j t", t=8), pyt.rearrange("p (t j) -> p j t", t=8),
                          AF.Gelu_apprx_tanh, [pytb], [bgy])

                def glu_part(oc):
                    pv_, pvb = bank(6)
                    pg_, pgb = bank(7)
                    for j in range(4):
                        P.mm(pv_, wv[:, j, oc * 128:(oc + 1) * 128], gy[:, j, :], j == 0, j == 3, [bwv, bgy], [pvb])
                    for j in range(4):
                        P.mm(pg_, wg[:, j, oc * 128:(oc + 1) * 128], gy[:, j, :], j == 0, j == 3, [bwv, bgy], [pgb])
                    P.act(sgt[:], pg_, AF.Sigmoid, [pgb], [bsg])
                    P.tt(ybt[:], pv_, sgt[:], ALU.mult, [pvb, bsg], [bybt])
                    P.tt(ybs[:, oc, :], ybt[:], gmb[blk % 2][:, oc, :], ALU.mult, [bybt, bgmb[blk % 2]], [byb[blk % 2]])

                def fin():
                    P.dma(nq(), YB[:, :, blk * 512:(blk + 1) * 512], ybs[:], reads=[byb[blk % 2]])
                sl = [[lambda gb=gb: y_part(gb) for gb in (2 * k_, 2 * k_ + 1)] for k_ in range(4)]
                sl.append([lambda cb=cb: sel_part(cb) for cb in (0, 1)])
                sl.append([lambda cb=cb: sel_part(cb) for cb in (2, 3)])
                sl.append([lambda oc=oc: glu_part(oc) for oc in range(4)])
                sl.append([lambda oc=oc: glu_part(oc) for oc in range(4, 8)] + [fin])
                return sl

            prep(0)
            for blk in range(NBK + 1):
                if blk >= 1:
                    ob = blk - 1
                    P.dma("sp", gmb[ob % 2][:], GM[:, 8:16, ob * 512:(ob + 1) * 512], writes=[bgmb[ob % 2]])
                osl = out_slices(blk - 1) if blk >= 1 else [[] for _ in range(8)]
                for pp in range(8):
                    if blk < NBK:
                        sa, sb_ = pair_steps(blk, 2 * pp, 0), pair_steps(blk, 2 * pp + 1, 1)
                        for fa, fb in zip(sa, sb_):
                            fa()
                            fb()
                    for fn_ in osl[pp]:
                        fn_()
                    if pp == 3 and blk + 1 < NBK:
                        prep(blk + 1)

        def phase2():
            P.off = persist_off
            bc2 = Buf("c2")
            kcT = P.sb([128, 512], BF16, "kcT")
            vca = P.sb([128, 4, 2, 128], BF16, "vca")
            bkc = Buf("kcT")
            bvca = Buf("vca")
            P.memset(vca[:], 1.0, [bvca])
            P.memset(kcT[:], 0.0, [bkc])
            ksT = P.sb([128, T], BF16, "ksT")
            kwT = P.sb([128, T], BF16, "kwT")
            vsw = P.sb([128, NT, 2, 2, 128], BF16, "vsw")
            we = P.sb([128, T], BF16, "we")
            bK, bV = Buf("K"), Buf("V")
            mark = P.off
            w1b = P.sb([128, 32, 256], BF16, "w1b")
            w2b = P.sb([128, 2, 64], BF16, "w2b")
            peT = P.sb([64, 32], F32, "peT")
            peTb = P.sb([64, 32], BF16, "peTb")
            P.dma("sp", peT[:], cmp_pe.rearrange("j d -> d j"), writes=[bc2], slow=True)
            P.cp(peTb[:], peT[:], [bc2], [bc2])
            raw = P.sb([128, T + 32], BF16, "raw")
            braw = Buf("raw")
            stg = [P.sb([128, 8, 256], F32, "stgc") for _ in range(2)]
            bstg = [Buf(), Buf()]
            stg2 = P.sb([128, 2, 64], F32, "stg2")
            bw1 = Buf("w1")
            hid = P.sb([128, 2, 512], BF16, "hid")
            bhid = Buf("hid")
            hbias = P.sb([128, 2], F32, "hbias")
            bhb_ = Buf("hbias")
            P.memset(raw[:, T:T + 32], 0.0, [braw])

            def resident_loads():
                P.dma("sp", ksT[:], KS, writes=[bK])
                P.dma("sp", kwT[:], KW, writes=[bK])
                for c4 in range(8):
                    P.dma(nq(), vsw[:, c4 * 8:(c4 + 1) * 8], VSW[c4 * 8:(c4 + 1) * 8].rearrange("t p s g d -> p t s g d"), writes=[bV])
                P.dma("sp", we[:], cd["we"], writes=[bc2])
            for kv in range(2):
                for jq in range(4):
                    for half in range(2):
                        P.dma(nq(), stg[jq % 2][64 * half:64 * half + 64, :, :],
                              cw1[kv][jq * 512:(jq + 1) * 512, :].rearrange("(j d) h -> d j h", d=64), writes=[bstg[jq % 2]])
                    P.cp(w1b[:, jq * 8:(jq + 1) * 8, :], stg[jq % 2][:], [bstg[jq % 2]], [bw1], eng="dve")
                P.dma("sp", stg2[:], cw2[kv].rearrange("(a p) d -> p a d", p=128), writes=[bc2])
                P.cp(w2b[:], stg2[:], [bc2], [bw1])
                P.dma("sp", raw[:, 0:T // 2], (KC if kv == 0 else VC)[:, 0:T // 2], writes=[braw])
                P.dma("sp", raw[:, T // 2:T], (KC if kv == 0 else VC)[:, T // 2:T], writes=[braw])
                if kv == 0:
                    resident_loads()
                pbi, pbib = bank(7)
                for hh in range(2):
                    for j in range(32):
                        P.mm(pbi[:, hh:hh + 1], w1b[0:64, j, hh * 128:(hh + 1) * 128], peTb[:, j:j + 1], j == 0, j == 31, [bw1, bc2], [pbib])
                P.cp(hbias[:], pbi[:, 0:2], [pbib], [bhb_])
                for g in range(2):
                    rows = slice(64 * g, 64 * g + 64)
                    for hh in range(2):
                        ph, phb = bank(2 + hh)
                        for j in range(32):
                            rhs = raw[rows, j:j + 16 * 512].rearrange("p (n s) -> p n s", s=16)[:, :, 0]
                            P.mm(ph, w1b[rows, j, hh * 128:(hh + 1) * 128], rhs, j == 0, j == 31, [bw1, braw], [phb])
                        P.act(hid[:, hh, :], ph, AF.Gelu_apprx_tanh, [phb, bhb_], [bhid], bias=hbias[:, hh:hh + 1])
                    if kv == 0:
                        po, pob = bank(4)
                        for hh in range(2):
                            P.mm(po[0:64, :], w2b[:, hh, :], hid[:, hh, :], hh == 0, hh == 1, [bw1, bhid], [pob])
                        P.cp(kcT[rows, 0:511], po[0:64, 0:511], [pob], [bkc])
                    else:
                        for nt in range(4):
                            po, pob = bank(4 + nt % 2)
                            for hh in range(2):
                                P.mm(po[:, 0:64], hid[:, hh, nt * 128:(nt + 1) * 128], w2b[:, hh, :], hh == 0, hh == 1, [bhid, bw1], [pob])
                            P.cp(vca[:, nt, g, 0:64], po[:, 0:64], [pob], [bvca])
            P.barrier()
            P.off = mark
            i4 = P.sb([128, 512], BF16, "i4")
            wmask = P.sb([128, 4, 128], BF16, "wmask")
            smask = P.sb([128, 2, 128], BF16, "smask")
            cbase = P.sb([128, 288], BF16, "cbase")
            ov = P.sb([128, 4, 128], BF16, "ov")
            wf = P.sb([128, 256], F32, "wf")
            ones = P.sb([128, 1], BF16, "ones")
            for dst, nm in ((i4, "i4"), (wmask, "wmask"), (smask, "smask"), (cbase, "cbase"), (ov, "ov"), (wf, "wf")):
                P.dma(nq(), dst[:], cd[nm], writes=[bc2])
            P.memset(ones[:], 1.0, [bc2])
            wat = P.sb([128, 4, D], BF16, "wat")
            wo = P.sb([128, 8, D], BF16, "wo")
            bwat = Buf("wat")
            mark2 = P.off
            stg = [P.sb([128, 4, D], F32, "stga") for _ in range(2)]
            bstg = [Buf(), Buf()]
            for g in range(2):
                P.dma(nq(), stg[0][64 * g:64 * g + 64, :, :],
                      w_attn[g * 256:(g + 1) * 256, :].rearrange("(hl d) o -> d hl o", d=64), writes=[bstg[0]])
            P.cp(wat[:], stg[0][:], [bstg[0]], [bwat], eng="dve")
            for hf in range(2):
                P.dma(nq(), stg[1 - hf][:], w_out[hf * 512:(hf + 1) * 512, :].rearrange("(c p) o -> p c o", p=128), writes=[bstg[1 - hf]])
                P.cp(wo[:, hf * 4:(hf + 1) * 4, :], stg[1 - hf][:], [bstg[1 - hf]], [bwat], eng="dve")
            P.barrier()
            P.off = mark2
            qr_t = [[P.sb([128, 4, 128], BF16, "qrt") for _g in range(2)] for _ in range(2)]
            qp_t = [[P.sb([128, 4, 128], BF16, "qpt") for _g in range(2)] for _ in range(2)]
            gbc_t = [P.sb([64, 2, 3, 4, 128], BF16, "gbc") for _ in range(2)]
            gm_t1 = P.sb([128, 8, 128], BF16, "gmt")
            yb_t1 = P.sb([128, 8, 128], BF16, "ybt")
            x_t1 = P.sb([128, D], F32, "xt2")
            gm_t, yb_t, x_t = [gm_t1] * 2, [yb_t1] * 2, [x_t1] * 2
            bq = [Buf(), Buf()]
            bq21 = Buf()
            bq2 = [bq21, bq21]
            NPT = 4
            PT = [P.sb([128, 512], BF16, "PT") for _ in range(NPT)]
            bPT = [Buf() for _ in range(NPT)]
            rdq = P.sb([128, 4], F32, "rdq")
            brdq = Buf()
            imp = P.sb([128, 128], F32, "imp")
            sc2 = P.sb([128, 128], F32, "sc2")
            m8 = P.sb([128, 16], F32, "m8")
            bimp = Buf()
            selb = P.sb([128, 128], BF16, "selb")
            bselb = Buf()
            selbT = [P.sb([128, 4, 128], BF16, "selbT") for _ in range(2)]
            selbs = [P.sb([128, 128], BF16, "selbs") for _ in range(2)]
            bselbs = [Buf(), Buf()]
            bselbT = [Buf(), Buf()]
            off_rden = P.off
            rden = P.sb([64, 512], F32, "rden")
            coef = P.sb([64, 512], F32, "coef")
            ctb = P.sb([64, 512], F32, "ctb")
            lnd = P.sb([64, 512], F32, "lnd")
            blnd = Buf()
            off_acc = P.off
            accT = [P.sb([64, 512], F32, "accT") for _ in range(2)]
            bacc = [Buf(), Buf()]
            bcomb = Buf()
            nsaT = P.sb([128, 4, 128], BF16, "nsaT")
            bnsa = Buf()
            m1 = P.sb([128, 8, 128], F32, "m1")
            bm1 = Buf()
            mT = P.sb([128, 8, 128], BF16, "mT")
            bmT = Buf()
            x2s = P.sb([128, D], F32, "x2s")
            bx2 = Buf()
            NGr = NG.rearrange("(g hl br) t -> g br hl t", g=2, hl=4, br=3)

            def loads(i):
                b2 = i % 2
                t0 = i * 128
                for g in range(2):
                    rw = slice(64 * g, 64 * g + 64)
                    P.dma("sp", qr_t[b2][g][rw], QR[rw, :, t0:t0 + 128], writes=[bq[b2]])
                    P.dma("sp", qp_t[b2][g][rw], QP[rw, :, t0:t0 + 128], writes=[bq[b2]])
                for g in range(2):
                    for br in range(3):
                        P.dma("sp", gbc_t[b2][:, g, br], NGr[g, br:br + 1, :, t0:t0 + 128].to_broadcast([64, 4, 128]),
                              writes=[bq[b2]], slow=True)

            def loads_e(i):
                b2 = i % 2
                t0 = i * 128
                P.dma("sp", gm_t[b2][:], GM[:, 0:8, t0:t0 + 128], writes=[bq2[b2]])
                P.dma("sp", yb_t[b2][:], YB[:, :, t0:t0 + 128], writes=[bq2[b2]])
                P.dma("sp", x_t[b2][:], xown[t0:t0 + 128, :], writes=[bq2[b2]])

            jobs = []
            oacc_banks = [2, 3, 7]
            oacc_i = [0]
            for i in range(NTO):
                nkc = (8 * (2 * i + 1) + 6) // 128 + 1
                k0 = max(0, 2 * i - 4)
                kmap = {0: list(range(nkc)), 2: list(range(k0, 2 * i + 2)), 1: list(range(2 * i + 2))}
                for br, g in ((0, 0), (2, 0), (0, 1), (2, 1), (1, 0), (1, 1)):
                    kts = kmap[br]
                    if True:
                        oacc_i[0] += 1
                        ob = oacc_banks[oacc_i[0] % 3]
                        for n_, kt in enumerate(kts):
                            jobs.append(dict(i=i, g=g, br=br, kt=kt, first=(n_ == 0), last=(n_ == len(kts) - 1), ob=ob,
                                             tile_first=(br == 0 and g == 0 and n_ == 0),
                                             tile_last=(br == 1 and g == 1 and n_ == len(kts) - 1)))
            sbi = [0]
            pti = [0]

            def score(J):
                i, g, br, kt = J["i"], J["g"], J["br"], J["kt"]
                b2 = i % 2
                rows = slice(64 * g, 64 * g + 64)
                masks = []
                if br == 0:
                    lhsT, rk, q_ap = kcT[:, kt * 128:(kt + 1) * 128], [bkc], qr_t[b2][g][:, :, :]
                    Dv = 16 * i - 128 * kt
                    if Dv < 130:
                        s0 = 136 - Dv
                        masks.append((cbase[:, s0:s0 + 128], i4[:], [bc2]))
                elif br == 1:
                    lhsT, rk, q_ap = ksT[:, kt * 128:(kt + 1) * 128], [bK], qp_t[b2][g][:, :, :]
                    masks.append((we[:, kt * 128:(kt + 1) * 128], selbT[g][:].rearrange("p h q -> p (h q)"), [bc2, bselbT[g]]))
                    if kt - 2 * i in (0, 1):
                        masks.append((smask[:, kt - 2 * i, :], i4[:], [bc2]))
                else:
                    lhsT, rk, q_ap = kwT[:, kt * 128:(kt + 1) * 128], [bK], qp_t[b2][g][:, :, :]
                    pofs = {1: 0, 0: 1, -3: 2, -4: 3}
                    if kt - 2 * i in pofs:
                        masks.append((wmask[:, pofs[kt - 2 * i], :], i4[:], [bc2]))
                sbi[0] += 1
                s_ps, s_pb = bank((0, 1, 5)[sbi[0] % 3])
                n = len(masks)
                P.mm(s_ps, lhsT, q_ap.rearrange("p h q -> p (h q)"), True, n == 0, rk + [bq[b2]], [s_pb])
                for mi, (ml, mr, mrd) in enumerate(masks):
                    P.mm(s_ps, ml, mr, False, mi == n - 1, mrd, [s_pb])
                pti[0] += 1
                k = pti[0] % NPT
                P.act(PT[k][:], s_ps, AF.Exp, [s_pb], [bPT[k]], scale=0.125)
                J["pt"] = (PT[k], bPT[k])

            def combine(J):
                i, g, br = J["i"], J["g"], J["br"]
                b2 = i % 2
                o_ps, o_pb = bank(J["ob"])
                if br == 0 and i == 0:
                    P.ts(rden[:], o_ps[64:128, :], 1e-30, None, ALU.add, None, [o_pb], [bcomb])
                    P.op("dve", lambda e: e.reciprocal(out=rden[:], in_=rden[:]), [bcomb], [bcomb])
                else:
                    P.act(lnd[:], o_ps[64:128, :], AF.Ln, [o_pb], [blnd])
                    P.act(rden[:], lnd[:], AF.Exp, [blnd], [bcomb], scale=-1.0)
                P.tt(coef[:], rden[:], gbc_t[b2][:, g, br].rearrange("p h q -> p (h q)"), ALU.mult, [bcomb, bq[b2]], [bcomb])
                if br == 0:
                    P.tt(accT[g][:], o_ps[0:64, :], coef[:], ALU.mult, [o_pb, bcomb], [bacc[g]])
                elif br == 2:
                    P.tt(ctb[:], o_ps[0:64, :], coef[:], ALU.mult, [o_pb, bcomb], [bcomb])
                    P.tt(accT[g][:], accT[g][:], ctb[:], ALU.add, [bcomb, bacc[g]], [bacc[g]])
                else:
                    P.tt(ctb[:], o_ps[0:64, :], coef[:], ALU.mult, [o_pb, bcomb], [bcomb])
                    P.tt(nsaT[64 * g:64 * g + 64, :, :], accT[g][:].rearrange("p (h q) -> p h q", h=4),
                         ctb[:].rearrange("p (h q) -> p h q", h=4), ALU.add, [bcomb, bacc[g]], [bnsa])

            def selection(J):
                i, g = J["i"], J["g"]
                imp_ps, imp_pb = bank(4)
                dn_ps, dn_pb = bank(6)
                P.ts(rdq[:], dn_ps[:, 4 * g:4 * g + 4], 1e-30, None, ALU.add, None, [dn_pb], [brdq])
                P.op("dve", lambda e: e.reciprocal(out=rdq[:], in_=rdq[:]), [brdq], [brdq])
                P.ts(imp[:], imp_ps[:, 0:128], rdq[:, 0:1], None, ALU.mult, None, [imp_pb, brdq], [bimp])
                for hl in range(1, 4):
                    P.stt(imp[:], imp_ps[:, hl * 128:(hl + 1) * 128], rdq[:, hl:hl + 1], imp[:], ALU.mult, ALU.add,
                          [imp_pb, brdq, bimp], [bimp])
                P.tt(imp[:], imp[:], wf[:, 128 - 4 * i:256 - 4 * i], ALU.add, [bimp, bc2], [bimp])
                P.ts(imp[:, 0:1], imp[:, 0:1], 1000.0, None, ALU.add, None, [bimp], [bimp])
                P.op("dve", lambda e: e.max(out=m8[:, 0:8], in_=imp[:]), [bimp], [bimp])
                P.op("dve", lambda e: e.match_replace(out=sc2[:], in_to_replace=m8[:, 0:8], in_values=imp[:], imm_value=-3e38),
                     [bimp], [bimp])
                P.op("dve", lambda e: e.max(out=m8[:, 8:16], in_=sc2[:]), [bimp], [bimp])
                P.ts(selb[:], imp[:], m8[:, 15:16], NEG, ALU.is_lt, ALU.mult, [bimp], [bselb])
                P.cp(selbs[g][:], selb[:], [bselb], [bselbs[g]])

            def selection_b(i, g):
                dn_ps, dn_pb = bank(6)
                tpv = dn_ps.bitcast(BF16)[:, 256 + 128 * g:384 + 128 * g]
                P.tr(tpv, selbs[g][:], ident[:], [bselbs[g], bconst], [dn_pb])
                P.cp(selbT[g][:], tpv.unsqueeze(1).to_broadcast([128, 4, 128]), [dn_pb], [bselbT[g]])

            def pv(J):
                i, g, br, kt = J["i"], J["g"], J["br"], J["kt"]
                pt, bpt = J["pt"]
                o_ps, o_pb = bank(J["ob"])
                if br == 0:
                    vl, rv = vca[:, kt, g, :], [bvca]
                elif br == 1:
                    vl, rv = vsw[:, kt, 0, g, :], [bV]
                else:
                    vl, rv = vsw[:, kt, 1, g, :], [bV]
                P.mm(o_ps, vl, pt[:], J["first"], J["last"], rv + [bpt], [o_pb])
                if br == 0:
                    imp_ps, imp_pb = bank(4)
                    dn_ps, dn_pb = bank(6)
                    for hl in range(4):
                        P.mm(imp_ps[:, hl * 128:(hl + 1) * 128], pt[:, hl * 128:(hl + 1) * 128], ov[:, kt, :],
                             J["first"] and hl == 0, J["last"], [bpt, bc2], [imp_pb])
                    for hl in range(4):
                        P.mm(dn_ps[:, 4 * g + hl:4 * g + hl + 1], pt[:, hl * 128:(hl + 1) * 128], ones[:],
                             J["first"] and hl == 0, J["last"], [bpt, bc2], [dn_pb])

            def epi_ya(i, h):
                b2 = i % 2
                ya, yab = bank(6)
                for oc4 in range(4):
                    ocn = 4 * h + oc4
                    for hl in range(4):
                        P.mm(ya[:, oc4 * 128:(oc4 + 1) * 128], wat[:, hl, ocn * 128:(ocn + 1) * 128], nsaT[:, hl, :],
                             hl == 0 and oc4 == 0, hl == 3, [bwat, bnsa], [yab])
                P.tt(m1[:, 4 * h:4 * h + 4, :], ya.rearrange("p (c q) -> p c q", c=4), gm_t[b2][:, 4 * h:4 * h + 4, :], ALU.mult,
                     [yab, bq2[b2]], [bm1])
                P.tt(mT[:, 4 * h:4 * h + 4, :], m1[:, 4 * h:4 * h + 4, :], yb_t[b2][:, 4 * h:4 * h + 4, :], ALU.add, [bm1, bq2[b2]], [bmT])

            def epi_xo(i, hf):
                b2 = i % 2
                t0 = i * 128
                xo, xob = bank(6)
                for kc in range(8):
                    P.mm(xo, mT[:, kc, :], wo[:, kc, hf * 512:(hf + 1) * 512], kc == 0, kc == 7, [bmT, bwat], [xob])
                P.tt(x2s[:, hf * 512:(hf + 1) * 512], xo, x_t[b2][:, hf * 512:(hf + 1) * 512], ALU.add, [xob, bq2[b2]], [bx2])
                if hf == 1:
                    P.dma("sp", X2[t0:t0 + 128, :], x2s[:], reads=[bx2])
                    if i + 1 < NTO:
                        loads_e(i + 1)

            for b2_ in range(2):
                for g_ in range(2):
                    ow = slice(64 * (1 - g_), 64 * (1 - g_) + 64)
                    P.memset(qr_t[b2_][g_][ow], 0.0, [bq[b2_]])
                    P.memset(qp_t[b2_][g_][ow], 0.0, [bq[b2_]])
            loads(0)
            loads_e(0)
            score(jobs[0])
            score(jobs[1])
            pend = []
            idxS = {}
            for j, J in enumerate(jobs):
                if J["br"] == 1 and J["first"]:
                    idxS[(J["i"], J["g"])] = j
            for j, J in enumerate(jobs):
                pend.sort(key=lambda t_: t_[0])
                while pend and pend[0][0] <= j:
                    _, fn_, ar_, h_ = pend.pop(0)
                    fn_(ar_, h_)
                if J["tile_first"] and J["i"] + 1 < NTO:
                    loads(J["i"] + 1)
                if j + 2 < len(jobs):
                    score(jobs[j + 2])
                pv(J)
                if J["last"]:
                    combine(J)
                    if J["br"] == 0:
                        selection(J)
                        tS = idxS[(J["i"], J["g"])]
                        pend.append((max(j + 1, min(j + 5, tS - 2)), selection_b, J["i"], J["g"]))
                if J["br"] == 2 and J["g"] == 1 and J["first"] and J["i"] > 0:
                    ip = J["i"] - 1
                    pend.extend([(j + 1, epi_ya, ip, 0), (j + 4, epi_ya, ip, 1), (j + 7, epi_xo, ip, 0), (j + 10, epi_xo, ip, 1)])
                if j == len(jobs) - 1:
                    pend.extend([(j, epi_ya, J["i"], 0), (j, epi_ya, J["i"], 1), (j, epi_xo, J["i"], 0), (j, epi_xo, J["i"], 1)])
                if J["tile_last"]:
                    pend.sort(key=lambda t_: (t_[1] is not selection_b, t_[0]))
                    keep = []
                    while pend:
                        it = pend.pop(0)
                        if it[1] is selection_b and it[2] != J["i"]:
                            keep.append(it)
                        else:
                            it[1](it[2], it[3])
                    pend = keep

        def phase3():
            P.off = persist_off
            wu = P.sb([128, 8, 4096], BF16, "wu")
            wd = P.sb([128, 32, D], BF16, "wd")
            bwu = Buf("wu")
            stg = [P.sb([128, 4096], F32, "stg3") for _ in range(2)]
            bstq = [[Buf() for _ in range(4)] for _ in range(2)]
            mark3 = None
            bstg = [Buf(), Buf()]
            k = 0
            for kc in range(8):
                for c_ in range(4):
                    P.dma(bulkq(), stg[k % 2][:, c_ * 1024:(c_ + 1) * 1024], w_up[kc * 128:(kc + 1) * 128, c_ * 1024:(c_ + 1) * 1024],
                          writes=[bstq[k % 2][c_]])
                P.cp(wu[:, kc, :], stg[k % 2][:], bstq[k % 2], [bwu] + bstq[k % 2], eng="dve")
                k += 1
            for c4 in range(8):
                for c_ in range(4):
                    P.dma(bulkq(), stg[k % 2][:, c_ * 1024:(c_ + 1) * 1024],
                          w_down[c4 * 512 + c_ * 128:c4 * 512 + (c_ + 1) * 128, :], writes=[bstq[k % 2][c_]])
                P.cp(wd[:, c4 * 4:(c4 + 1) * 4, :], stg[k % 2][:].rearrange("p (c o) -> p c o", c=4), bstq[k % 2], [bwu] + bstq[k % 2],
                     eng="dve")
                k += 1
            P.barrier()
            P.off -= 2 * 4096 * 4
            g2T = P.sb([128, 8], F32)
            gfb = P.sb([128, D], F32)
            bc3 = Buf()
            P.dma("sp", g2T[:], g_mlp.rearrange("(c p) -> p c", p=128), writes=[bc3], slow=True)
            P.dma("sp", gfb[:], g_fin.rearrange("(a d) -> a d", a=1).to_broadcast([128, D]), writes=[bc3], slow=True)
            xt2 = [P.sb([128, 2, D], F32) for _ in range(2)]
            bxt2 = [[Buf(), Buf()], [Buf(), Buf()]]
            junk = P.sb([128, D], BF16)
            bjunk = Buf()
            ssq = P.sb([128, 8], F32)
            bss = [Buf() for _ in range(4)]
            hb2 = [P.sb([128, D], BF16) for _ in range(2)]
            bhb2 = [Buf(), Buf()]
            hT2 = [P.sb([128, 8, 256], BF16) for _ in range(2)]
            bhT2 = [Buf(), Buf()]
            rl = P.sb([128, 256], F32)
            brl = Buf()
            hidT = P.sb([128, 32, 256], BF16)
            bhid = Buf()
            x3 = P.sb([128, D], F32)
            bx3 = Buf()
            junk2 = junk
            ss2 = P.sb([128, 2], F32)
            bss2 = Buf()
            ot1 = P.sb([128, D], F32)
            bot1 = Buf()
            psT2 = [ps[0][:, 0:512].bitcast(BF16), ps[0][:, 512:1024].bitcast(BF16)]
            bpsT2 = [psb[0][0], psb[0][1]]
            NB3 = NTO // 2

            def stats3(blk):
                xt = xt2[blk % 2]
                for tt in range(2):
                    tile = blk * 2 + tt
                    bx = bxt2[blk % 2][tt]
                    P.dma(nq(), xt[:, tt, :], X2[tile * 128:(tile + 1) * 128, :], writes=[bx])
                    P.act(junk[:], xt[:, tt, :], AF.Square, [bx], [bjunk, bss[tt]], accum=ssq[:, tt:tt + 1])
                    P.ts(ssq[:, 4 + tt:5 + tt], ssq[:, tt:tt + 1], 1.0 / D, 1e-6, ALU.mult, ALU.add, [bss[tt]], [bss[tt]])
                    P.act(ssq[:, 4 + tt:5 + tt], ssq[:, 4 + tt:5 + tt], AF.Sqrt, [bss[tt]], [bss[tt]])
                    P.op("dve", (lambda tt=tt: (lambda e: e.reciprocal(out=ssq[:, 4 + tt:5 + tt], in_=ssq[:, 4 + tt:5 + tt])))(), [bss[tt]], [bss[tt]])
                    P.act(hb2[tt][:], xt[:, tt, :], AF.Copy, [bx, bss[tt]], [bhb2[tt]], scale=ssq[:, 4 + tt:5 + tt])

            def trans3(blk):
                hT, bhT = hT2[blk % 2], bhT2[blk % 2]
                for tt in range(2):
                    pT, bpT = psT2[tt], bpsT2[tt]
                    for kc in range(8):
                        P.tr(pT[:, kc * 128:(kc + 1) * 128], hb2[tt][:, kc * 128:(kc + 1) * 128], ident[:], [bhb2[tt], bconst], [bpT])
                    P.tt(hT[:, :, tt * 128:(tt + 1) * 128], pT.rearrange("p (c t) -> p c t", c=8),
                         g2T[:].unsqueeze(2).to_broadcast([128, 8, 128]), ALU.mult, [bpT, bc3], [bhT])

            def up3(blk):
                hT, bhT = hT2[blk % 2], bhT2[blk % 2]
                for f in range(32):
                    pa, pb = bank(2 + f % 2)
                    for kc in range(8):
                        P.mm(pa[:, 0:256], wu[:, kc, f * 128:(f + 1) * 128], hT[:, kc, :], kc == 0, kc == 7, [bwu, bhT], [pb])
                    P.act(rl[:], pa[:, 0:256], AF.Relu, [pb], [brl])
                    P.tt(hidT[:, f, :], rl[:], rl[:], ALU.mult, [brl], [bhid], eng="dve")

            def down3(blk):
                res = []
                for tt in range(2):
                    po = ps[2 + tt % 2][:, :]
                    pob = psb[2 + tt % 2]
                    for hf in range(2):
                        for f in range(32):
                            P.mm(po[:, hf * 512:(hf + 1) * 512], hidT[:, f, tt * 128:(tt + 1) * 128], wd[:, f, hf * 512:(hf + 1) * 512],
                                 f == 0, f == 31, [bhid, bwu], [pob[hf]])

            def epi3(blk):
                xt = xt2[blk % 2]
                for tt in range(2):
                    tile = blk * 2 + tt
                    bx = bxt2[blk % 2][tt]
                    po = ps[2 + tt % 2][:, :]
                    pob = psb[2 + tt % 2]
                    P.tt(x3[:], po, xt[:, tt, :], ALU.add, [pob[0], pob[1], bx], [bx3])
                    P.act(junk2[:], x3[:], AF.Square, [bx3], [bss2, bjunk], accum=ss2[:, 0:1])
                    P.ts(ss2[:, 1:2], ss2[:, 0:1], 1.0 / D, 1e-6, ALU.mult, ALU.add, [bss2], [bss2])
                    P.act(ss2[:, 1:2], ss2[:, 1:2], AF.Sqrt, [bss2], [bss2])
                    P.op("dve", lambda e: e.reciprocal(out=ss2[:, 1:2], in_=ss2[:, 1:2]), [bss2], [bss2])
                    P.stt(ot1[:], x3[:], ss2[:, 1:2], gfb[:], ALU.mult, ALU.mult, [bx3, bss2, bc3], [bot1])
                    P.dma(nq(), out[tile * 128:(tile + 1) * 128, :], ot1[:], reads=[bot1])

            stats3(0)
            trans3(0)
            for blk in range(NB3):
                up3(blk)
                if blk + 1 < NB3:
                    stats3(blk + 1)
                down3(blk)
                if blk + 1 < NB3:
                    trans3(blk + 1)
                epi3(blk)

        phases = dbg.get("phases", "1a,1b,2,3") if dbg else "1a,1b,2,3"
        if "1a" in phases:
            phase1a()
            P.barrier()
        if "1b" in phases:
            phase1b()
            P.barrier()
        if "2" in phases:
            phase2()
            P.barrier()
        if "3" in phases:
            phase3()
        if dbg and "dump" in dbg:
            P.barrier()
            for nm in dbg["dump"]:
                src = {"QR": QR, "QP": QP, "KS": KS, "KW": KW, "KC": KC, "VC": VC, "VSW": VSW, "GM": GM, "UT": UT,
                       "NG": NG, "YB": YB, "X2": X2}[nm]
                dd = nc.dram_tensor("dump_" + nm, list(src.shape), src.dtype, kind="ExternalOutput").ap()
                P.dma("sp", dd, src)
        nw = P.finalize()
        print(f"[build] ops={len(P.ops)} waits={nw} sems={P.nsem}")
    return nc, consts


_CACHE = {}


def kernel(**inputs):
    if "nc" not in _CACHE:
        _CACHE["nc"] = build()
        _CACHE["consts"] = [host_consts(0), host_consts(1)]
    nc, _ = _CACHE["nc"]
    x = np.asarray(inputs["x"], dtype=np.float32)
    B = x.shape[0]
    shared = {}
    for k, v in inputs.items():
        if k == "x":
            continue
        a = np.ascontiguousarray(np.asarray(v, dtype=np.float32))
        if k != "norm_final_g":
            a = a[0]
        shared[k] = np.ascontiguousarray(a)
    in_maps = []
    for c in range(8):
        bidx, par = c // 2, c % 2
        m = dict(shared)
        for k, v in _CACHE["consts"][par].items():
            m["c_" + k] = v
        xb = x[bidx % B]
        m["x"] = np.ascontiguousarray(xb)
        m["xown"] = np.ascontiguousarray(xb.reshape(NTO, 2, 128, D)[:, par].reshape(TO, D))
        in_maps.append(m)
    res = run_bass_kernel_spmd(nc, in_maps, core_ids=list(range(8)))
    out = np.empty((B, T, D), np.float32)
    ov = out.reshape(B, NTO, 2, 128, D)
    for c in range(8):
        bidx, par = c // 2, c % 2
        if bidx < B:
            ov[bidx, :, par] = np.asarray(res.results[c]["out"], dtype=np.float32).reshape(NTO, 128, D)
    return out
```
